# Optimizing a Trainium2 kernel written in Bass

```python
import math
import jax, jax.numpy as jnp
from jax import lax
import numpy as np

D_MODEL = 2048
BATCH = 32
SEQ = 256
DEPTH = 2
DEC_BATCH = 8
DEC_SEQ = 2048
PAST_LEN = 256

GRID_W = 64
N_DIRS = 2
MIX_WIDTH = D_MODEL
HG_DK = 128
HG_DV = 128
HG_HEADS = MIX_WIDTH // 2 // HG_DV
HG_QK_DIM = HG_HEADS * HG_DK
HG_V_DIM = HG_HEADS * HG_DV
GD_DK = 128
GD_DV = 128
GD_HEADS = MIX_WIDTH // 2 // GD_DV
GD_QK_DIM = GD_HEADS * GD_DK
GD_V_DIM = GD_HEADS * GD_DV
GD_CONV_DIM = 2 * GD_QK_DIM + GD_V_DIM
CONV_K = 5
HG_CHUNK = 32
GD_CHUNK = 64
D_FF = 5504
N_MOD = 9
EPS = 1e-6
SPLIT_SIZES = (HG_QK_DIM, HG_V_DIM, HG_QK_DIM, HG_QK_DIM, HG_V_DIM,
               GD_QK_DIM, GD_QK_DIM, GD_V_DIM, GD_V_DIM,
               GD_HEADS, GD_HEADS, GD_HEADS, GD_HEADS)
IN_COLS = 3 * HG_QK_DIM + 2 * HG_V_DIM + 2 * GD_QK_DIM + 2 * GD_V_DIM + 4 * GD_HEADS

kernel_name = 'hymba_hgrn2_gdn_macaron_dit_step'


def _rms_norm(x, w):
    x32 = x.astype(jnp.float32)
    y = x32 * lax.rsqrt(jnp.mean(x32 * x32, axis=-1, keepdims=True) + EPS) * w.astype(jnp.float32)
    return y.astype(x.dtype)


def _modulation(cond, w, b):
    return (jax.nn.silu(cond) @ w + b).reshape(cond.shape[0], N_MOD, -1)


def _swiglu(h, w_in, w_out):
    gate, up = jnp.split(h @ w_in, 2, axis=-1)
    return (jax.nn.silu(gate) * up) @ w_out


def _heads(x, n):
    b, t, _ = x.shape
    return x.reshape(b, t, n, -1).transpose(0, 2, 1, 3)


def _merge(x):
    b, h, t, d = x.shape
    return x.transpose(0, 2, 1, 3).reshape(b, t, h * d)


def _l2norm(x):
    return x * lax.rsqrt(jnp.sum(x * x, axis=-1, keepdims=True) + EPS)


def _head_norm_gate(o, w, gate, dtype):
    on = o * lax.rsqrt(jnp.mean(o * o, axis=-1, keepdims=True) + EPS) * w.astype(jnp.float32)
    return (_merge(on) * jax.nn.silu(gate.astype(jnp.float32))).astype(dtype)


def _short_conv(x, w, on_grid):
    b, t, ch = x.shape
    if on_grid:
        rows = t // GRID_W
        x = x.reshape(b * rows, GRID_W, ch)
    y = lax.conv_general_dilated(x, w[:, None, :].astype(x.dtype), window_strides=(1,),
                                 padding=[(CONV_K // 2, CONV_K // 2)],
                                 dimension_numbers=('NWC', 'WIO', 'NWC'),
                                 feature_group_count=ch)
    return y.reshape(b, t, ch)


def _to_chunks(a, c):
    b, h, t = a.shape[:3]
    return jnp.moveaxis(a.reshape(b, h, t // c, c, *a.shape[3:]), 2, 0)


def _from_chunks(a):
    n, b, h, c = a.shape[:4]
    return jnp.moveaxis(a, 0, 2).reshape(b, h, n * c, *a.shape[4:])


def _hgrn2_chunk_scan(q, k, v, log_f, s0):
    c = HG_CHUNK
    tri = jnp.tril(jnp.ones((c, c), bool))

    def step(s, blk):
        qc, kc, vc, gc = blk
        bcum = jnp.cumsum(gc, axis=-2)
        diff = bcum[..., :, None, :] - bcum[..., None, :, :]
        decay = jnp.exp(jnp.where(tri[:, :, None], diff, -jnp.inf))
        att = jnp.einsum('bhtd,bhsd,bhtsd->bhts', qc, kc, decay)
        o = (jnp.einsum('bhts,bhsv->bhtv', att, vc)
             + jnp.einsum('bhtd,bhdv->bhtv', qc * jnp.exp(bcum), s))
        b_last = bcum[..., -1, :]
        s = (jnp.exp(b_last)[..., None] * s
             + jnp.einsum('bhsd,bhsv->bhdv', kc * jnp.exp(b_last[..., None, :] - bcum), vc))
        return s, o

    xs = tuple(_to_chunks(a, c) for a in (q, k, v, log_f))
    s_fin, o = lax.scan(step, s0, xs)
    return _from_chunks(o), s_fin


def _gdn_chunk_scan(q, k, v, g, beta, s0):
    c = GD_CHUNK
    b, h, t, _ = q.shape
    dv = v.shape[-1]
    n = t // c
    q, k, v = (a.reshape(b, h, n, c, a.shape[-1]) for a in (q, k, v))
    g, beta = (a.reshape(b, h, n, c) for a in (g, beta))
    gam = jnp.cumsum(g, axis=-1)
    incl = jnp.tril(jnp.ones((c, c), bool))
    strict = jnp.tril(jnp.ones((c, c), bool), -1)
    decay = jnp.exp(jnp.where(incl, gam[..., :, None] - gam[..., None, :], -jnp.inf))
    kb = k * beta[..., None]
    a_low = jnp.where(strict, jnp.einsum('bhntd,bhnsd->bhnts', kb, k) * decay, 0.0)
    rhs = jnp.concatenate([v * beta[..., None], kb * jnp.exp(gam)[..., None]], axis=-1)
    sol = lax.linalg.triangular_solve(a_low + jnp.eye(c, dtype=q.dtype), rhs, left_side=True,
                                      lower=True, unit_diagonal=True)
    u, w = sol[..., :dv], sol[..., dv:]
    a_qk = jnp.einsum('bhntd,bhnsd->bhnts', q, k) * decay

    def step(s, blk):
        qc, kc, uc, wc, ac, gc = blk
        v_new = uc - jnp.einsum('bhtd,bhdv->bhtv', wc, s)
        o = (jnp.einsum('bhtd,bhdv->bhtv', qc * jnp.exp(gc)[..., None], s)
             + jnp.einsum('bhts,bhsv->bhtv', ac, v_new))
        g_last = gc[..., -1]
        s = (jnp.exp(g_last)[..., None, None] * s
             + jnp.einsum('bhsd,bhsv->bhdv', kc * jnp.exp(g_last[..., None] - gc)[..., None], v_new))
        return s, o

    xs = tuple(jnp.moveaxis(a, 2, 0) for a in (q, k, u, w, a_qk, gam))
    s_fin, o = lax.scan(step, s0, xs)
    return jnp.moveaxis(o, 0, 2).reshape(b, h, t, dv), s_fin


def _scan_dir(fn, arrays, s0, reverse):
    if reverse:
        arrays = [jnp.flip(a, axis=2) for a in arrays]
    o, s = fn(*arrays, s0)
    if reverse:
        o = jnp.flip(o, axis=2)
    return o, s


def _hgrn2_lower_bounds(hg_lower_bounds):
    cs = jnp.cumsum(jax.nn.softmax(hg_lower_bounds.astype(jnp.float32), axis=1), axis=1)
    return cs - cs[:, :1]


def _mixer(h, p, lbs, l, s_hg0, s_gd0, on_grid):
    f32 = jnp.float32
    dt = h.dtype
    offs = [sum(SPLIT_SIZES[:i + 1]) for i in range(len(SPLIT_SIZES) - 1)]
    (hq, hi, hf_f, hf_b, hgate, gq, gk, gv, ggate,
     ga_f, ga_b, gb_f, gb_b) = jnp.split(h @ p['w_in'][l], offs, axis=-1)

    q = _heads(jax.nn.silu(hq.astype(f32)), HG_HEADS) * HG_DK ** -0.5
    v = _heads(hi.astype(f32), HG_HEADS)
    o_hg = 0.0
    s_hg = []
    for d, f_raw in ((0, hf_f), (1, hf_b)):
        lb = lbs[d, l]
        fr = f_raw.astype(f32)
        log_f = _heads(jnp.logaddexp(jnp.log(lb), jnp.log1p(-lb) + jax.nn.log_sigmoid(fr)), HG_HEADS)
        k = _heads((1.0 - lb) * jax.nn.sigmoid(-fr), HG_HEADS)
        o_d, s_d = _scan_dir(_hgrn2_chunk_scan, [q, k, v, log_f], s_hg0[:, d].astype(f32), d == 1)
        o_hg = o_hg + o_d
        s_hg.append(s_d)
    out_hg = _head_norm_gate(o_hg, p['hg_norm_w'][l], hgate, dt)

    qkv = jax.nn.silu(_short_conv(jnp.concatenate([gq, gk, gv], axis=-1), p['gd_conv_w'][l], on_grid))
    cq, ck, cv = jnp.split(qkv.astype(f32), [GD_QK_DIM, 2 * GD_QK_DIM], axis=-1)
    q = _l2norm(_heads(cq, GD_HEADS)) * GD_DK ** -0.5
    k = _l2norm(_heads(ck, GD_HEADS))
    v = _heads(cv, GD_HEADS)
    o_gd = 0.0
    s_gd = []
    for d, a_raw, b_raw in ((0, ga_f, gb_f), (1, ga_b, gb_b)):
        a_log = p['gd_A_log'][l, d].astype(f32)
        dt_bias = p['gd_dt_bias'][l, d].astype(f32)
        g = (-jnp.exp(a_log) * jax.nn.softplus(a_raw.astype(f32) + dt_bias)).transpose(0, 2, 1)
        beta = jax.nn.sigmoid(b_raw.astype(f32)).transpose(0, 2, 1)
        o_d, s_d = _scan_dir(_gdn_chunk_scan, [q, k, v, g, beta], s_gd0[:, d].astype(f32), d == 1)
        o_gd = o_gd + o_d
        s_gd.append(s_d)
    out_gd = _head_norm_gate(o_gd, p['gd_norm_w'][l], ggate, dt)

    out = jnp.concatenate([out_hg, out_gd], axis=-1) @ p['w_out'][l]
    return out, jnp.stack(s_hg, axis=1), jnp.stack(s_gd, axis=1)


def _layer(x, mod, p, lbs, l, s_hg0, s_gd0, on_grid):
    m = [mod[:, i, None, :] for i in range(N_MOD)]
    h = _rms_norm(x, p['norm_w'][l, 0]) * (1 + m[1]) + m[0]
    x = x + 0.5 * m[2] * _swiglu(h, p['ffn_w_in'][l, 0], p['ffn_w_out'][l, 0])
    h = _rms_norm(x, p['norm_w'][l, 1]) * (1 + m[4]) + m[3]
    o, s_hg, s_gd = _mixer(h, p, lbs, l, s_hg0, s_gd0, on_grid)
    x = x + m[5] * o
    h = _rms_norm(x, p['norm_w'][l, 2]) * (1 + m[7]) + m[6]
    x = x + 0.5 * m[8] * _swiglu(h, p['ffn_w_in'][l, 1], p['ffn_w_out'][l, 1])
    return x, s_hg, s_gd


def setup_inputs(seed: int = 0) -> dict:
    key = jax.random.key(seed)
    ks = jax.random.split(key, 20)
    f32 = jnp.float32

    def nrm(k, shape, s):
        return jax.random.normal(k, shape, f32) * s

    x_prompt = nrm(ks[0], (BATCH, SEQ, D_MODEL), 1.0)
    x_sample = nrm(ks[1], (DEC_BATCH, DEC_SEQ, D_MODEL), 1.0)
    c = nrm(ks[2], (DEC_BATCH, D_MODEL), 1.0)
    state_hgrn2 = nrm(ks[3], (DEC_BATCH, DEPTH, N_DIRS, HG_HEADS, HG_DK, HG_DV), 0.5)
    state_gdn = nrm(ks[4], (DEC_BATCH, DEPTH, N_DIRS, GD_HEADS, GD_DK, GD_DV), 0.1)
    c_ctx = nrm(ks[5], (D_MODEL,), 1.0)
    norm_w = 1.0 + nrm(ks[6], (DEPTH, 3, D_MODEL), 0.01)
    w_mod = nrm(ks[7], (DEPTH, D_MODEL, N_MOD * D_MODEL), D_MODEL ** -0.5)
    b_mod = nrm(ks[8], (DEPTH, N_MOD * D_MODEL), 0.02)
    ffn_w_in = nrm(ks[9], (DEPTH, 2, D_MODEL, 2 * D_FF), D_MODEL ** -0.5)
    ffn_w_out = nrm(ks[10], (DEPTH, 2, D_FF, D_MODEL), D_FF ** -0.5)
    w_in = nrm(ks[11], (DEPTH, D_MODEL, IN_COLS), D_MODEL ** -0.5)
    hg_lower_bounds = nrm(ks[12], (N_DIRS, DEPTH, HG_QK_DIM), 0.5)
    hg_norm_w = 1.0 + nrm(ks[13], (DEPTH, HG_DV), 0.01)
    gd_conv_w = nrm(ks[14], (DEPTH, CONV_K, GD_CONV_DIM), CONV_K ** -0.5)
    gd_A_log = jnp.log(jax.random.uniform(ks[15], (DEPTH, N_DIRS, GD_HEADS), f32, 1.0, 16.0))
    dt0 = jnp.exp(jax.random.uniform(ks[16], (DEPTH, N_DIRS, GD_HEADS), f32,
                                     math.log(1e-3), math.log(1e-1)))
    gd_dt_bias = dt0 + jnp.log(-jnp.expm1(-dt0))
    gd_norm_w = 1.0 + nrm(ks[17], (DEPTH, GD_DV), 0.01)
    w_out = nrm(ks[18], (DEPTH, MIX_WIDTH, D_MODEL), MIX_WIDTH ** -0.5)
    final_norm_w = 1.0 + nrm(ks[19], (D_MODEL,), 0.01)
    return {'x_prompt': x_prompt, 'x_sample': x_sample, 'c': c,
            'state_hgrn2': state_hgrn2, 'state_gdn': state_gdn, 'c_ctx': c_ctx,
            'norm_w': norm_w, 'w_mod': w_mod, 'b_mod': b_mod,
            'ffn_w_in': ffn_w_in, 'ffn_w_out': ffn_w_out, 'w_in': w_in,
            'hg_lower_bounds': hg_lower_bounds, 'hg_norm_w': hg_norm_w,
            'gd_conv_w': gd_conv_w, 'gd_A_log': gd_A_log, 'gd_dt_bias': gd_dt_bias,
            'gd_norm_w': gd_norm_w, 'w_out': w_out, 'final_norm_w': final_norm_w}


def reference(x_prompt, x_sample, c, state_hgrn2, state_gdn, c_ctx, norm_w, w_mod, b_mod,
              ffn_w_in, ffn_w_out, w_in, hg_lower_bounds, hg_norm_w, gd_conv_w,
              gd_A_log, gd_dt_bias, gd_norm_w, w_out, final_norm_w):
    p = {'norm_w': norm_w, 'ffn_w_in': ffn_w_in, 'ffn_w_out': ffn_w_out, 'w_in': w_in,
         'hg_norm_w': hg_norm_w, 'gd_conv_w': gd_conv_w, 'gd_A_log': gd_A_log,
         'gd_dt_bias': gd_dt_bias, 'gd_norm_w': gd_norm_w, 'w_out': w_out}
    lbs = _hgrn2_lower_bounds(hg_lower_bounds)
    n_ctx = x_prompt.shape[0]
    zero_hg = jnp.zeros((n_ctx, N_DIRS, HG_HEADS, HG_DK, HG_DV), jnp.float32)
    zero_gd = jnp.zeros((n_ctx, N_DIRS, GD_HEADS, GD_DK, GD_DV), jnp.float32)
    x_p, x_s = x_prompt, x_sample
    new_hg, new_gd = [], []
    for l in range(DEPTH):
        mod_ctx = _modulation(c_ctx[None, :], w_mod[l], b_mod[l])
        x_p, s_hg, s_gd = _layer(x_p, mod_ctx, p, lbs, l, zero_hg, zero_gd, False)
        new_hg.append(s_hg)
        new_gd.append(s_gd)
        mod_lat = _modulation(c, w_mod[l], b_mod[l])
        x_s, _, _ = _layer(x_s, mod_lat, p, lbs, l, state_hgrn2[:, l], state_gdn[:, l], True)
    y_prompt = _rms_norm(x_p, final_norm_w)
    y_sample = _rms_norm(x_s, final_norm_w)
    new_state_hgrn2 = jnp.stack(new_hg, axis=1)
    new_state_gdn = jnp.stack(new_gd, axis=1)
    return (y_prompt, y_sample, new_state_hgrn2, new_state_gdn)
```

```python
import numpy as np
import concourse.bass as bass
import concourse.mybir as mybir
from concourse.bass_utils import run_bass_kernel_spmd

F32 = mybir.dt.float32
BF16 = mybir.dt.bfloat16
AF = mybir.ActivationFunctionType
ALU = mybir.AluOpType

D = 2048
KC = 16
DFF = 5504
NFC = 43
DEPTH = 2
NTOK = 3072
EPS = 1e-6
QS = 128.0 ** -0.5
NEG = -30000.0


class Buf:
    __slots__ = ("w", "r", "excl")

    def __init__(self, excl=False):
        self.w = None
        self.r = []
        self.excl = excl


class Op:
    __slots__ = ("eng", "fn", "waits", "is_dma", "sem", "val", "marked", "extra_wait")


class Rec:
    def __init__(self):
        self.calls = []

    def __getattr__(self, name):
        def m(*a, **k):
            self.calls.append((name, a, k))
            return self
        return m


class Prog:
    ENGS = ["pe", "act", "dve", "pool", "sp"]

    def __init__(self, nc, n_dma_sems=12):
        self.nc = nc
        self.ops = {e: [] for e in self.ENGS}
        self.dma_ops = []
        self.n_dma_sems = n_dma_sems
        self.last_real = {e: None for e in self.ENGS}
        self.dma_since_barrier = []

    def _new(self, eng, fn, dma):
        op = Op()
        op.eng = eng
        op.fn = fn
        op.is_dma = dma
        op.marked = False
        op.sem = None
        op.val = None
        op.extra_wait = None
        op.waits = []
        return op

    def add(self, eng, fn, reads=(), writes=(), dma=False):
        rec = Rec()
        fn(rec)
        op = self._new(eng, rec.calls, dma)
        waits = op.waits
        seen = set()

        def consider(d, raw):
            if d is None or id(d) in seen:
                return
            if (not d.is_dma) and (not dma) and d.eng == eng:
                if not raw or eng == "pe":
                    return
            seen.add(id(d))
            waits.append(d)

        for b in reads:
            consider(b.w, True)
            if b.excl:
                for r in b.r:
                    consider(r, False)
        for b in writes:
            consider(b.w, False)
            for r in b.r:
                consider(r, False)
        for d in waits:
            d.marked = True
        for b in reads:
            if b.excl:
                b.w = op
                b.r = []
            else:
                b.r.append(op)
        for b in writes:
            b.w = op
            b.r = []
        self.ops[eng].append(op)
        if dma:
            self.dma_ops.append(op)
            self.dma_since_barrier.append(op)
        else:
            self.last_real[eng] = op
        return op

    def barrier(self):
        lasts = [self.last_real[e] for e in self.ENGS if self.last_real[e] is not None]
        dmas = list(self.dma_since_barrier)
        self.dma_since_barrier = []
        for e in self.ENGS:
            op = self._new(e, None, False)
            for d in lasts:
                if d.eng != e:
                    op.waits.append(d)
                    d.marked = True
            op.waits.extend(dmas)
            self.ops[e].append(op)

    def finish(self):
        op = self._new("sp", None, False)
        op.waits = list(self.dma_ops)
        self.ops["sp"].append(op)

    def emit(self):
        nc = self.nc
        sems = {e: nc.alloc_semaphore(name=f"s_{e}") for e in self.ENGS}
        dma_sems = {
            q: [nc.alloc_semaphore(name=f"d_{q}{i}") for i in range(self.n_dma_sems)]
            for q in ("sp", "act", "pool")
        }
        for e in self.ENGS:
            cnt = 0
            dcnt = 0
            uses = [0] * self.n_dma_sems
            for op in self.ops[e]:
                if op.is_dma:
                    slot = dcnt % self.n_dma_sems
                    dcnt += 1
                    op.sem = dma_sems[e][slot]
                    if uses[slot] > 0:
                        op.extra_wait = (op.sem, 16 * uses[slot])
                    uses[slot] += 1
                    op.val = 16 * uses[slot]
                elif op.marked:
                    cnt += 1
                    op.sem = sems[e]
                    op.val = cnt
        progs = self.ops

        def run(e, h):
            waited = {}
            for op in progs[e]:
                ws = [(d.sem, d.val) for d in op.waits]
                if op.extra_wait is not None:
                    ws.append(op.extra_wait)
                for sem, val in ws:
                    k = id(sem)
                    if waited.get(k, 0) >= val:
                        continue
                    waited[k] = val
                    h.wait_ge(sem, val)
                if op.fn is None:
                    continue
                ins = None
                for name, a, k in op.fn:
                    ins = getattr(h, name)(*a, **k)
                if op.is_dma:
                    ins.then_inc(op.sem, 16)
                elif op.marked:
                    ins.then_inc(op.sem, 1)

        with nc.Block() as block:

            @block.sync
            def _(h):
                run("sp", h)

            @block.scalar
            def _(h):
                run("act", h)

            @block.vector
            def _(h):
                run("dve", h)

            @block.gpsimd
            def _(h):
                run("pool", h)

            @block.tensor
            def _(h):
                run("pe", h)


class Arena:
    def __init__(self, nc, nbytes):
        self.words = nbytes // 4
        self.t = nc.alloc_sbuf_tensor("arena", [128, self.words], F32)
        self.off = 0
        self.peak = 0
        self.marks = []

    def mark(self):
        self.marks.append(self.off)

    def release(self):
        self.off = self.marks.pop()

    def f32(self, n):
        o = self.off
        self.off += n
        assert self.off <= self.words, ("arena overflow", self.off * 4)
        self.peak = max(self.peak, self.off)
        return self.t[:, o:o + n]

    def bf16(self, n):
        w = (n + 1) // 2
        o = self.off
        self.off += w
        assert self.off <= self.words, ("arena overflow", self.off * 4)
        self.peak = max(self.peak, self.off)
        return self.t[:, o:o + w].bitcast(BF16)[:, 0:n]


def round_robin(gens):
    gens = list(gens)
    while gens:
        for g in list(gens):
            try:
                next(g)
            except StopIteration:
                gens.remove(g)


def pipelined(n, depth, load, compute):
    for i in range(min(depth - 1, n)):
        load(i, i % depth)
    for i in range(n):
        if i + depth - 1 < n:
            load(i + depth - 1, (i + depth - 1) % depth)
        compute(i, i % depth)


C_IDENT = 0
C_TRIU = 128
C_TRIL = 256
C_NEGF = 384
C_NEGB = 512
C_LMF = 640
C_LMB = 640 + 896
C_HMASK = 640 + 1792
C_ONES = C_HMASK + 64
C_RESET = C_ONES + 128
C_END = C_RESET + 512


def build_consts():
    c = np.zeros((128, C_END), np.float32)
    p = np.arange(128)[:, None]
    f = np.arange(128)[None, :]
    c[:, C_IDENT:C_IDENT + 128] = (p == f)
    c[:, C_TRIU:C_TRIU + 128] = (p <= f)
    c[:, C_TRIL:C_TRIL + 128] = (p >= f)
    c[:, C_NEGF:C_NEGF + 128] = np.where(p <= f, 0.0, NEG)
    c[:, C_NEGB:C_NEGB + 128] = np.where(p >= f, 0.0, NEG)
    for i in range(7):
        b = 1 << i
        same = (p // (2 * b)) == (f // (2 * b))
        lm = same & ((p % (2 * b)) < b) & ((f % (2 * b)) >= b)
        c[:, C_LMF + i * 128:C_LMF + (i + 1) * 128] = lm
        c[:, C_LMB + i * 128:C_LMB + (i + 1) * 128] = lm.T
    s = (np.arange(64) % 32)[:, None]
    t = np.arange(32)[None, :]
    c[:64, C_HMASK:C_HMASK + 32] = (s <= t)
    c[:64, C_HMASK + 32:C_HMASK + 64] = (s >= t)
    c[:, C_ONES:C_ONES + 128] = 1.0
    r = np.ones(512, np.float32)
    r[::32] = 0.0
    c[:, C_RESET:C_RESET + 512] = r[None, :]
    return c


T_COND = 0
T_BMOD = T_COND + 32
T_NORM = T_BMOD + 288
T_FNW = T_NORM + 96
T_HGLB = T_FNW + 16
T_HGNW = T_HGLB + 32
T_GDNW = T_HGNW + 2
T_CONV = T_GDNW + 2
T_GDPAR = T_CONV + 240
T_END = T_GDPAR + 64


def build_program(stop_after=None):
    nc = bass.Bass("TRN2", target_bir_lowering=False)
    P = Prog(nc)

    def din(name, shape, dt=F32):
        return nc.dram_tensor(name, list(shape), dt, kind="ExternalInput").ap()

    def dout(name, shape, dt=F32):
        return nc.dram_tensor(name, list(shape), dt, kind="ExternalOutput").ap()

    x_in = din("x_in", [128, KC, NTOK])
    tab_d = din("tab", [128, T_END])
    cst_d = din("cst", [128, C_END])
    st_hg_d = din("st_hg", [DEPTH, 2, 8, 128, 128])
    st_gd_d = din("st_gd", [DEPTH, 2, 8, 128, 128])
    wmod_d = din("wmod", [DEPTH, 36, 128, KC * 512])
    wfi_d = din("wfi", [DEPTH, 2, NFC, 128, KC * 256])
    wfo_d = din("wfo", [DEPTH, 2, 16, 128, NFC * 128])
    whg_d = din("whg", [DEPTH, 8, 128, KC * 640])
    wgd_d = din("wgd", [DEPTH, 8, 128, KC * 516])
    wmo_d = din("wmo", [DEPTH, 16, 128, KC * 128])
    y_out = dout("y_out", [128, KC, NTOK])
    nhg_out = dout("nhg_out", [4, DEPTH, 2, 8, 128, 128])
    ngd_out = dout("ngd_out", [4, DEPTH, 2, 8, 128, 128])
    X = nc.dram_tensor("Xs", [128, KC, NTOK], F32).ap()
    OT = nc.dram_tensor("OTs", [128, KC, NTOK], BF16).ap()

    A = Arena(nc, 206 * 1024)
    banks = [nc.alloc_psum_tensor(f"pb{i}", [128, 512], F32) for i in range(8)]
    bslot = []
    for _i in range(8):
        _b = Buf(excl=True)
        bslot.append([_b, _b, _b, _b])

    def bank_bufs(i):
        return bslot[i]

    cst = A.f32(C_END)
    b_cst = Buf()
    P.add("sp", lambda h: h.dma_start(out=cst, in_=cst_d), writes=[b_cst], dma=True)
    tab = A.f32(T_END)
    b_tab = Buf()
    P.add("sp", lambda h: h.dma_start(out=tab, in_=tab_d), writes=[b_tab], dma=True)
    ident_f = cst[:, C_IDENT:C_IDENT + 128]
    ones_f = cst[:, C_ONES:C_ONES + 128]
    cbf = A.bf16(512)
    b_cbf = Buf()
    ident_b = cbf[:, 0:128]
    ones_b = cbf[:, 128:256]
    negf_b = cbf[:, 256:384]
    negb_b = cbf[:, 384:512]
    P.add("dve", lambda h: h.tensor_copy(out=ident_b, in_=ident_f), reads=[b_cst], writes=[b_cbf])
    P.add("dve", lambda h: h.tensor_copy(out=ones_b, in_=ones_f), reads=[b_cst], writes=[b_cbf])
    P.add("dve", lambda h: h.tensor_copy(out=cbf[:, 256:512], in_=cst[:, C_NEGF:C_NEGF + 256]),
          reads=[b_cst], writes=[b_cbf])
    hmask = cst[0:64, C_HMASK:C_HMASK + 64]

    tA = A.f32(DEPTH * 3 * KC * 2)
    tB = A.f32(DEPTH * 3 * KC * 2)
    tG = A.f32(DEPTH * 3 * KC * 2)
    b_mod = Buf()
    lbt = A.f32(32)
    oml = A.f32(32)
    gdp = A.f32(64)
    b_par = Buf()

    def tix(l, j, kc, r):
        return ((l * 3 + j) * KC + kc) * 2 + r

    def phase_mod():
        A.mark()
        sc = A.bf16(32)
        b_sc = Buf()
        P.add("act", lambda h: h.activation(out=sc, in_=tab[:, T_COND:T_COND + 32], func=AF.Silu),
              reads=[b_tab], writes=[b_sc])
        wr = [A.bf16(KC * 512) for _ in range(2)]
        wb = [Buf() for _ in range(2)]
        modT = A.f32(DEPTH * 288)
        mps = banks[0]
        for l in range(DEPTH):
            def load(i, s, l=l):
                P.add("pool", lambda h: h.dma_start(out=wr[s], in_=wmod_d[l, i]), writes=[wb[s]], dma=True)

            def comp(i, s, l=l):
                w3 = wr[s].rearrange("p (k c) -> p k c", k=KC)
                sc3 = sc.rearrange("p (k r) -> p k r", k=KC)

                def f(h):
                    ins = None
                    for cc in range(4):
                        g = i * 4 + cc
                        for kc in range(KC):
                            ins = h.matmul(mps[:, g * 2:g * 2 + 2], lhsT=w3[:, kc, cc * 128:(cc + 1) * 128],
                                           rhs=sc3[:, kc, :], start=(kc == 0), stop=(kc == KC - 1))
                    return ins
                P.add("pe", f, reads=[wb[s], b_sc], writes=bank_bufs(0))
            pipelined(36, 2, load, comp)
            mt = modT[:, l * 288:(l + 1) * 288]
            bm = tab[:, T_BMOD + l * 144:T_BMOD + (l + 1) * 144]
            P.add("dve", lambda h, mt=mt, bm=bm: h.tensor_tensor(
                out=mt.rearrange("p (g r) -> p g r", r=2), in0=mps[:, 0:288].rearrange("p (g r) -> p g r", r=2),
                in1=bm.unsqueeze(2).to_broadcast([128, 144, 2]), op=ALU.add),
                reads=bank_bufs(0) + [b_tab], writes=[b_mod])
            for j in range(3):
                o = tix(l, j, 0, 0)
                sh = mt[:, (3 * j) * 32:(3 * j + 1) * 32]
                scl = mt[:, (3 * j + 1) * 32:(3 * j + 2) * 32]
                gt = mt[:, (3 * j + 2) * 32:(3 * j + 3) * 32]
                nw = tab[:, T_NORM + (l * 3 + j) * KC:T_NORM + (l * 3 + j + 1) * KC]
                P.add("dve", lambda h, o=o, scl=scl, nw=nw: h.scalar_tensor_tensor(
                    out=tA[:, o:o + 32].rearrange("p (k r) -> p k r", r=2),
                    in0=scl.rearrange("p (k r) -> p k r", r=2), scalar=1.0,
                    in1=nw.unsqueeze(2).to_broadcast([128, KC, 2]), op0=ALU.add, op1=ALU.mult),
                    reads=[b_mod, b_tab], writes=[b_mod])
                P.add("dve", lambda h, o=o, sh=sh: h.tensor_copy(out=tB[:, o:o + 32], in_=sh),
                      reads=[b_mod], writes=[b_mod])
                gs = 1.0 if j == 1 else 0.5
                P.add("dve", lambda h, o=o, gt=gt, gs=gs: h.tensor_scalar(
                    out=tG[:, o:o + 32], in0=gt, scalar1=gs, scalar2=None, op0=ALU.mult),
                    reads=[b_mod], writes=[b_mod])
        hg = tab[:, T_HGLB:T_HGLB + 32].rearrange("p (d l h) -> p d l h", d=2, l=2)
        lb4 = lbt.rearrange("p (d l h) -> p d l h", d=2, l=2)
        P.add("dve", lambda h: h.memset(lbt, 0.0), writes=[b_par])
        P.add("dve", lambda h: h.tensor_tensor(out=lb4[:, :, 1, :], in0=hg[:, :, 1, :], in1=hg[:, :, 0, :],
                                               op=ALU.subtract), reads=[b_tab], writes=[b_par])
        P.add("act", lambda h: h.activation(out=lb4[:, :, 1, :], in_=lb4[:, :, 1, :], func=AF.Sigmoid),
              reads=[b_par], writes=[b_par])
        P.add("dve", lambda h: h.tensor_scalar(out=oml, in0=lbt, scalar1=-1.0, scalar2=1.0,
                                               op0=ALU.mult, op1=ALU.add), reads=[b_par], writes=[b_par])
        gp = tab[:, T_GDPAR:T_GDPAR + 64].rearrange("p (x two) -> p x two", two=2)
        gd3 = gdp.rearrange("p (x two) -> p x two", two=2)
        P.add("act", lambda h: h.activation(out=gd3[:, :, 0], in_=gp[:, :, 0], func=AF.Exp),
              reads=[b_tab], writes=[b_par])
        P.add("dve", lambda h: h.tensor_scalar(out=gd3[:, :, 0], in0=gd3[:, :, 0], scalar1=-1.0, scalar2=None,
                                               op0=ALU.mult), reads=[b_par], writes=[b_par])
        P.add("dve", lambda h: h.tensor_copy(out=gd3[:, :, 1], in_=gp[:, :, 1]), reads=[b_tab], writes=[b_par])
        P.barrier()
        A.release()

    def phase_ffn(l, j, xsrc, xdst):
        jn = 0 if j == 0 else 2
        A.mark()
        hT = A.bf16(KC * 1024)
        actT = A.bf16(NFC * 1024)
        h3 = hT.rearrange("p (k t) -> p k t", k=KC)
        a3 = actT.rearrange("p (k t) -> p k t", k=NFC)
        for blk in range(3):
            r = 0 if blk == 0 else 1
            tok0 = blk * 1024
            norm_mod(xsrc, tok0, 1024, l, jn, r, hT, 1024, 0)
            A.mark()
            wr = [A.bf16(KC * 256) for _ in range(3)]
            wb = [Buf() for _ in range(3)]
            sg = [A.f32(512) for _ in range(2)]
            sgb = [Buf() for _ in range(2)]
            b_act = Buf()

            def load1(i, s):
                P.add("pool", lambda h: h.dma_start(out=wr[s], in_=wfi_d[l, j, i]), writes=[wb[s]], dma=True)

            def comp1(i, s):
                w3 = wr[s].rearrange("p (k c) -> p k c", k=KC)
                for sb in range(2):
                    q = (i * 2 + sb) % 2
                    gb, ub = 1 + q, 3 + q

                    def f(h, sb=sb, gb=gb, ub=ub):
                        ins = None
                        for kc in range(KC):
                            ins = h.matmul(banks[gb][:, :], lhsT=w3[:, kc, 0:128], rhs=h3[:, kc, sb * 512:(sb + 1) * 512],
                                           start=(kc == 0), stop=(kc == KC - 1))
                        for kc in range(KC):
                            ins = h.matmul(banks[ub][:, :], lhsT=w3[:, kc, 128:256], rhs=h3[:, kc, sb * 512:(sb + 1) * 512],
                                           start=(kc == 0), stop=(kc == KC - 1))
                        return ins
                    P.add("pe", f, reads=[wb[s]], writes=bank_bufs(gb) + bank_bufs(ub))
                    P.add("act", lambda h, q=q, gb=gb: h.activation(out=sg[q], in_=banks[gb][:, :], func=AF.Silu),
                          reads=bank_bufs(gb), writes=[sgb[q]])
                    P.add("dve", lambda h, q=q, ub=ub, sb=sb: h.tensor_tensor(
                        out=a3[:, i, sb * 512:(sb + 1) * 512], in0=sg[q], in1=banks[ub][:, :], op=ALU.mult),
                        reads=[sgb[q]] + bank_bufs(ub), writes=[b_act])
            pipelined(NFC, 3, load1, comp1)
            P.barrier()
            A.release()
            A.mark()
            wo = [A.bf16(NFC * 128) for _ in range(3)]
            wob = [Buf() for _ in range(3)]
            xo = [A.f32(1024) for _ in range(3)]
            xob = [Buf() for _ in range(3)]

            def load2(i, s):
                P.add("pool", lambda h: h.dma_start(out=wo[s], in_=wfo_d[l, j, i]), writes=[wob[s]], dma=True)
                P.add("sp", lambda h: h.dma_start(out=xo[s], in_=xsrc[:, i, tok0:tok0 + 1024]), writes=[xob[s]], dma=True)

            def comp2(i, s):
                w3 = wo[s].rearrange("p (k c) -> p k c", k=NFC)
                ig = tix(l, jn, i, r)
                for sb in range(2):
                    yb = 5 + (i * 2 + sb) % 2

                    def f(h, sb=sb, yb=yb):
                        ins = None
                        for fc in range(NFC):
                            ins = h.matmul(banks[yb][:, :], lhsT=w3[:, fc, :], rhs=a3[:, fc, sb * 512:(sb + 1) * 512],
                                           start=(fc == 0), stop=(fc == NFC - 1))
                        return ins
                    P.add("pe", f, reads=[wob[s]], writes=bank_bufs(yb))
                    P.add("dve", lambda h, sb=sb, yb=yb: h.scalar_tensor_tensor(
                        out=xo[s][:, sb * 512:(sb + 1) * 512], in0=banks[yb][:, :], scalar=tG[:, ig:ig + 1],
                        in1=xo[s][:, sb * 512:(sb + 1) * 512], op0=ALU.mult, op1=ALU.add),
                        reads=bank_bufs(yb) + [xob[s], b_mod], writes=[xob[s]])
                P.add("sp", lambda h: h.dma_start(out=xdst[:, i, tok0:tok0 + 1024], in_=xo[s]), reads=[xob[s]], dma=True)
            pipelined(16, 3, load2, comp2)
            P.barrier()
            A.release()
        A.release()

    def phase_final():
        A.mark()
        xs = A.f32(KC * 512)
        b_x = Buf()
        sq = A.bf16(KC * 512)
        b_sq = Buf()
        lnv = A.f32(512)
        rstd = A.f32(512)
        b_r = Buf()
        yo = A.f32(KC * 512)
        b_y = Buf()
        x3 = xs.rearrange("p (k t) -> p k t", k=KC)
        y3 = yo.rearrange("p (k t) -> p k t", k=KC)
        sq3 = sq.rearrange("p (k t) -> p k t", k=KC)
        for sb in range(NTOK // 512):
            t0 = sb * 512
            P.add("sp", lambda h, t0=t0: h.dma_start(out=x3, in_=X[:, :, t0:t0 + 512]), writes=[b_x], dma=True)
            P.add("act", lambda h: h.activation(out=sq, in_=xs, func=AF.Square), reads=[b_x], writes=[b_sq])

            def f(h):
                ins = None
                for kc in range(KC):
                    ins = h.matmul(banks[0][:, :], lhsT=ones_b, rhs=sq3[:, kc, :], start=(kc == 0), stop=(kc == KC - 1))
                return ins
            P.add("pe", f, reads=[b_sq, b_cbf], writes=bank_bufs(0))
            P.add("act", lambda h: h.activation(out=lnv, in_=banks[0][:, :], func=AF.Ln, bias=EPS, scale=1.0 / D),
                  reads=bank_bufs(0), writes=[b_r])
            P.add("act", lambda h: h.activation(out=rstd, in_=lnv, func=AF.Exp, scale=-0.5), reads=[b_r], writes=[b_r])
            for kc in range(KC):
                P.add("dve", lambda h, kc=kc: h.scalar_tensor_tensor(
                    out=y3[:, kc, :], in0=x3[:, kc, :], scalar=tab[:, T_FNW + kc:T_FNW + kc + 1], in1=rstd,
                    op0=ALU.mult, op1=ALU.mult), reads=[b_x, b_r, b_tab], writes=[b_y])
            P.add("sp", lambda h, t0=t0: h.dma_start(out=y_out[:, :, t0:t0 + 512], in_=y3), reads=[b_y], dma=True)
        P.barrier()
        A.release()

    def head_finalize(l, hidx, nwcol, ntok, tok0, gsil, b_gs, obanks):
        A.mark()
        osq = [A.bf16(512) for _ in range(2)]
        osb = [Buf() for _ in range(2)]
        lnv = [A.f32(512) for _ in range(2)]
        lb_ = [Buf() for _ in range(2)]
        t1 = [A.f32(512) for _ in range(2)]
        t1b = [Buf() for _ in range(2)]
        ob = [A.bf16(512) for _ in range(2)]
        obb = [Buf() for _ in range(2)]
        for sb in range(ntok // 512):
            q = sb % 2
            op_ = banks[obanks[sb]]
            P.add("act", lambda h, q=q, op_=op_: h.activation(out=osq[q], in_=op_[:, :], func=AF.Square),
                  reads=bank_bufs(obanks[sb]), writes=[osb[q]])
            sbk = 4 + q
            P.add("pe", lambda h, q=q, sbk=sbk: h.matmul(banks[sbk][:, :], lhsT=ones_b, rhs=osq[q], start=True, stop=True),
                  reads=[osb[q], b_cbf], writes=bank_bufs(sbk))
            P.add("act", lambda h, q=q, sbk=sbk: h.activation(out=lnv[q], in_=banks[sbk][:, :], func=AF.Ln, bias=EPS,
                                                             scale=1.0 / 128.0), reads=bank_bufs(sbk), writes=[lb_[q]])
            P.add("act", lambda h, q=q: h.activation(out=lnv[q], in_=lnv[q], func=AF.Exp, scale=-0.5),
                  reads=[lb_[q]], writes=[lb_[q]])
            P.add("dve", lambda h, q=q, op_=op_: h.scalar_tensor_tensor(
                out=t1[q], in0=op_[:, :], scalar=tab[:, nwcol:nwcol + 1], in1=lnv[q], op0=ALU.mult, op1=ALU.mult),
                reads=bank_bufs(obanks[sb]) + [lb_[q], b_tab], writes=[t1b[q]])
            P.add("dve", lambda h, q=q, sb=sb: h.tensor_tensor(out=ob[q], in0=t1[q], in1=gsil[:, sb * 512:(sb + 1) * 512],
                                                               op=ALU.mult), reads=[t1b[q], b_gs], writes=[obb[q]])
            P.add("sp", lambda h, q=q, sb=sb: h.dma_start(out=OT[:, hidx, tok0 + sb * 512:tok0 + (sb + 1) * 512], in_=ob[q]),
                  reads=[obb[q]], dma=True)
        P.barrier()
        A.release()

    def proj_fm(w3, c0, h3, ntok, evac):
        for sb in range(ntok // 512):
            bk = 4 + sb % 2

            def f(h, sb=sb, bk=bk):
                ins = None
                for kc in range(KC):
                    ins = h.matmul(banks[bk][:, :], lhsT=w3[:, kc, c0:c0 + 128], rhs=h3[:, kc, sb * 512:(sb + 1) * 512],
                                   start=(kc == 0), stop=(kc == KC - 1))
                return ins
            P.add("pe", f, reads=[b_w[0], b_hT[0]], writes=bank_bufs(bk))
            evac(sb, bk)

    b_w = [Buf()]
    b_hT = [Buf()]

    def hgrn2_head(l, hd, grp, h3, wbuf, ntok, tok0, seqs, T):
        A.mark()
        nch = ntok // 32
        ntt = ntok // 64
        w3 = wbuf.rearrange("p (k c) -> p k c", k=KC)[:, :, 0:640]
        if hd == 0:
            P.add("pool", lambda h: h.dma_start(out=wbuf[:, 0:KC * 640], in_=whg_d[l, hd]), writes=[b_w[0]], dma=True)
        qf = A.bf16(ntok)
        b_q = Buf()
        gsil = A.bf16(ntok)
        b_gs = Buf()
        vtok = A.bf16(ntt * 128)
        b_v = Buf()
        v3 = vtok.rearrange("p (t c) -> p t c", c=128)
        qb, qg, kg, kgz, kbT, dec, b_d = [], [], [], [], [], [], []
        for d in range(2):
            qb.append(A.bf16(ntok))
            qg.append(A.bf16(ntok))
            kg.append(A.bf16(ntok))
            kgz.append(A.bf16(ntok))
            kbT.append(A.bf16(ntt * 128))
            dec.append(A.f32(nch))
            b_d.append(Buf())
            P.add("pool", lambda h, d=d: h.memset(kgz[d], 0.0), writes=[b_d[d]])
        Sf = [A.f32(128) for _ in range(2)]
        Sb = [A.bf16(128) for _ in range(2)]
        bSf = [Buf(), Buf()]
        bSb = [Buf(), Buf()]
        att = [[A.bf16(32) for _ in range(4)] for _ in range(2)]
        attb = [[Buf() for _ in range(4)] for _ in range(2)]
        A.mark()
        ff = A.f32(ntok)
        kk = A.bf16(ntok)
        bc = A.f32(ntok)
        ee = A.f32(ntok)
        kbf = A.bf16(ntok)
        b_t = Buf()
        resetm = cst[:, C_RESET:C_RESET + 512]
        proj_fm(w3, 0, h3, ntok, lambda sb, bk: P.add("act", lambda h: h.activation(
            out=qf[:, sb * 512:(sb + 1) * 512], in_=banks[bk][:, :], func=AF.Silu), reads=bank_bufs(bk), writes=[b_q]))
        proj_fm(w3, 512, h3, ntok, lambda sb, bk: P.add("act", lambda h: h.activation(
            out=gsil[:, sb * 512:(sb + 1) * 512], in_=banks[bk][:, :], func=AF.Silu), reads=bank_bufs(bk), writes=[b_gs]))
        for g4 in range(ntt // 4):
            bk = 4 + g4 % 2

            def f(h, g4=g4, bk=bk):
                ins = None
                for u in range(4):
                    tt = g4 * 4 + u
                    for kc in range(KC):
                        ins = h.matmul(banks[bk][0:64, u * 128:(u + 1) * 128], lhsT=h3[:, kc, tt * 64:(tt + 1) * 64],
                                       rhs=w3[:, kc, 128:256], start=(kc == 0), stop=(kc == KC - 1))
                return ins
            P.add("pe", f, reads=[b_w[0], b_hT[0]], writes=bank_bufs(bk))
            P.add("act", lambda h, g4=g4, bk=bk: h.activation(out=vtok[0:64, g4 * 512:(g4 + 1) * 512],
                                                            in_=banks[bk][0:64, :], func=AF.Copy),
                  reads=bank_bufs(bk), writes=[b_v])
        for d in range(2):
            il = (d * 2 + l) * 8 + hd
            proj_fm(w3, 256 + d * 128, h3, ntok, lambda sb, bk: P.add("act", lambda h: h.activation(
                out=ff[:, sb * 512:(sb + 1) * 512], in_=banks[bk][:, :], func=AF.Sigmoid),
                reads=bank_bufs(bk), writes=[b_t]))
            P.add("dve", lambda h, il=il: h.tensor_scalar(out=ff, in0=ff, scalar1=oml[:, il:il + 1],
                                                          scalar2=lbt[:, il:il + 1], op0=ALU.mult, op1=ALU.add),
                  reads=[b_t, b_par], writes=[b_t])
            P.add("dve", lambda h: h.tensor_scalar(out=kk, in0=ff, scalar1=-1.0, scalar2=1.0, op0=ALU.mult, op1=ALU.add),
                  reads=[b_t], writes=[b_t])
            P.add("act", lambda h: h.activation(out=ff, in_=ff, func=AF.Ln), reads=[b_t], writes=[b_t])
            for sg4 in range(ntok // 512):
                ssl = slice(sg4 * 512, (sg4 + 1) * 512)
                P.add("dve", lambda h, ssl=ssl: h.tensor_tensor_scan(out=bc[:, ssl], data0=resetm, data1=ff[:, ssl],
                                                                     initial=0.0, op0=ALU.mult, op1=ALU.add),
                      reads=[b_t, b_cst], writes=[b_t])
            bc3 = bc.rearrange("p (c k) -> p c k", k=32)
            ff3 = ff.rearrange("p (c k) -> p c k", k=32)
            ee3 = ee.rearrange("p (c k) -> p c k", k=32)
            if d == 1:
                P.add("dve", lambda h, bc3=bc3, ee3=ee3: h.tensor_tensor(
                    out=ee3, in0=bc3[:, :, 31:32].to_broadcast([128, nch, 32]), in1=bc3, op=ALU.subtract),
                    reads=[b_t], writes=[b_t])
                P.add("dve", lambda h: h.tensor_tensor(out=bc, in0=ee, in1=ff, op=ALU.add), reads=[b_t], writes=[b_t])
            last = 31 if d == 0 else 0
            P.add("act", lambda h: h.activation(out=ee, in_=bc, func=AF.Exp), reads=[b_t], writes=[b_t])
            P.add("dve", lambda h, d=d: h.scalar_tensor_tensor(out=qb[d], in0=qf, scalar=QS, in1=ee, op0=ALU.mult,
                                                             op1=ALU.mult), reads=[b_q, b_t], writes=[b_d[d]])
            P.add("act", lambda h, d=d, last=last, bc3=bc3: h.activation(out=dec[d], in_=bc3[:, :, last], func=AF.Exp),
                  reads=[b_t], writes=[b_d[d]])
            P.add("dve", lambda h, bc3=bc3, ff3=ff3: h.tensor_tensor(
                out=ff3, in0=bc3, in1=bc3[:, :, 15:16].to_broadcast([128, nch, 32]), op=ALU.subtract),
                reads=[b_t], writes=[b_t])
            P.add("act", lambda h: h.activation(out=ee, in_=ff, func=AF.Exp), reads=[b_t], writes=[b_t])
            P.add("dve", lambda h, d=d: h.scalar_tensor_tensor(out=qg[d], in0=qf, scalar=QS, in1=ee, op0=ALU.mult,
                                                             op1=ALU.mult), reads=[b_q, b_t], writes=[b_d[d]])
            P.add("act", lambda h: h.activation(out=ee, in_=ff, func=AF.Exp, scale=-1.0), reads=[b_t], writes=[b_t])
            P.add("dve", lambda h, d=d: h.tensor_tensor(out=kg[d], in0=kk, in1=ee, op=ALU.mult), reads=[b_t],
                  writes=[b_d[d]])
            hs = slice(0, 16) if d == 0 else slice(16, 32)
            kg3 = kg[d].rearrange("p (c k) -> p c k", k=32)
            kz3 = kgz[d].rearrange("p (c k) -> p c k", k=32)
            P.add("pool", lambda h, kg3=kg3, kz3=kz3, hs=hs: h.tensor_copy(out=kz3[:, :, hs], in_=kg3[:, :, hs]),
                  reads=[b_d[d]], writes=[b_d[d]])
            P.add("dve", lambda h, last=last, bc3=bc3, ff3=ff3: h.tensor_tensor(
                out=ff3, in0=bc3[:, :, last:last + 1].to_broadcast([128, nch, 32]), in1=bc3, op=ALU.subtract),
                reads=[b_t], writes=[b_t])
            P.add("act", lambda h: h.activation(out=ee, in_=ff, func=AF.Exp), reads=[b_t], writes=[b_t])
            P.add("dve", lambda h: h.tensor_tensor(out=kbf, in0=kk, in1=ee, op=ALU.mult), reads=[b_t], writes=[b_t])
            for g4 in range(ntt // 4):
                bk = 4 + g4 % 2
                pbf = banks[bk][:, :].bitcast(BF16)

                def f(h, g4=g4, pbf=pbf):
                    ins = None
                    for u in range(4):
                        tt = g4 * 4 + u
                        ins = h.transpose(pbf[0:64, u * 128:(u + 1) * 128], kbf[:, tt * 64:(tt + 1) * 64], ident_b)
                    return ins
                P.add("pe", f, reads=[b_t, b_cbf], writes=bank_bufs(bk))
                P.add("act", lambda h, d=d, g4=g4, pbf=pbf: h.activation(
                    out=kbT[d][0:64, g4 * 512:(g4 + 1) * 512], in_=pbf[0:64, 0:512], func=AF.Copy),
                    reads=bank_bufs(bk), writes=[b_d[d]])
        P.barrier()
        A.release()
        P.add("pool", lambda h: h.dma_start(out=wbuf[:, 0:KC * 516], in_=wgd_d[l, hd]), writes=[b_w[0]], dma=True)
        obanks = list(range(ntok // 512))
        for ob_ in obanks:
            P.add("dve", lambda h, ob_=ob_: h.memset(banks[ob_][:, :], 0.0), writes=bank_bufs(ob_))
        Sf2 = [[Sf[d], A.f32(128)] for d in range(2)]
        Sb2 = [[Sb[d], A.bf16(128)] for d in range(2)]
        bSf2 = [[Buf(), Buf()] for _ in range(2)]
        bSb2 = [[Buf(), Buf()] for _ in range(2)]
        nseqch = T // 32

        def hchain(d, si, s):
            sc0 = si * nseqch
            order = [sc0 + (i if d == 0 else nseqch - 1 - i) for i in range(nseqch)]
            k3 = kbT[d].rearrange("p (t c) -> p t c", c=128)
            if grp == 1:
                P.add("sp", lambda h: h.dma_start(out=Sf2[d][0], in_=st_hg_d[l, d, hd]), writes=[bSf2[d][0]], dma=True)
            else:
                P.add("dve", lambda h: h.memset(Sf2[d][0], 0.0), writes=[bSf2[d][0]])
            yield
            P.add("pool", lambda h: h.tensor_copy(out=Sb2[d][0], in_=Sf2[d][0]), reads=[bSf2[d][0]], writes=[bSb2[d][0]])
            yield

            def geo(i):
                c = order[i]
                bank = (4 + d) if i % 2 == 0 else (6 + d)
                return c, c // 2, 32 * (c % 2), i % 4, bank

            def stage_a(i):
                c, tt, p0, a, bank = geo(i)
                cs = slice(c * 32, (c + 1) * 32)
                c32 = c * 32

                def fa(h):
                    l0 = kgz[d] if d == 0 else kg[d]
                    l1 = kg[d] if d == 0 else kgz[d]
                    h.matmul(banks[bank][p0:p0 + 32, 0:16], lhsT=l0[:, cs], rhs=qg[d][:, c32:c32 + 16], start=True, stop=True)
                    h.matmul(banks[bank][p0:p0 + 32, 16:32], lhsT=l1[:, cs], rhs=qg[d][:, c32 + 16:c32 + 32],
                             start=True, stop=True)
                    return h.matmul(banks[bank][:, 128:256], lhsT=k3[p0:p0 + 32, tt, :], rhs=v3[p0:p0 + 32, tt, :],
                                    start=True, stop=True)
                P.add("pe", fa, reads=[b_d[d], b_v], writes=bank_bufs(bank))
                yield
                P.add("dve", lambda h: h.tensor_tensor(
                    out=att[d][a][p0:p0 + 32, :], in0=banks[bank][p0:p0 + 32, 0:32],
                    in1=hmask[p0:p0 + 32, d * 32:(d + 1) * 32], op=ALU.mult),
                    reads=bank_bufs(bank) + [b_cst], writes=[attb[d][a]])
                yield

            def stage_b(i):
                c, tt, p0, a, bank = geo(i)
                cs = slice(c * 32, (c + 1) * 32)
                obk = c // 16
                ocol = (c % 16) * 32
                k0, k1 = i % 2, (i + 1) % 2

                def fo(h):
                    h.matmul(banks[obk][:, ocol:ocol + 32], lhsT=Sb2[d][k0], rhs=qb[d][:, cs], start=False, stop=False,
                             skip_group_check=True)
                    return h.matmul(banks[obk][:, ocol:ocol + 32], lhsT=v3[p0:p0 + 32, tt, :], rhs=att[d][a][p0:p0 + 32, :],
                                    start=False, stop=True, skip_group_check=True)
                P.add("pe", fo, reads=[bSb2[d][k0], b_d[d], b_v, attb[d][a]], writes=bank_bufs(obk))
                yield
                P.add("dve", lambda h: h.scalar_tensor_tensor(
                    out=Sf2[d][k1], in0=Sf2[d][k0], scalar=dec[d][:, c:c + 1], in1=banks[bank][:, 128:256],
                    op0=ALU.mult, op1=ALU.add), reads=[bSf2[d][k0], b_d[d]] + bank_bufs(bank), writes=[bSf2[d][k1]])
                yield
                P.add("pool", lambda h: h.tensor_copy(out=Sb2[d][k1], in_=Sf2[d][k1]), reads=[bSf2[d][k1]],
                      writes=[bSb2[d][k1]])
                yield

            yield from stage_a(0)
            for i in range(nseqch):
                if i + 1 < nseqch:
                    yield from stage_a(i + 1)
                yield from stage_b(i)
            if grp == 0:
                kf = nseqch % 2
                P.add("sp", lambda h: h.dma_start(out=nhg_out[s, l, d, hd], in_=Sf2[d][kf]), reads=[bSf2[d][kf]], dma=True)
                yield

        for si, s in enumerate(seqs):
            round_robin([hchain(0, si, s), hchain(1, si, s)])
        P.barrier()
        head_finalize(l, hd, T_HGNW + l, ntok, tok0, gsil, b_gs, obanks)
        A.release()

    def gdn_head(l, hd, grp, h3, wbuf, ntok, tok0, seqs, T):
        A.mark()
        nT = ntok // 128
        R = 64 if grp == 1 else 256
        w3 = wbuf[:, 0:KC * 516].rearrange("p (k c) -> p k c", k=KC)
        gsil = A.bf16(ntok)
        b_gs = Buf()
        proj_fm(w3, 384, h3, ntok, lambda sb, bk: P.add("act", lambda h: h.activation(
            out=gsil[:, sb * 512:(sb + 1) * 512], in_=banks[bk][:, :], func=AF.Silu), reads=bank_bufs(bk), writes=[b_gs]))
        abt = A.f32(nT * 4)
        b_ab = Buf()

        def fab(h):
            ins = None
            for j in range(nT):
                for kc in range(KC):
                    ins = h.matmul(banks[4][:, j * 4:(j + 1) * 4], lhsT=h3[:, kc, j * 128:(j + 1) * 128],
                                   rhs=w3[:, kc, 512:516], start=(kc == 0), stop=(kc == KC - 1))
            return ins
        P.add("pe", fab, reads=[b_w[0], b_hT[0]], writes=bank_bufs(4))
        P.add("act", lambda h: h.activation(out=abt, in_=banks[4][:, 0:nT * 4], func=AF.Copy), reads=bank_bufs(4),
              writes=[b_ab])
        ab3 = abt.rearrange("p (j c) -> p j c", c=4)
        gg = A.f32(nT * 2)
        be = A.f32(nT * 2)
        gam = A.f32(nT * 2)
        ngam = A.f32(nT * 2)
        eg = A.f32(nT * 2)
        rws = A.f32(nT * 2)
        b_g = Buf()
        gg3 = gg.rearrange("p (j d) -> p j d", d=2)
        be3 = be.rearrange("p (j d) -> p j d", d=2)
        for d in range(2):
            ip = ((l * 2 + d) * 8 + hd) * 2
            P.add("act", lambda h, d=d, ip=ip: h.activation(out=gg3[:, :, d], in_=ab3[:, :, d], func=AF.Exp,
                                                          bias=gdp[:, ip + 1:ip + 2], scale=1.0),
                  reads=[b_ab, b_par], writes=[b_g])
            P.add("act", lambda h, d=d: h.activation(out=gg3[:, :, d], in_=gg3[:, :, d], func=AF.Ln, bias=1.0, scale=1.0),
                  reads=[b_g], writes=[b_g])
            P.add("dve", lambda h, d=d, ip=ip: h.tensor_scalar(out=gg3[:, :, d], in0=gg3[:, :, d],
                                                             scalar1=gdp[:, ip:ip + 1], scalar2=None, op0=ALU.mult),
                  reads=[b_g, b_par], writes=[b_g])
            P.add("act", lambda h, d=d: h.activation(out=be3[:, :, d], in_=ab3[:, :, 2 + d], func=AF.Sigmoid),
                  reads=[b_ab], writes=[b_g])
        for d in range(2):
            tri = cst[:, C_TRIU:C_TRIU + 128] if d == 0 else cst[:, C_TRIL:C_TRIL + 128]
            P.add("pe", lambda h, d=d, tri=tri: h.matmul(banks[5][:, d * nT:(d + 1) * nT], lhsT=tri, rhs=gg3[:, :, d],
                                                        start=True, stop=True), reads=[b_g, b_cst], writes=bank_bufs(5))
        P.add("act", lambda h: h.activation(out=gam, in_=banks[5][:, 0:2 * nT], func=AF.Copy), reads=bank_bufs(5),
              writes=[b_g])
        P.add("dve", lambda h: h.tensor_scalar(out=ngam, in0=gam, scalar1=-1.0, scalar2=None, op0=ALU.mult),
              reads=[b_g], writes=[b_g])
        P.add("act", lambda h: h.activation(out=eg, in_=gam, func=AF.Exp), reads=[b_g], writes=[b_g])
        eg3 = eg.rearrange("p (d j) -> p d j", d=2)
        rw3 = rws.rearrange("p (d j) -> p d j", d=2)
        for d in range(2):
            P.add("dve", lambda h, d=d: h.tensor_tensor(out=rw3[:, d, :], in0=eg3[:, d, :], in1=be3[:, :, d], op=ALU.mult),
                  reads=[b_g], writes=[b_g])
        xr = A.f32(ntok)
        b_xr = Buf()
        yc = A.f32(ntok)
        b_yc = Buf()
        qn = A.bf16(ntok)
        kn = A.bf16(ntok)
        cv = A.bf16(ntok)
        b_qkv = [Buf(), Buf(), Buf()]
        outs = [qn, kn, cv]
        sqb = A.bf16(512)
        b_sqb = Buf()
        rn = A.f32(512)
        b_rn = Buf()
        nr = ntok // R
        for which in range(3):
            proj_fm(w3, which * 128, h3, ntok, lambda sb, bk: P.add("act", lambda h: h.activation(
                out=xr[:, sb * 512:(sb + 1) * 512], in_=banks[bk][:, :], func=AF.Copy), reads=bank_bufs(bk), writes=[b_xr]))
            cw = T_CONV + ((l * 3 + which) * 8 + hd) * 5
            x3 = xr.rearrange("p (r t) -> p r t", t=R)
            y3 = yc.rearrange("p (r t) -> p r t", t=R)
            P.add("dve", lambda h, cw=cw: h.tensor_scalar(out=yc, in0=xr, scalar1=tab[:, cw + 2:cw + 3], scalar2=None,
                                                         op0=ALU.mult), reads=[b_xr, b_tab], writes=[b_yc])
            for tap, sh in ((1, -1), (0, -2), (3, 1), (4, 2)):
                if sh < 0:
                    ysl = y3[:, :, -sh:R]
                    xsl = x3[:, :, 0:R + sh]
                else:
                    ysl = y3[:, :, 0:R - sh]
                    xsl = x3[:, :, sh:R]
                P.add("dve", lambda h, cw=cw, tap=tap, ysl=ysl, xsl=xsl: h.scalar_tensor_tensor(
                    out=ysl, in0=xsl, scalar=tab[:, cw + tap:cw + tap + 1], in1=ysl, op0=ALU.mult, op1=ALU.add),
                    reads=[b_xr, b_tab, b_yc], writes=[b_yc])
            if which == 2:
                P.add("act", lambda h: h.activation(out=cv, in_=yc, func=AF.Silu), reads=[b_yc], writes=[b_qkv[2]])
            else:
                P.add("act", lambda h: h.activation(out=yc, in_=yc, func=AF.Silu), reads=[b_yc], writes=[b_yc])
                for sb in range(ntok // 512):
                    ssl = slice(sb * 512, (sb + 1) * 512)
                    bk = 4 + sb % 2
                    P.add("act", lambda h, ssl=ssl: h.activation(out=sqb, in_=yc[:, ssl], func=AF.Square), reads=[b_yc],
                          writes=[b_sqb])
                    P.add("pe", lambda h, bk=bk: h.matmul(banks[bk][:, :], lhsT=ones_b, rhs=sqb, start=True, stop=True),
                          reads=[b_sqb, b_cbf], writes=bank_bufs(bk))
                    P.add("act", lambda h, bk=bk: h.activation(out=rn, in_=banks[bk][:, :], func=AF.Ln, bias=EPS, scale=1.0),
                          reads=bank_bufs(bk), writes=[b_rn])
                    P.add("act", lambda h: h.activation(out=rn, in_=rn, func=AF.Exp, scale=-0.5), reads=[b_rn], writes=[b_rn])
                    sc_ = QS if which == 0 else 1.0
                    P.add("dve", lambda h, which=which, ssl=ssl, sc_=sc_: h.scalar_tensor_tensor(
                        out=outs[which][:, ssl], in0=yc[:, ssl], scalar=sc_, in1=rn, op0=ALU.mult, op1=ALU.mult),
                        reads=[b_yc, b_rn], writes=[b_qkv[which]])
        knT = A.bf16(nT * 128)
        vT = A.bf16(nT * 128)
        b_tok = Buf()
        for src, dst, bsrc in ((kn, knT, b_qkv[1]), (cv, vT, b_qkv[2])):
            for g4 in range(nT // 4):
                bk = 4 + g4 % 2
                pbf = banks[bk][:, :].bitcast(BF16)

                def f(h, g4=g4, pbf=pbf, src=src):
                    ins = None
                    for u in range(4):
                        j = g4 * 4 + u
                        ins = h.transpose(pbf[:, u * 128:(u + 1) * 128], src[:, j * 128:(j + 1) * 128], ident_b)
                    return ins
                P.add("pe", f, reads=[bsrc, b_cbf], writes=bank_bufs(bk))
                P.add("act", lambda h, g4=g4, pbf=pbf, dst=dst: h.activation(
                    out=dst[:, g4 * 512:(g4 + 1) * 512], in_=pbf[:, 0:512], func=AF.Copy), reads=bank_bufs(bk),
                    writes=[b_tok])
        kn3 = knT.rearrange("p (j c) -> p j c", c=128)
        vt3 = vT.rearrange("p (j c) -> p j c", c=128)
        P.barrier()
        if hd < 7:
            P.add("pool", lambda h: h.dma_start(out=wbuf[:, 0:KC * 640], in_=whg_d[l, hd + 1]), writes=[b_w[0]], dma=True)
        def mk():
            return dict(dg=A.f32(256), grbr=A.f32(256), DT=A.f32(128), ATf=A.f32(128), aqk=A.bf16(128),
                        Tm=A.f32(128), TT=A.f32(128), Xs=A.f32(128), TTb=A.bf16(128),
                        sm=A.f32(4), EGR=A.f32(128), Rw=A.bf16(128), Ru=A.bf16(128), kd=A.bf16(128), qgb=A.bf16(128),
                        wT=A.bf16(128), us=A.f32(128), vn=A.bf16(128), b=Buf())
        W = [mk() for _ in range(4)]
        SfA = [[A.f32(128) for _ in range(2)] for _ in range(2)]
        bSfA = [[Buf(), Buf()] for _ in range(2)]
        Sb = [A.bf16(128) for _ in range(2)]
        bSb = [Buf(), Buf()]
        obanks = list(range(ntok // 512))
        for ob_ in obanks:
            P.add("dve", lambda h, ob_=ob_: h.memset(banks[ob_][:, :], 0.0), writes=bank_bufs(ob_))
        ntile_seq = T // 128
        ntot = len(seqs) * ntile_seq
        gam3 = gam.rearrange("p (d j) -> p d j", d=2)
        ngam3 = ngam.rearrange("p (d j) -> p d j", d=2)
        next_rec = [0, 0]
        DELAY = 33

        def gchain(d, par):
            w = W[d * 2 + par]
            bw = w["b"]
            pb = 4 + d * 2 + par
            for _ in range(par * DELAY):
                yield
            for g in range(par, ntot, 2):
                si = g // ntile_seq
                i = g % ntile_seq
                j = si * ntile_seq + (i if d == 0 else ntile_seq - 1 - i)
                Sf = SfA[d][si % 2]
                bSf = bSfA[d][si % 2]
                ts = slice(j * 128, (j + 1) * 128)
                last = 127 if d == 0 else 0
                gcol = gam3[:, d, j:j + 1]
                ngcol = ngam3[:, d, j:j + 1]
                bcol = be3[:, j, d:d + 1]
                P.add("dve", lambda h, w=w, gcol=gcol: h.tensor_scalar(out=w["dg"][:, 0:128], in0=ident_f, scalar1=gcol,
                                                                      scalar2=None, op0=ALU.mult),
                      reads=[b_g, b_cst], writes=[bw])
                yield
                P.add("dve", lambda h, w=w, bcol=bcol: h.tensor_scalar(out=w["dg"][:, 128:256], in0=ident_f, scalar1=bcol,
                                                                      scalar2=None, op0=ALU.mult),
                      reads=[b_g, b_cst], writes=[bw])
                yield
                P.add("pe", lambda h, w=w, pb=pb: h.matmul(banks[pb][:, 0:256], lhsT=ones_f, rhs=w["dg"], start=True, stop=True),
                      reads=[bw, b_cst], writes=[bslot[pb][0], bslot[pb][1]])
                yield
                P.add("act", lambda h, w=w, pb=pb: h.activation(out=w["grbr"], in_=banks[pb][:, 0:256], func=AF.Copy),
                      reads=[bslot[pb][0], bslot[pb][1]], writes=[bw])
                yield
                negm = negf_b if d == 0 else negb_b

                def fm(h, w=w, pb=pb, negm=negm):
                    h.matmul(banks[pb][:, 256:384], lhsT=ones_f, rhs=w["dg"][:, 0:128], start=True, stop=False,
                             skip_group_check=True)
                    return h.matmul(banks[pb][:, 256:384], lhsT=ident_b, rhs=negm, start=False, stop=True,
                                    skip_group_check=True)
                P.add("pe", fm, reads=[bw, b_cst, b_cbf], writes=[bslot[pb][2]])
                yield
                P.add("act", lambda h, w=w, pb=pb, ngcol=ngcol: h.activation(
                    out=w["DT"], in_=banks[pb][:, 256:384], func=AF.Exp, bias=ngcol, scale=1.0),
                    reads=[bslot[pb][2], b_g], writes=[bw])
                yield
                P.add("pe", lambda h, ts=ts, pb=pb: h.matmul(banks[pb][:, 0:128], lhsT=kn[:, ts], rhs=kn[:, ts], start=True, stop=True),
                      reads=[b_qkv[1]], writes=[bslot[pb][0]])
                yield
                P.add("pe", lambda h, ts=ts, pb=pb: h.matmul(banks[pb][:, 128:256], lhsT=kn[:, ts], rhs=qn[:, ts], start=True, stop=True),
                      reads=[b_qkv[0], b_qkv[1]], writes=[bslot[pb][1]])
                yield
                P.add("dve", lambda h, w=w, pb=pb: h.tensor_tensor(out=w["ATf"], in0=banks[pb][:, 0:128], in1=w["DT"], op=ALU.mult),
                      reads=[bslot[pb][0], bw], writes=[bw])
                yield
                P.add("dve", lambda h, w=w: h.tensor_tensor(out=w["ATf"], in0=w["ATf"], in1=w["grbr"][:, 128:256], op=ALU.mult),
                      reads=[bw], writes=[bw])
                yield
                P.add("dve", lambda h, w=w, pb=pb: h.tensor_tensor(out=w["aqk"], in0=banks[pb][:, 128:256], in1=w["DT"], op=ALU.mult),
                      reads=[bslot[pb][1], bw], writes=[bw])
                yield
                lmA = cst[:, C_LMF:C_LMF + 896] if d == 0 else cst[:, C_LMB:C_LMB + 896]
                lmX = cst[:, C_LMB:C_LMB + 896] if d == 0 else cst[:, C_LMF:C_LMF + 896]
                P.add("dve", lambda h, w=w, lmA=lmA: h.tensor_tensor(out=w["Xs"], in0=w["ATf"], in1=lmA[:, 0:128], op=ALU.mult),
                      reads=[bw, b_cst], writes=[bw])
                yield
                P.add("dve", lambda h, w=w: h.tensor_tensor(out=w["TT"], in0=ident_f, in1=w["Xs"], op=ALU.subtract),
                      reads=[bw, b_cst], writes=[bw])
                yield
                P.add("pe", lambda h, w=w, pb=pb: h.transpose(banks[pb][:, 384:512], w["Xs"], ident_f),
                      reads=[bw, b_cst], writes=[bslot[pb][3]])
                yield
                P.add("dve", lambda h, w=w, pb=pb: h.tensor_tensor(out=w["Tm"], in0=ident_f, in1=banks[pb][:, 384:512],
                                                                  op=ALU.subtract), reads=[bslot[pb][3], b_cst], writes=[bw])
                yield
                for lv in range(1, 7):
                    P.add("pe", lambda h, w=w, pb=pb: h.matmul(
                        banks[pb][:, 0:128], lhsT=w["ATf"], rhs=w["Tm"], start=True, stop=True),
                        reads=[bw], writes=[bslot[pb][0]])
                    yield
                    P.add("dve", lambda h, w=w, pb=pb, lmX=lmX, lv=lv: h.tensor_tensor(
                        out=w["Xs"], in0=banks[pb][:, 0:128], in1=lmX[:, lv * 128:(lv + 1) * 128], op=ALU.mult),
                        reads=[bslot[pb][0], b_cst], writes=[bw])
                    yield
                    b_ = 1 << lv
                    hx = d
                    ho = 1 - d

                    def half(t, hsel, b_=b_):
                        return t.rearrange("p (k two b) -> p k two b", two=2, b=b_)[:, :, hsel, :]

                    def comp(slot, pb=pb, b_=b_):
                        return banks[pb][:, slot * 128:slot * 128 + 64].rearrange("p (k b) -> p k b", b=b_)
                    if lv < 6:
                        P.add("pe", lambda h, w=w, o_=comp(1), r_=half(w["Xs"], hx): h.matmul(
                            o_, lhsT=w["TT"], rhs=r_, start=True, stop=True), reads=[bw], writes=[bslot[pb][1]])
                        yield
                    P.add("pe", lambda h, w=w, o_=comp(2), r_=half(w["TT"], ho): h.matmul(
                        o_, lhsT=w["Xs"], rhs=r_, start=True, stop=True), reads=[bw], writes=[bslot[pb][2]])
                    yield
                    if lv < 6:
                        P.add("dve", lambda h, t_=half(w["Tm"], hx), i_=comp(1): h.tensor_tensor(
                            out=t_, in0=t_, in1=i_, op=ALU.subtract), reads=[bw, bslot[pb][1]], writes=[bw])
                        yield
                    P.add("dve", lambda h, t_=half(w["TT"], ho), i_=comp(2): h.tensor_tensor(
                        out=t_, in0=t_, in1=i_, op=ALU.subtract), reads=[bw, bslot[pb][2]], writes=[bw])
                    yield
                P.add("act", lambda h, w=w: h.activation(out=w["TTb"], in_=w["TT"], func=AF.Copy), reads=[bw], writes=[bw])
                yield
                sm = w["sm"]
                glast = w["grbr"][:, last:last + 1]
                P.add("act", lambda h, w=w, gcol=gcol, glast=glast, sm=sm: h.activation(
                    out=sm[:, 0:1], in_=gcol, func=AF.Exp, bias=glast, scale=-1.0), reads=[bw, b_g], writes=[bw])
                yield
                P.add("act", lambda h, w=w, glast=glast, sm=sm: h.activation(out=sm[:, 1:2], in_=glast, func=AF.Exp),
                      reads=[bw], writes=[bw])
                yield
                P.add("act", lambda h, w=w: h.activation(out=w["EGR"], in_=w["grbr"][:, 0:128], func=AF.Exp), reads=[bw],
                      writes=[bw])
                yield
                rcol = rw3[:, d, j:j + 1]
                P.add("dve", lambda h, w=w, rcol=rcol, j=j: h.tensor_scalar(out=w["Rw"], in0=kn3[:, j, :], scalar1=rcol,
                                                                          scalar2=None, op0=ALU.mult),
                      reads=[b_tok, b_g], writes=[bw])
                yield
                P.add("dve", lambda h, w=w, bcol=bcol, j=j: h.tensor_scalar(out=w["Ru"], in0=vt3[:, j, :], scalar1=bcol,
                                                                          scalar2=None, op0=ALU.mult),
                      reads=[b_tok, b_g], writes=[bw])
                yield
                P.add("dve", lambda h, w=w, sm=sm, j=j: h.tensor_scalar(out=w["kd"], in0=kn3[:, j, :], scalar1=sm[:, 0:1],
                                                                      scalar2=None, op0=ALU.mult),
                      reads=[b_tok, bw], writes=[bw])
                yield
                P.add("dve", lambda h, w=w, ts=ts: h.tensor_tensor(out=w["qgb"], in0=qn[:, ts], in1=w["EGR"], op=ALU.mult),
                      reads=[b_qkv[0], bw], writes=[bw])
                yield
                P.add("pe", lambda h, w=w, pb=pb: h.matmul(banks[pb][:, 0:128], lhsT=w["Rw"], rhs=w["TTb"], start=True, stop=True),
                      reads=[bw], writes=[bslot[pb][0]])
                yield
                P.add("pe", lambda h, w=w, pb=pb: h.matmul(banks[pb][:, 128:256], lhsT=w["TTb"], rhs=w["Ru"], start=True, stop=True),
                      reads=[bw], writes=[bslot[pb][1]])
                yield
                P.add("act", lambda h, w=w, pb=pb: h.activation(out=w["wT"], in_=banks[pb][:, 0:128], func=AF.Copy),
                      reads=[bslot[pb][0]], writes=[bw])
                yield
                P.add("act", lambda h, w=w, pb=pb: h.activation(out=w["us"], in_=banks[pb][:, 128:256], func=AF.Copy),
                      reads=[bslot[pb][1]], writes=[bw])
                yield
                assert next_rec[d] == g, (d, g, next_rec[d])
                next_rec[d] = g + 1
                if i == 0:
                    if grp == 1:
                        P.add("sp", lambda h, Sf=Sf: h.dma_start(out=Sf, in_=st_gd_d[l, d, hd]), writes=[bSf], dma=True)
                    else:
                        P.add("dve", lambda h, Sf=Sf: h.memset(Sf, 0.0), writes=[bSf])
                    yield
                    P.add("pool", lambda h, Sf=Sf: h.tensor_copy(out=Sb[d], in_=Sf), reads=[bSf], writes=[bSb[d]])
                    yield
                P.add("pe", lambda h, w=w, pb=pb: h.matmul(banks[pb][:, 256:384], lhsT=w["wT"], rhs=Sb[d], start=True, stop=True),
                      reads=[bw, bSb[d]], writes=[bslot[pb][2]])
                yield
                P.add("dve", lambda h, w=w, pb=pb: h.tensor_tensor(out=w["vn"], in0=w["us"], in1=banks[pb][:, 256:384],
                                                                  op=ALU.subtract), reads=[bw, bslot[pb][2]], writes=[bw])
                yield
                obk = j // 4
                ocol = (j % 4) * 128

                def fo(h, w=w, obk=obk, ocol=ocol):
                    h.matmul(banks[obk][:, ocol:ocol + 128], lhsT=Sb[d], rhs=w["qgb"], start=False, stop=False,
                             skip_group_check=True)
                    return h.matmul(banks[obk][:, ocol:ocol + 128], lhsT=w["vn"], rhs=w["aqk"], start=False,
                                    stop=True, skip_group_check=True)
                P.add("pe", fo, reads=[bw, bSb[d]], writes=[bslot[obk][j % 4]])
                yield
                P.add("pe", lambda h, w=w, pb=pb: h.matmul(banks[pb][:, 384:512], lhsT=w["kd"], rhs=w["vn"], start=True, stop=True),
                      reads=[bw], writes=[bslot[pb][3]])
                yield
                P.add("dve", lambda h, w=w, pb=pb, sm=sm, Sf=Sf: h.scalar_tensor_tensor(
                    out=Sf, in0=Sf, scalar=sm[:, 1:2], in1=banks[pb][:, 384:512], op0=ALU.mult, op1=ALU.add),
                    reads=[bSf, bw, bslot[pb][3]], writes=[bSf])
                yield
                P.add("pool", lambda h, Sf=Sf: h.tensor_copy(out=Sb[d], in_=Sf), reads=[bSf], writes=[bSb[d]])
                yield
                if grp == 0 and i == ntile_seq - 1:
                    P.add("sp", lambda h, Sf=Sf, s=seqs[si]: h.dma_start(out=ngd_out[s, l, d, hd], in_=Sf), reads=[bSf],
                          dma=True)
                    yield

        round_robin([gchain(0, 0), gchain(1, 0), gchain(0, 1), gchain(1, 1)])
        P.barrier()
        head_finalize(l, 8 + hd, T_GDNW + l, ntok, tok0, gsil, b_gs, obanks)
        A.release()

    def phase_mixer(l):
        for grp in (1, 0):
            A.mark()
            if grp == 1:
                ntok, tok0, seqs, T, r = 2048, 1024, [0], 2048, 1
            else:
                ntok, tok0, seqs, T, r = 1024, 0, [0, 1, 2, 3], 256, 0
            hT = A.bf16(KC * ntok)
            h3 = hT.rearrange("p (k t) -> p k t", k=KC)
            for half in range(ntok // 1024):
                hv = h3[:, :, half * 1024:(half + 1) * 1024]
                norm_mod_view(X, tok0 + half * 1024, 1024, l, 1, r, hv)
            wbuf = A.bf16(KC * 640)
            for hd in range(8):
                hgrn2_head(l, hd, grp, h3, wbuf, ntok, tok0, seqs, T)
                gdn_head(l, hd, grp, h3, wbuf, ntok, tok0, seqs, T)
            A.release()
            A.mark()
            ot = [A.bf16(KC * 512) for _ in range(2)]
            otb = [Buf() for _ in range(2)]
            xo = [A.f32(KC * 512) for _ in range(2)]
            xob = [Buf() for _ in range(2)]
            wres = A.bf16(16 * KC * 128)
            wrb = [Buf() for _ in range(16)]
            nsb = ntok // 512

            def load_tok(sb):
                t0 = tok0 + sb * 512
                q = sb % 2
                o3 = ot[q].rearrange("p (k t) -> p k t", k=KC)
                x3 = xo[q].rearrange("p (k t) -> p k t", k=KC)
                P.add("sp", lambda h: h.dma_start(out=o3, in_=OT[:, :, t0:t0 + 512]), writes=[otb[q]], dma=True)
                P.add("sp", lambda h: h.dma_start(out=x3, in_=X[:, :, t0:t0 + 512]), writes=[xob[q]], dma=True)

            load_tok(0)
            for i in range(16):
                P.add("pool", lambda h, i=i: h.dma_start(out=wres[:, i * KC * 128:(i + 1) * KC * 128], in_=wmo_d[l, i]),
                      writes=[wrb[i]], dma=True)
            for sb in range(nsb):
                t0 = tok0 + sb * 512
                q = sb % 2
                o3 = ot[q].rearrange("p (k t) -> p k t", k=KC)
                x3 = xo[q].rearrange("p (k t) -> p k t", k=KC)
                if sb + 1 < nsb:
                    load_tok(sb + 1)
                for i in range(16):
                    w3 = wres[:, i * KC * 128:(i + 1) * KC * 128].rearrange("p (k c) -> p k c", k=KC)
                    yb = 5 + i % 2
                    ig = tix(l, 1, i, r)

                    def f(h, w3=w3, yb=yb, o3=o3):
                        ins = None
                        for kc in range(KC):
                            ins = h.matmul(banks[yb][:, :], lhsT=w3[:, kc, :], rhs=o3[:, kc, :], start=(kc == 0), stop=(kc == KC - 1))
                        return ins
                    P.add("pe", f, reads=[wrb[i], otb[q]], writes=bank_bufs(yb))
                    P.add("dve", lambda h, x3=x3, i=i, yb=yb, ig=ig: h.scalar_tensor_tensor(
                        out=x3[:, i, :], in0=banks[yb][:, :], scalar=tG[:, ig:ig + 1],
                        in1=x3[:, i, :], op0=ALU.mult, op1=ALU.add),
                        reads=bank_bufs(yb) + [xob[q], b_mod], writes=[xob[q]])
                P.add("sp", lambda h, x3=x3, t0=t0: h.dma_start(out=X[:, :, t0:t0 + 512], in_=x3), reads=[xob[q]], dma=True)
            P.barrier()
            A.release()

    def norm_mod_view(xsrc, tok0, ntok, l, j, r, hv):
        A.mark()
        xs = A.f32(KC * 512)
        xb = Buf()
        sq = A.bf16(KC * 512)
        b_sq = Buf()
        lnv = A.f32(512)
        rstd = A.f32(512)
        b_r = Buf()
        tmp = [A.f32(512) for _ in range(2)]
        tb = [Buf() for _ in range(2)]
        x3 = xs.rearrange("p (k t) -> p k t", k=KC)
        sq3 = sq.rearrange("p (k t) -> p k t", k=KC)
        for sb in range(ntok // 512):
            t0 = tok0 + sb * 512
            P.add("sp", lambda h, t0=t0: h.dma_start(out=x3, in_=xsrc[:, :, t0:t0 + 512]), writes=[xb], dma=True)
            P.add("act", lambda h: h.activation(out=sq, in_=xs, func=AF.Square), reads=[xb], writes=[b_sq])

            def f(h):
                ins = None
                for kc in range(KC):
                    ins = h.matmul(banks[0][:, :], lhsT=ones_b, rhs=sq3[:, kc, :], start=(kc == 0), stop=(kc == KC - 1))
                return ins
            P.add("pe", f, reads=[b_sq, b_cbf], writes=bank_bufs(0))
            P.add("act", lambda h: h.activation(out=lnv, in_=banks[0][:, :], func=AF.Ln, bias=EPS, scale=1.0 / D),
                  reads=bank_bufs(0), writes=[b_r])
            P.add("act", lambda h: h.activation(out=rstd, in_=lnv, func=AF.Exp, scale=-0.5), reads=[b_r], writes=[b_r])
            for kc in range(KC):
                k2 = kc % 2
                ia = tix(l, j, kc, r)
                P.add("dve", lambda h, kc=kc, k2=k2, ia=ia: h.scalar_tensor_tensor(
                    out=tmp[k2], in0=x3[:, kc, :], scalar=tA[:, ia:ia + 1], in1=rstd, op0=ALU.mult, op1=ALU.mult),
                    reads=[xb, b_r, b_mod], writes=[tb[k2]])
                P.add("act", lambda h, kc=kc, k2=k2, ia=ia, sb=sb: h.activation(
                    out=hv[:, kc, sb * 512:(sb + 1) * 512], in_=tmp[k2], func=AF.Identity, bias=tB[:, ia:ia + 1], scale=1.0),
                    reads=[tb[k2], b_mod], writes=[b_hT[0]])
        P.barrier()
        A.release()

    def norm_mod(xsrc, tok0, ntok, l, j, r, hT, hw, sbank):
        hv = hT.rearrange("p (k t) -> p k t", k=KC)
        norm_mod_view(xsrc, tok0, ntok, l, j, r, hv)

    P.barrier()
    phase_mod()
    src = x_in
    done = False
    for l in range(DEPTH):
        phase_ffn(l, 0, src, X)
        src = X
        if stop_after == ("ffn1", l):
            done = True
            break
        phase_mixer(l)
        if stop_after == ("mix", l):
            done = True
            break
        phase_ffn(l, 1, X, X)
    phase_final()
    P.finish()
    P.emit()
    return nc


def _prep_shared(inp):
    f = np.float32
    w_mod = np.asarray(inp["w_mod"], f)
    wmod = np.ascontiguousarray(w_mod.reshape(DEPTH, KC, 128, 36, 512).transpose(0, 3, 2, 1, 4)).reshape(DEPTH, 36, 128, KC * 512)
    fwi = np.asarray(inp["ffn_w_in"], f)
    wfi = np.ascontiguousarray(fwi.reshape(DEPTH, 2, KC, 128, 2, NFC, 128).transpose(0, 1, 5, 3, 2, 4, 6)).reshape(
        DEPTH, 2, NFC, 128, KC * 256)
    fwo = np.asarray(inp["ffn_w_out"], f)
    wfo = np.ascontiguousarray(fwo.reshape(DEPTH, 2, NFC, 128, 16, 128).transpose(0, 1, 4, 3, 2, 5)).reshape(
        DEPTH, 2, 16, 128, NFC * 128)
    w_in = np.asarray(inp["w_in"], f).reshape(DEPTH, KC, 128, 9248)
    whg = np.empty((DEPTH, 8, 128, KC, 640), f)
    wgd = np.empty((DEPTH, 8, 128, KC, 516), f)
    for hd in range(8):
        for qi, off in enumerate((0, 1024, 2048, 3072, 4096)):
            whg[:, hd, :, :, qi * 128:(qi + 1) * 128] = w_in[:, :, :, off + hd * 128:off + (hd + 1) * 128].transpose(0, 2, 1, 3)
        for qi, off in enumerate((5120, 6144, 7168, 8192)):
            wgd[:, hd, :, :, qi * 128:(qi + 1) * 128] = w_in[:, :, :, off + hd * 128:off + (hd + 1) * 128].transpose(0, 2, 1, 3)
        for qi, off in enumerate((9216, 9224, 9232, 9240)):
            wgd[:, hd, :, :, 512 + qi] = w_in[:, :, :, off + hd].transpose(0, 2, 1)
    whg = whg.reshape(DEPTH, 8, 128, KC * 640)
    wgd = wgd.reshape(DEPTH, 8, 128, KC * 516)
    w_out = np.asarray(inp["w_out"], f)
    wmo = np.ascontiguousarray(w_out.reshape(DEPTH, KC, 128, 16, 128).transpose(0, 3, 2, 1, 4)).reshape(DEPTH, 16, 128, KC * 128)
    tab = np.zeros((128, T_END), f)
    tab[:, T_BMOD:T_BMOD + 288] = np.asarray(inp["b_mod"], f).reshape(DEPTH, 144, 128).transpose(2, 0, 1).reshape(128, 288)
    tab[:, T_NORM:T_NORM + 96] = np.asarray(inp["norm_w"], f).reshape(DEPTH, 3, KC, 128).transpose(3, 0, 1, 2).reshape(128, 96)
    tab[:, T_FNW:T_FNW + 16] = np.asarray(inp["final_norm_w"], f).reshape(KC, 128).T
    tab[:, T_HGLB:T_HGLB + 32] = np.asarray(inp["hg_lower_bounds"], f).reshape(2, DEPTH, 8, 128).transpose(3, 0, 1, 2).reshape(128, 32)
    tab[:, T_HGNW:T_HGNW + 2] = np.asarray(inp["hg_norm_w"], f).T
    tab[:, T_GDNW:T_GDNW + 2] = np.asarray(inp["gd_norm_w"], f).T
    cw = np.asarray(inp["gd_conv_w"], f).reshape(DEPTH, 5, 3, 8, 128)
    tab[:, T_CONV:T_CONV + 240] = cw.transpose(4, 0, 2, 3, 1).reshape(128, 240)
    gp = np.stack([np.asarray(inp["gd_A_log"], f), np.asarray(inp["gd_dt_bias"], f)], axis=-1)
    tab[:, T_GDPAR:T_GDPAR + 64] = np.broadcast_to(gp.reshape(1, 64), (128, 64))
    return dict(wmod=wmod, wfi=wfi, wfo=wfo, whg=whg, wgd=wgd, wmo=wmo, cst=build_consts()), tab


def _prep_core(inp, core, tab):
    f = np.float32
    xp = np.asarray(inp["x_prompt"], f)[4 * core:4 * core + 4].reshape(1024, D)
    xs = np.asarray(inp["x_sample"], f)[core].reshape(2048, D)
    xt = np.concatenate([xp, xs], axis=0)
    x_in = np.ascontiguousarray(xt.T.reshape(KC, 128, NTOK).transpose(1, 0, 2))
    t = tab.copy()
    cond = np.stack([np.asarray(inp["c_ctx"], f), np.asarray(inp["c"], f)[core]], axis=0)
    t[:, T_COND:T_COND + 32] = cond.reshape(2, KC, 128).transpose(2, 1, 0).reshape(128, 32)
    return dict(x_in=x_in, tab=t,
                st_hg=np.ascontiguousarray(np.asarray(inp["state_hgrn2"], f)[core]),
                st_gd=np.ascontiguousarray(np.asarray(inp["state_gdn"], f)[core]))


def _unpack_y(y):
    return np.ascontiguousarray(y.transpose(2, 1, 0)).reshape(NTOK, D)


def kernel(**inputs):
    n = 8
    shared, tab = _prep_shared(inputs)
    nc = build_program()
    in_maps = []
    for c in range(n):
        m = dict(shared)
        m.update(_prep_core(inputs, c, tab))
        in_maps.append(m)
    res = run_bass_kernel_spmd(nc, in_maps, core_ids=list(range(n)))
    yp = np.empty((32, 256, D), np.float32)
    ys = np.empty((8, 2048, D), np.float32)
    nhg = np.empty((32, DEPTH, 2, 8, 128, 128), np.float32)
    ngd = np.empty((32, DEPTH, 2, 8, 128, 128), np.float32)
    for c in range(n):
        r = res.results[c]
        y = _unpack_y(np.asarray(r["y_out"], np.float32))
        yp[4 * c:4 * c + 4] = y[:1024].reshape(4, 256, D)
        ys[c] = y[1024:]
        nhg[4 * c:4 * c + 4] = np.asarray(r["nhg_out"], np.float32)
        ngd[4 * c:4 * c + 4] = np.asarray(r["ngd_out"], np.float32)
    return (yp, ys, nhg, ngd)
```

```python
import numpy as np
import concourse.bass as bass
import concourse.mybir as mybir
from concourse.bass_utils import run_bass_kernel_spmd

F32 = mybir.dt.float32
BF16 = mybir.dt.bfloat16
AF = mybir.ActivationFunctionType
ALU = mybir.AluOpType

D = 2048
KC = 16
DFF = 5504
NFC = 43
DEPTH = 2
NTOK = 3072
EPS = 1e-6
QS = 128.0 ** -0.5
NEG = -30000.0


class Buf:
    __slots__ = ("w", "r", "excl")

    def __init__(self, excl=False):
        self.w = None
        self.r = []
        self.excl = excl


class Op:
    __slots__ = ("eng", "fn", "waits", "is_dma", "sem", "val", "marked", "extra_wait")


class Rec:
    def __init__(self):
        self.calls = []

    def __getattr__(self, name):
        def m(*a, **k):
            self.calls.append((name, a, k))
            return self
        return m


class Prog:
    ENGS = ["pe", "act", "dve", "pool", "sp"]

    def __init__(self, nc, n_dma_sems=12):
        self.nc = nc
        self.ops = {e: [] for e in self.ENGS}
        self.dma_ops = []
        self.n_dma_sems = n_dma_sems
        self.last_real = {e: None for e in self.ENGS}
        self.dma_since_barrier = []

    def _new(self, eng, fn, dma):
        op = Op()
        op.eng = eng
        op.fn = fn
        op.is_dma = dma
        op.marked = False
        op.sem = None
        op.val = None
        op.extra_wait = None
        op.waits = []
        return op

    def add(self, eng, fn, reads=(), writes=(), dma=False):
        rec = Rec()
        fn(rec)
        op = self._new(eng, rec.calls, dma)
        waits = op.waits
        seen = set()

        def consider(d, raw):
            if d is None or id(d) in seen:
                return
            if (not d.is_dma) and (not dma) and d.eng == eng:
                if not raw or eng == "pe":
                    return
            seen.add(id(d))
            waits.append(d)

        for b in reads:
            consider(b.w, True)
            if b.excl:
                for r in b.r:
                    consider(r, False)
        for b in writes:
            consider(b.w, False)
            for r in b.r:
                consider(r, False)
        for d in waits:
            d.marked = True
        for b in reads:
            if b.excl:
                b.w = op
                b.r = []
            else:
                b.r.append(op)
        for b in writes:
            b.w = op
            b.r = []
        self.ops[eng].append(op)
        if dma:
            self.dma_ops.append(op)
            self.dma_since_barrier.append(op)
        else:
            self.last_real[eng] = op
        return op

    def barrier(self):
        lasts = [self.last_real[e] for e in self.ENGS if self.last_real[e] is not None]
        dmas = list(self.dma_since_barrier)
        self.dma_since_barrier = []
        for e in self.ENGS:
            op = self._new(e, None, False)
            for d in lasts:
                if d.eng != e:
                    op.waits.append(d)
                    d.marked = True
            op.waits.extend(dmas)
            self.ops[e].append(op)

    def finish(self):
        op = self._new("sp", None, False)
        op.waits = list(self.dma_ops)
        self.ops["sp"].append(op)

    def emit(self):
        nc = self.nc
        sems = {e: nc.alloc_semaphore(name=f"s_{e}") for e in self.ENGS}
        dma_sems = {
            q: [nc.alloc_semaphore(name=f"d_{q}{i}") for i in range(self.n_dma_sems)]
            for q in ("sp", "act", "pool")
        }
        for e in self.ENGS:
            cnt = 0
            dcnt = 0
            uses = [0] * self.n_dma_sems
            for op in self.ops[e]:
                if op.is_dma:
                    slot = dcnt % self.n_dma_sems
                    dcnt += 1
                    op.sem = dma_sems[e][slot]
                    if uses[slot] > 0:
                        op.extra_wait = (op.sem, 16 * uses[slot])
                    uses[slot] += 1
                    op.val = 16 * uses[slot]
                elif op.marked:
                    cnt += 1
                    op.sem = sems[e]
                    op.val = cnt
        progs = self.ops

        def run(e, h):
            waited = {}
            for op in progs[e]:
                ws = [(d.sem, d.val) for d in op.waits]
                if op.extra_wait is not None:
                    ws.append(op.extra_wait)
                for sem, val in ws:
                    k = id(sem)
                    if waited.get(k, 0) >= val:
                        continue
                    waited[k] = val
                    h.wait_ge(sem, val)
                if op.fn is None:
                    continue
                ins = None
                for name, a, k in op.fn:
                    ins = getattr(h, name)(*a, **k)
                if op.is_dma:
                    ins.then_inc(op.sem, 16)
                elif op.marked:
                    ins.then_inc(op.sem, 1)

        with nc.Block() as block:

            @block.sync
            def _(h):
                run("sp", h)

            @block.scalar
            def _(h):
                run("act", h)

            @block.vector
            def _(h):
                run("dve", h)

            @block.gpsimd
            def _(h):
                run("pool", h)

            @block.tensor
            def _(h):
                run("pe", h)


class Arena:
    def __init__(self, nc, nbytes):
        self.words = nbytes // 4
        self.t = nc.alloc_sbuf_tensor("arena", [128, self.words], F32)
        self.off = 0
        self.peak = 0
        self.marks = []

    def mark(self):
        self.marks.append(self.off)

    def release(self):
        self.off = self.marks.pop()

    def f32(self, n):
        o = self.off
        self.off += n
        assert self.off <= self.words, ("arena overflow", self.off * 4)
        self.peak = max(self.peak, self.off)
        return self.t[:, o:o + n]

    def bf16(self, n):
        w = (n + 1) // 2
        o = self.off
        self.off += w
        assert self.off <= self.words, ("arena overflow", self.off * 4)
        self.peak = max(self.peak, self.off)
        return self.t[:, o:o + w].bitcast(BF16)[:, 0:n]


def round_robin(gens):
    gens = list(gens)
    while gens:
        for g in list(gens):
            try:
                next(g)
            except StopIteration:
                gens.remove(g)


def pipelined(n, depth, load, compute):
    for i in range(min(depth - 1, n)):
        load(i, i % depth)
    for i in range(n):
        if i + depth - 1 < n:
            load(i + depth - 1, (i + depth - 1) % depth)
        compute(i, i % depth)


C_IDENT = 0
C_TRIU = 128
C_TRIL = 256
C_NEGF = 384
C_NEGB = 512
C_LMF = 640
C_LMB = 640 + 896
C_HMASK = 640 + 1792
C_ONES = C_HMASK + 64
C_RESET = C_ONES + 128
C_END = C_RESET + 512


def build_consts():
    c = np.zeros((128, C_END), np.float32)
    p = np.arange(128)[:, None]
    f = np.arange(128)[None, :]
    c[:, C_IDENT:C_IDENT + 128] = (p == f)
    c[:, C_TRIU:C_TRIU + 128] = (p <= f)
    c[:, C_TRIL:C_TRIL + 128] = (p >= f)
    c[:, C_NEGF:C_NEGF + 128] = np.where(p <= f, 0.0, NEG)
    c[:, C_NEGB:C_NEGB + 128] = np.where(p >= f, 0.0, NEG)
    for i in range(7):
        b = 1 << i
        same = (p // (2 * b)) == (f // (2 * b))
        lm = same & ((p % (2 * b)) < b) & ((f % (2 * b)) >= b)
        c[:, C_LMF + i * 128:C_LMF + (i + 1) * 128] = lm
        c[:, C_LMB + i * 128:C_LMB + (i + 1) * 128] = lm.T
    s = (np.arange(64) % 32)[:, None]
    t = np.arange(32)[None, :]
    c[:64, C_HMASK:C_HMASK + 32] = (s <= t)
    c[:64, C_HMASK + 32:C_HMASK + 64] = (s >= t)
    c[:, C_ONES:C_ONES + 128] = 1.0
    r = np.ones(512, np.float32)
    r[::32] = 0.0
    c[:, C_RESET:C_RESET + 512] = r[None, :]
    return c


T_COND = 0
T_BMOD = T_COND + 32
T_NORM = T_BMOD + 288
T_FNW = T_NORM + 96
T_HGLB = T_FNW + 16
T_HGNW = T_HGLB + 32
T_GDNW = T_HGNW + 2
T_CONV = T_GDNW + 2
T_GDPAR = T_CONV + 240
T_END = T_GDPAR + 64


def build_program(stop_after=None):
    nc = bass.Bass("TRN2", target_bir_lowering=False)
    P = Prog(nc)

    def din(name, shape, dt=F32):
        return nc.dram_tensor(name, list(shape), dt, kind="ExternalInput").ap()

    def dout(name, shape, dt=F32):
        return nc.dram_tensor(name, list(shape), dt, kind="ExternalOutput").ap()

    x_in = din("x_in", [128, KC, NTOK])
    tab_d = din("tab", [128, T_END])
    cst_d = din("cst", [128, C_END])
    st_hg_d = din("st_hg", [DEPTH, 2, 8, 128, 128])
    st_gd_d = din("st_gd", [DEPTH, 2, 8, 128, 128])
    wmod_d = din("wmod", [DEPTH, 36, 128, KC * 512])
    wfi_d = din("wfi", [DEPTH, 2, NFC, 128, KC * 256])
    wfo_d = din("wfo", [DEPTH, 2, 16, 128, NFC * 128])
    whg_d = din("whg", [DEPTH, 8, 128, KC * 640])
    wgd_d = din("wgd", [DEPTH, 8, 128, KC * 516])
    wmo_d = din("wmo", [DEPTH, 16, 128, KC * 128])
    y_out = dout("y_out", [128, KC, NTOK])
    nhg_out = dout("nhg_out", [4, DEPTH, 2, 8, 128, 128])
    ngd_out = dout("ngd_out", [4, DEPTH, 2, 8, 128, 128])
    X = nc.dram_tensor("Xs", [128, KC, NTOK], F32).ap()
    OT = nc.dram_tensor("OTs", [128, KC, NTOK], BF16).ap()

    A = Arena(nc, 206 * 1024)
    banks = [nc.alloc_psum_tensor(f"pb{i}", [128, 512], F32) for i in range(8)]
    bslot = []
    for _i in range(8):
        _b = Buf(excl=True)
        bslot.append([_b, _b, _b, _b])

    def bank_bufs(i):
        return bslot[i]

    cst = A.f32(C_END)
    b_cst = Buf()
    P.add("sp", lambda h: h.dma_start(out=cst, in_=cst_d), writes=[b_cst], dma=True)
    tab = A.f32(T_END)
    b_tab = Buf()
    P.add("sp", lambda h: h.dma_start(out=tab, in_=tab_d), writes=[b_tab], dma=True)
    ident_f = cst[:, C_IDENT:C_IDENT + 128]
    ones_f = cst[:, C_ONES:C_ONES + 128]
    cbf = A.bf16(512)
    b_cbf = Buf()
    ident_b = cbf[:, 0:128]
    ones_b = cbf[:, 128:256]
    negf_b = cbf[:, 256:384]
    negb_b = cbf[:, 384:512]
    P.add("dve", lambda h: h.tensor_copy(out=ident_b, in_=ident_f), reads=[b_cst], writes=[b_cbf])
    P.add("dve", lambda h: h.tensor_copy(out=ones_b, in_=ones_f), reads=[b_cst], writes=[b_cbf])
    P.add("dve", lambda h: h.tensor_copy(out=cbf[:, 256:512], in_=cst[:, C_NEGF:C_NEGF + 256]),
          reads=[b_cst], writes=[b_cbf])
    hmask = cst[0:64, C_HMASK:C_HMASK + 64]

    tA = A.f32(DEPTH * 3 * KC * 2)
    tB = A.f32(DEPTH * 3 * KC * 2)
    tG = A.f32(DEPTH * 3 * KC * 2)
    b_mod = Buf()
    lbt = A.f32(32)
    oml = A.f32(32)
    gdp = A.f32(64)
    b_par = Buf()

    def tix(l, j, kc, r):
        return ((l * 3 + j) * KC + kc) * 2 + r

    def phase_mod():
        A.mark()
        sc = A.bf16(32)
        b_sc = Buf()
        P.add("act", lambda h: h.activation(out=sc, in_=tab[:, T_COND:T_COND + 32], func=AF.Silu),
              reads=[b_tab], writes=[b_sc])
        wr = [A.bf16(KC * 512) for _ in range(2)]
        wb = [Buf() for _ in range(2)]
        modT = A.f32(DEPTH * 288)
        mps = banks[0]
        for l in range(DEPTH):
            def load(i, s, l=l):
                P.add("pool", lambda h: h.dma_start(out=wr[s], in_=wmod_d[l, i]), writes=[wb[s]], dma=True)

            def comp(i, s, l=l):
                w3 = wr[s].rearrange("p (k c) -> p k c", k=KC)
                sc3 = sc.rearrange("p (k r) -> p k r", k=KC)

                def f(h):
                    ins = None
                    for cc in range(4):
                        g = i * 4 + cc
                        for kc in range(KC):
                            ins = h.matmul(mps[:, g * 2:g * 2 + 2], lhsT=w3[:, kc, cc * 128:(cc + 1) * 128],
                                           rhs=sc3[:, kc, :], start=(kc == 0), stop=(kc == KC - 1))
                    return ins
                P.add("pe", f, reads=[wb[s], b_sc], writes=bank_bufs(0))
            pipelined(36, 2, load, comp)
            mt = modT[:, l * 288:(l + 1) * 288]
            bm = tab[:, T_BMOD + l * 144:T_BMOD + (l + 1) * 144]
            P.add("dve", lambda h, mt=mt, bm=bm: h.tensor_tensor(
                out=mt.rearrange("p (g r) -> p g r", r=2), in0=mps[:, 0:288].rearrange("p (g r) -> p g r", r=2),
                in1=bm.unsqueeze(2).to_broadcast([128, 144, 2]), op=ALU.add),
                reads=bank_bufs(0) + [b_tab], writes=[b_mod])
            for j in range(3):
                o = tix(l, j, 0, 0)
                sh = mt[:, (3 * j) * 32:(3 * j + 1) * 32]
                scl = mt[:, (3 * j + 1) * 32:(3 * j + 2) * 32]
                gt = mt[:, (3 * j + 2) * 32:(3 * j + 3) * 32]
                nw = tab[:, T_NORM + (l * 3 + j) * KC:T_NORM + (l * 3 + j + 1) * KC]
                P.add("dve", lambda h, o=o, scl=scl, nw=nw: h.scalar_tensor_tensor(
                    out=tA[:, o:o + 32].rearrange("p (k r) -> p k r", r=2),
                    in0=scl.rearrange("p (k r) -> p k r", r=2), scalar=1.0,
                    in1=nw.unsqueeze(2).to_broadcast([128, KC, 2]), op0=ALU.add, op1=ALU.mult),
                    reads=[b_mod, b_tab], writes=[b_mod])
                P.add("dve", lambda h, o=o, sh=sh: h.tensor_copy(out=tB[:, o:o + 32], in_=sh),
                      reads=[b_mod], writes=[b_mod])
                gs = 1.0 if j == 1 else 0.5
                P.add("dve", lambda h, o=o, gt=gt, gs=gs: h.tensor_scalar(
                    out=tG[:, o:o + 32], in0=gt, scalar1=gs, scalar2=None, op0=ALU.mult),
                    reads=[b_mod], writes=[b_mod])
        hg = tab[:, T_HGLB:T_HGLB + 32].rearrange("p (d l h) -> p d l h", d=2, l=2)
        lb4 = lbt.rearrange("p (d l h) -> p d l h", d=2, l=2)
        P.add("dve", lambda h: h.memset(lbt, 0.0), writes=[b_par])
        P.add("dve", lambda h: h.tensor_tensor(out=lb4[:, :, 1, :], in0=hg[:, :, 1, :], in1=hg[:, :, 0, :],
                                               op=ALU.subtract), reads=[b_tab], writes=[b_par])
        P.add("act", lambda h: h.activation(out=lb4[:, :, 1, :], in_=lb4[:, :, 1, :], func=AF.Sigmoid),
              reads=[b_par], writes=[b_par])
        P.add("dve", lambda h: h.tensor_scalar(out=oml, in0=lbt, scalar1=-1.0, scalar2=1.0,
                                               op0=ALU.mult, op1=ALU.add), reads=[b_par], writes=[b_par])
        gp = tab[:, T_GDPAR:T_GDPAR + 64].rearrange("p (x two) -> p x two", two=2)
        gd3 = gdp.rearrange("p (x two) -> p x two", two=2)
        P.add("act", lambda h: h.activation(out=gd3[:, :, 0], in_=gp[:, :, 0], func=AF.Exp),
              reads=[b_tab], writes=[b_par])
        P.add("dve", lambda h: h.tensor_scalar(out=gd3[:, :, 0], in0=gd3[:, :, 0], scalar1=-1.0, scalar2=None,
                                               op0=ALU.mult), reads=[b_par], writes=[b_par])
        P.add("dve", lambda h: h.tensor_copy(out=gd3[:, :, 1], in_=gp[:, :, 1]), reads=[b_tab], writes=[b_par])
        P.barrier()
        A.release()

    def phase_ffn(l, j, xsrc, xdst):
        jn = 0 if j == 0 else 2
        A.mark()
        hT = A.bf16(KC * 1024)
        actT = A.bf16(NFC * 1024)
        h3 = hT.rearrange("p (k t) -> p k t", k=KC)
        a3 = actT.rearrange("p (k t) -> p k t", k=NFC)
        for blk in range(3):
            r = 0 if blk == 0 else 1
            tok0 = blk * 1024
            norm_mod(xsrc, tok0, 1024, l, jn, r, hT, 1024, 0)
            A.mark()
            wr = [A.bf16(KC * 256) for _ in range(3)]
            wb = [Buf() for _ in range(3)]
            sg = [A.f32(512) for _ in range(2)]
            sgb = [Buf() for _ in range(2)]
            b_act = Buf()

            def load1(i, s):
                P.add("pool", lambda h: h.dma_start(out=wr[s], in_=wfi_d[l, j, i]), writes=[wb[s]], dma=True)

            def comp1(i, s):
                w3 = wr[s].rearrange("p (k c) -> p k c", k=KC)
                for sb in range(2):
                    q = (i * 2 + sb) % 2
                    gb, ub = 1 + q, 3 + q

                    def f(h, sb=sb, gb=gb, ub=ub):
                        ins = None
                        for kc in range(KC):
                            ins = h.matmul(banks[gb][:, :], lhsT=w3[:, kc, 0:128], rhs=h3[:, kc, sb * 512:(sb + 1) * 512],
                                           start=(kc == 0), stop=(kc == KC - 1))
                        for kc in range(KC):
                            ins = h.matmul(banks[ub][:, :], lhsT=w3[:, kc, 128:256], rhs=h3[:, kc, sb * 512:(sb + 1) * 512],
                                           start=(kc == 0), stop=(kc == KC - 1))
                        return ins
                    P.add("pe", f, reads=[wb[s]], writes=bank_bufs(gb) + bank_bufs(ub))
                    P.add("act", lambda h, q=q, gb=gb: h.activation(out=sg[q], in_=banks[gb][:, :], func=AF.Silu),
                          reads=bank_bufs(gb), writes=[sgb[q]])
                    P.add("dve", lambda h, q=q, ub=ub, sb=sb: h.tensor_tensor(
                        out=a3[:, i, sb * 512:(sb + 1) * 512], in0=sg[q], in1=banks[ub][:, :], op=ALU.mult),
                        reads=[sgb[q]] + bank_bufs(ub), writes=[b_act])
            pipelined(NFC, 3, load1, comp1)
            P.barrier()
            A.release()
            A.mark()
            wo = [A.bf16(NFC * 128) for _ in range(3)]
            wob = [Buf() for _ in range(3)]
            xo = [A.f32(1024) for _ in range(3)]
            xob = [Buf() for _ in range(3)]

            def load2(i, s):
                P.add("pool", lambda h: h.dma_start(out=wo[s], in_=wfo_d[l, j, i]), writes=[wob[s]], dma=True)
                P.add("sp", lambda h: h.dma_start(out=xo[s], in_=xsrc[:, i, tok0:tok0 + 1024]), writes=[xob[s]], dma=True)

            def comp2(i, s):
                w3 = wo[s].rearrange("p (k c) -> p k c", k=NFC)
                ig = tix(l, jn, i, r)
                for sb in range(2):
                    yb = 5 + (i * 2 + sb) % 2

                    def f(h, sb=sb, yb=yb):
                        ins = None
                        for fc in range(NFC):
                            ins = h.matmul(banks[yb][:, :], lhsT=w3[:, fc, :], rhs=a3[:, fc, sb * 512:(sb + 1) * 512],
                                           start=(fc == 0), stop=(fc == NFC - 1))
                        return ins
                    P.add("pe", f, reads=[wob[s]], writes=bank_bufs(yb))
                    P.add("dve", lambda h, sb=sb, yb=yb: h.scalar_tensor_tensor(
                        out=xo[s][:, sb * 512:(sb + 1) * 512], in0=banks[yb][:, :], scalar=tG[:, ig:ig + 1],
                        in1=xo[s][:, sb * 512:(sb + 1) * 512], op0=ALU.mult, op1=ALU.add),
                        reads=bank_bufs(yb) + [xob[s], b_mod], writes=[xob[s]])
                P.add("sp", lambda h: h.dma_start(out=xdst[:, i, tok0:tok0 + 1024], in_=xo[s]), reads=[xob[s]], dma=True)
            pipelined(16, 3, load2, comp2)
            P.barrier()
            A.release()
        A.release()

    def phase_final():
        A.mark()
        xs = A.f32(KC * 512)
        b_x = Buf()
        sq = A.bf16(KC * 512)
        b_sq = Buf()
        lnv = A.f32(512)
        rstd = A.f32(512)
        b_r = Buf()
        yo = A.f32(KC * 512)
        b_y = Buf()
        x3 = xs.rearrange("p (k t) -> p k t", k=KC)
        y3 = yo.rearrange("p (k t) -> p k t", k=KC)
        sq3 = sq.rearrange("p (k t) -> p k t", k=KC)
        for sb in range(NTOK // 512):
            t0 = sb * 512
            P.add("sp", lambda h, t0=t0: h.dma_start(out=x3, in_=X[:, :, t0:t0 + 512]), writes=[b_x], dma=True)
            P.add("act", lambda h: h.activation(out=sq, in_=xs, func=AF.Square), reads=[b_x], writes=[b_sq])

            def f(h):
                ins = None
                for kc in range(KC):
                    ins = h.matmul(banks[0][:, :], lhsT=ones_b, rhs=sq3[:, kc, :], start=(kc == 0), stop=(kc == KC - 1))
                return ins
            P.add("pe", f, reads=[b_sq, b_cbf], writes=bank_bufs(0))
            P.add("act", lambda h: h.activation(out=lnv, in_=banks[0][:, :], func=AF.Ln, bias=EPS, scale=1.0 / D),
                  reads=bank_bufs(0), writes=[b_r])
            P.add("act", lambda h: h.activation(out=rstd, in_=lnv, func=AF.Exp, scale=-0.5), reads=[b_r], writes=[b_r])
            for kc in range(KC):
                P.add("dve", lambda h, kc=kc: h.scalar_tensor_tensor(
                    out=y3[:, kc, :], in0=x3[:, kc, :], scalar=tab[:, T_FNW + kc:T_FNW + kc + 1], in1=rstd,
                    op0=ALU.mult, op1=ALU.mult), reads=[b_x, b_r, b_tab], writes=[b_y])
            P.add("sp", lambda h, t0=t0: h.dma_start(out=y_out[:, :, t0:t0 + 512], in_=y3), reads=[b_y], dma=True)
        P.barrier()
        A.release()

    def head_finalize(l, hidx, nwcol, ntok, tok0, gsil, b_gs, obanks):
        A.mark()
        osq = [A.bf16(512) for _ in range(2)]
        osb = [Buf() for _ in range(2)]
        lnv = [A.f32(512) for _ in range(2)]
        lb_ = [Buf() for _ in range(2)]
        t1 = [A.f32(512) for _ in range(2)]
        t1b = [Buf() for _ in range(2)]
        ob = [A.bf16(512) for _ in range(2)]
        obb = [Buf() for _ in range(2)]
        for sb in range(ntok // 512):
            q = sb % 2
            op_ = banks[obanks[sb]]
            P.add("act", lambda h, q=q, op_=op_: h.activation(out=osq[q], in_=op_[:, :], func=AF.Square),
                  reads=bank_bufs(obanks[sb]), writes=[osb[q]])
            sbk = 4 + q
            P.add("pe", lambda h, q=q, sbk=sbk: h.matmul(banks[sbk][:, :], lhsT=ones_b, rhs=osq[q], start=True, stop=True),
                  reads=[osb[q], b_cbf], writes=bank_bufs(sbk))
            P.add("act", lambda h, q=q, sbk=sbk: h.activation(out=lnv[q], in_=banks[sbk][:, :], func=AF.Ln, bias=EPS,
                                                             scale=1.0 / 128.0), reads=bank_bufs(sbk), writes=[lb_[q]])
            P.add("act", lambda h, q=q: h.activation(out=lnv[q], in_=lnv[q], func=AF.Exp, scale=-0.5),
                  reads=[lb_[q]], writes=[lb_[q]])
            P.add("dve", lambda h, q=q, op_=op_: h.scalar_tensor_tensor(
                out=t1[q], in0=op_[:, :], scalar=tab[:, nwcol:nwcol + 1], in1=lnv[q], op0=ALU.mult, op1=ALU.mult),
                reads=bank_bufs(obanks[sb]) + [lb_[q], b_tab], writes=[t1b[q]])
            P.add("dve", lambda h, q=q, sb=sb: h.tensor_tensor(out=ob[q], in0=t1[q], in1=gsil[:, sb * 512:(sb + 1) * 512],
                                                               op=ALU.mult), reads=[t1b[q], b_gs], writes=[obb[q]])
            P.add("sp", lambda h, q=q, sb=sb: h.dma_start(out=OT[:, hidx, tok0 + sb * 512:tok0 + (sb + 1) * 512], in_=ob[q]),
                  reads=[obb[q]], dma=True)
        P.barrier()
        A.release()

    def proj_fm(w3, c0, h3, ntok, evac):
        for sb in range(ntok // 512):
            bk = 4 + sb % 2

            def f(h, sb=sb, bk=bk):
                ins = None
                for kc in range(KC):
                    ins = h.matmul(banks[bk][:, :], lhsT=w3[:, kc, c0:c0 + 128], rhs=h3[:, kc, sb * 512:(sb + 1) * 512],
                                   start=(kc == 0), stop=(kc == KC - 1))
                return ins
            P.add("pe", f, reads=[b_w[0], b_hT[0]], writes=bank_bufs(bk))
            evac(sb, bk)

    b_w = [Buf()]
    b_hT = [Buf()]

    def hgrn2_head(l, hd, grp, h3, wbuf, ntok, tok0, seqs, T):
        A.mark()
        nch = ntok // 32
        ntt = ntok // 64
        w3 = wbuf.rearrange("p (k c) -> p k c", k=KC)[:, :, 0:640]
        if hd == 0:
            P.add("pool", lambda h: h.dma_start(out=wbuf[:, 0:KC * 640], in_=whg_d[l, hd]), writes=[b_w[0]], dma=True)
        qf = A.bf16(ntok)
        b_q = Buf()
        gsil = A.bf16(ntok)
        b_gs = Buf()
        vtok = A.bf16(ntt * 128)
        b_v = Buf()
        v3 = vtok.rearrange("p (t c) -> p t c", c=128)
        qb, qg, kg, kgz, kbT, dec, b_d = [], [], [], [], [], [], []
        for d in range(2):
            qb.append(A.bf16(ntok))
            qg.append(A.bf16(ntok))
            kg.append(A.bf16(ntok))
            kgz.append(A.bf16(ntok))
            kbT.append(A.bf16(ntt * 128))
            dec.append(A.f32(nch))
            b_d.append(Buf())
            P.add("pool", lambda h, d=d: h.memset(kgz[d], 0.0), writes=[b_d[d]])
        Sf = [A.f32(128) for _ in range(2)]
        Sb = [A.bf16(128) for _ in range(2)]
        bSf = [Buf(), Buf()]
        bSb = [Buf(), Buf()]
        att = [[A.bf16(32) for _ in range(4)] for _ in range(2)]
        attb = [[Buf() for _ in range(4)] for _ in range(2)]
        A.mark()
        ff = A.f32(ntok)
        kk = A.bf16(ntok)
        bc = A.f32(ntok)
        ee = A.f32(ntok)
        kbf = A.bf16(ntok)
        b_t = Buf()
        resetm = cst[:, C_RESET:C_RESET + 512]
        proj_fm(w3, 0, h3, ntok, lambda sb, bk: P.add("act", lambda h: h.activation(
            out=qf[:, sb * 512:(sb + 1) * 512], in_=banks[bk][:, :], func=AF.Silu), reads=bank_bufs(bk), writes=[b_q]))
        proj_fm(w3, 512, h3, ntok, lambda sb, bk: P.add("act", lambda h: h.activation(
            out=gsil[:, sb * 512:(sb + 1) * 512], in_=banks[bk][:, :], func=AF.Silu), reads=bank_bufs(bk), writes=[b_gs]))
        for g4 in range(ntt // 4):
            bk = 4 + g4 % 2

            def f(h, g4=g4, bk=bk):
                ins = None
                for u in range(4):
                    tt = g4 * 4 + u
                    for kc in range(KC):
                        ins = h.matmul(banks[bk][0:64, u * 128:(u + 1) * 128], lhsT=h3[:, kc, tt * 64:(tt + 1) * 64],
                                       rhs=w3[:, kc, 128:256], start=(kc == 0), stop=(kc == KC - 1))
                return ins
            P.add("pe", f, reads=[b_w[0], b_hT[0]], writes=bank_bufs(bk))
            P.add("act", lambda h, g4=g4, bk=bk: h.activation(out=vtok[0:64, g4 * 512:(g4 + 1) * 512],
                                                            in_=banks[bk][0:64, :], func=AF.Copy),
                  reads=bank_bufs(bk), writes=[b_v])
        for d in range(2):
            il = (d * 2 + l) * 8 + hd
            proj_fm(w3, 256 + d * 128, h3, ntok, lambda sb, bk: P.add("act", lambda h: h.activation(
                out=ff[:, sb * 512:(sb + 1) * 512], in_=banks[bk][:, :], func=AF.Sigmoid),
                reads=bank_bufs(bk), writes=[b_t]))
            P.add("dve", lambda h, il=il: h.tensor_scalar(out=ff, in0=ff, scalar1=oml[:, il:il + 1],
                                                          scalar2=lbt[:, il:il + 1], op0=ALU.mult, op1=ALU.add),
                  reads=[b_t, b_par], writes=[b_t])
            P.add("dve", lambda h: h.tensor_scalar(out=kk, in0=ff, scalar1=-1.0, scalar2=1.0, op0=ALU.mult, op1=ALU.add),
                  reads=[b_t], writes=[b_t])
            P.add("act", lambda h: h.activation(out=ff, in_=ff, func=AF.Ln), reads=[b_t], writes=[b_t])
            for sg4 in range(ntok // 512):
                ssl = slice(sg4 * 512, (sg4 + 1) * 512)
                P.add("dve", lambda h, ssl=ssl: h.tensor_tensor_scan(out=bc[:, ssl], data0=resetm, data1=ff[:, ssl],
                                                                     initial=0.0, op0=ALU.mult, op1=ALU.add),
                      reads=[b_t, b_cst], writes=[b_t])
            bc3 = bc.rearrange("p (c k) -> p c k", k=32)
            ff3 = ff.rearrange("p (c k) -> p c k", k=32)
            ee3 = ee.rearrange("p (c k) -> p c k", k=32)
            if d == 1:
                P.add("dve", lambda h, bc3=bc3, ee3=ee3: h.tensor_tensor(
                    out=ee3, in0=bc3[:, :, 31:32].to_broadcast([128, nch, 32]), in1=bc3, op=ALU.subtract),
                    reads=[b_t], writes=[b_t])
                P.add("dve", lambda h: h.tensor_tensor(out=bc, in0=ee, in1=ff, op=ALU.add), reads=[b_t], writes=[b_t])
            last = 31 if d == 0 else 0
            P.add("act", lambda h: h.activation(out=ee, in_=bc, func=AF.Exp), reads=[b_t], writes=[b_t])
            P.add("dve", lambda h, d=d: h.scalar_tensor_tensor(out=qb[d], in0=qf, scalar=QS, in1=ee, op0=ALU.mult,
                                                             op1=ALU.mult), reads=[b_q, b_t], writes=[b_d[d]])
            P.add("act", lambda h, d=d, last=last, bc3=bc3: h.activation(out=dec[d], in_=bc3[:, :, last], func=AF.Exp),
                  reads=[b_t], writes=[b_d[d]])
            P.add("dve", lambda h, bc3=bc3, ff3=ff3: h.tensor_tensor(
                out=ff3, in0=bc3, in1=bc3[:, :, 15:16].to_broadcast([128, nch, 32]), op=ALU.subtract),
                reads=[b_t], writes=[b_t])
            P.add("act", lambda h: h.activation(out=ee, in_=ff, func=AF.Exp), reads=[b_t], writes=[b_t])
            P.add("dve", lambda h, d=d: h.scalar_tensor_tensor(out=qg[d], in0=qf, scalar=QS, in1=ee, op0=ALU.mult,
                                                             op1=ALU.mult), reads=[b_q, b_t], writes=[b_d[d]])
            P.add("act", lambda h: h.activation(out=ee, in_=ff, func=AF.Exp, scale=-1.0), reads=[b_t], writes=[b_t])
            P.add("dve", lambda h, d=d: h.tensor_tensor(out=kg[d], in0=kk, in1=ee, op=ALU.mult), reads=[b_t],
                  writes=[b_d[d]])
            hs = slice(0, 16) if d == 0 else slice(16, 32)
            kg3 = kg[d].rearrange("p (c k) -> p c k", k=32)
            kz3 = kgz[d].rearrange("p (c k) -> p c k", k=32)
            P.add("pool", lambda h, kg3=kg3, kz3=kz3, hs=hs: h.tensor_copy(out=kz3[:, :, hs], in_=kg3[:, :, hs]),
                  reads=[b_d[d]], writes=[b_d[d]])
            P.add("dve", lambda h, last=last, bc3=bc3, ff3=ff3: h.tensor_tensor(
                out=ff3, in0=bc3[:, :, last:last + 1].to_broadcast([128, nch, 32]), in1=bc3, op=ALU.subtract),
                reads=[b_t], writes=[b_t])
            P.add("act", lambda h: h.activation(out=ee, in_=ff, func=AF.Exp), reads=[b_t], writes=[b_t])
            P.add("dve", lambda h: h.tensor_tensor(out=kbf, in0=kk, in1=ee, op=ALU.mult), reads=[b_t], writes=[b_t])
            for g4 in range(ntt // 4):
                bk = 4 + g4 % 2
                pbf = banks[bk][:, :].bitcast(BF16)

                def f(h, g4=g4, pbf=pbf):
                    ins = None
                    for u in range(4):
                        tt = g4 * 4 + u
                        ins = h.transpose(pbf[0:64, u * 128:(u + 1) * 128], kbf[:, tt * 64:(tt + 1) * 64], ident_b)
                    return ins
                P.add("pe", f, reads=[b_t, b_cbf], writes=bank_bufs(bk))
                P.add("act", lambda h, d=d, g4=g4, pbf=pbf: h.activation(
                    out=kbT[d][0:64, g4 * 512:(g4 + 1) * 512], in_=pbf[0:64, 0:512], func=AF.Copy),
                    reads=bank_bufs(bk), writes=[b_d[d]])
        P.barrier()
        A.release()
        P.add("pool", lambda h: h.dma_start(out=wbuf[:, 0:KC * 516], in_=wgd_d[l, hd]), writes=[b_w[0]], dma=True)
        obanks = list(range(ntok // 512))
        for ob_ in obanks:
            P.add("dve", lambda h, ob_=ob_: h.memset(banks[ob_][:, :], 0.0), writes=bank_bufs(ob_))
        Sf2 = [[Sf[d], A.f32(128)] for d in range(2)]
        Sb2 = [[Sb[d], A.bf16(128)] for d in range(2)]
        bSf2 = [[Buf(), Buf()] for _ in range(2)]
        bSb2 = [[Buf(), Buf()] for _ in range(2)]
        nseqch = T // 32

        def hchain(d, si, s):
            sc0 = si * nseqch
            order = [sc0 + (i if d == 0 else nseqch - 1 - i) for i in range(nseqch)]
            k3 = kbT[d].rearrange("p (t c) -> p t c", c=128)
            if grp == 1:
                P.add("sp", lambda h: h.dma_start(out=Sf2[d][0], in_=st_hg_d[l, d, hd]), writes=[bSf2[d][0]], dma=True)
            else:
                P.add("dve", lambda h: h.memset(Sf2[d][0], 0.0), writes=[bSf2[d][0]])
            yield
            P.add("pool", lambda h: h.tensor_copy(out=Sb2[d][0], in_=Sf2[d][0]), reads=[bSf2[d][0]], writes=[bSb2[d][0]])
            yield

            def geo(i):
                c = order[i]
                bank = (4 + d) if i % 2 == 0 else (6 + d)
                return c, c // 2, 32 * (c % 2), i % 4, bank

            def stage_a(i):
                c, tt, p0, a, bank = geo(i)
                cs = slice(c * 32, (c + 1) * 32)
                c32 = c * 32

                def fa(h):
                    l0 = kgz[d] if d == 0 else kg[d]
                    l1 = kg[d] if d == 0 else kgz[d]
                    h.matmul(banks[bank][p0:p0 + 32, 0:16], lhsT=l0[:, cs], rhs=qg[d][:, c32:c32 + 16], start=True, stop=True)
                    h.matmul(banks[bank][p0:p0 + 32, 16:32], lhsT=l1[:, cs], rhs=qg[d][:, c32 + 16:c32 + 32],
                             start=True, stop=True)
                    return h.matmul(banks[bank][:, 128:256], lhsT=k3[p0:p0 + 32, tt, :], rhs=v3[p0:p0 + 32, tt, :],
                                    start=True, stop=True)
                P.add("pe", fa, reads=[b_d[d], b_v], writes=bank_bufs(bank))
                yield
                P.add("dve", lambda h: h.tensor_tensor(
                    out=att[d][a][p0:p0 + 32, :], in0=banks[bank][p0:p0 + 32, 0:32],
                    in1=hmask[p0:p0 + 32, d * 32:(d + 1) * 32], op=ALU.mult),
                    reads=bank_bufs(bank) + [b_cst], writes=[attb[d][a]])
                yield

            def stage_b(i):
                c, tt, p0, a, bank = geo(i)
                cs = slice(c * 32, (c + 1) * 32)
                obk = c // 16
                ocol = (c % 16) * 32
                k0, k1 = i % 2, (i + 1) % 2

                def fo(h):
                    h.matmul(banks[obk][:, ocol:ocol + 32], lhsT=Sb2[d][k0], rhs=qb[d][:, cs], start=False, stop=False,
                             skip_group_check=True)
                    return h.matmul(banks[obk][:, ocol:ocol + 32], lhsT=v3[p0:p0 + 32, tt, :], rhs=att[d][a][p0:p0 + 32, :],
                                    start=False, stop=True, skip_group_check=True)
                P.add("pe", fo, reads=[bSb2[d][k0], b_d[d], b_v, attb[d][a]], writes=bank_bufs(obk))
                yield
                P.add("dve", lambda h: h.scalar_tensor_tensor(
                    out=Sf2[d][k1], in0=Sf2[d][k0], scalar=dec[d][:, c:c + 1], in1=banks[bank][:, 128:256],
                    op0=ALU.mult, op1=ALU.add), reads=[bSf2[d][k0], b_d[d]] + bank_bufs(bank), writes=[bSf2[d][k1]])
                yield
                P.add("pool", lambda h: h.tensor_copy(out=Sb2[d][k1], in_=Sf2[d][k1]), reads=[bSf2[d][k1]],
                      writes=[bSb2[d][k1]])
                yield

            yield from stage_a(0)
            for i in range(nseqch):
                if i + 1 < nseqch:
                    yield from stage_a(i + 1)
                yield from stage_b(i)
            if grp == 0:
                kf = nseqch % 2
                P.add("sp", lambda h: h.dma_start(out=nhg_out[s, l, d, hd], in_=Sf2[d][kf]), reads=[bSf2[d][kf]], dma=True)
                yield

        for si, s in enumerate(seqs):
            round_robin([hchain(0, si, s), hchain(1, si, s)])
        P.barrier()
        head_finalize(l, hd, T_HGNW + l, ntok, tok0, gsil, b_gs, obanks)
        A.release()

    def gdn_head(l, hd, grp, h3, wbuf, ntok, tok0, seqs, T):
        A.mark()
        nT = ntok // 128
        R = 64 if grp == 1 else 256
        w3 = wbuf[:, 0:KC * 516].rearrange("p (k c) -> p k c", k=KC)
        gsil = A.bf16(ntok)
        b_gs = Buf()
        proj_fm(w3, 384, h3, ntok, lambda sb, bk: P.add("act", lambda h: h.activation(
            out=gsil[:, sb * 512:(sb + 1) * 512], in_=banks[bk][:, :], func=AF.Silu), reads=bank_bufs(bk), writes=[b_gs]))
        abt = A.f32(nT * 4)
        b_ab = Buf()

        def fab(h):
            ins = None
            for j in range(nT):
                for kc in range(KC):
                    ins = h.matmul(banks[4][:, j * 4:(j + 1) * 4], lhsT=h3[:, kc, j * 128:(j + 1) * 128],
                                   rhs=w3[:, kc, 512:516], start=(kc == 0), stop=(kc == KC - 1))
            return ins
        P.add("pe", fab, reads=[b_w[0], b_hT[0]], writes=bank_bufs(4))
        P.add("act", lambda h: h.activation(out=abt, in_=banks[4][:, 0:nT * 4], func=AF.Copy), reads=bank_bufs(4),
              writes=[b_ab])
        ab3 = abt.rearrange("p (j c) -> p j c", c=4)
        gg = A.f32(nT * 2)
        be = A.f32(nT * 2)
        gam = A.f32(nT * 2)
        ngam = A.f32(nT * 2)
        eg = A.f32(nT * 2)
        rws = A.f32(nT * 2)
        b_g = Buf()
        gg3 = gg.rearrange("p (j d) -> p j d", d=2)
        be3 = be.rearrange("p (j d) -> p j d", d=2)
        for d in range(2):
            ip = ((l * 2 + d) * 8 + hd) * 2
            P.add("act", lambda h, d=d, ip=ip: h.activation(out=gg3[:, :, d], in_=ab3[:, :, d], func=AF.Exp,
                                                          bias=gdp[:, ip + 1:ip + 2], scale=1.0),
                  reads=[b_ab, b_par], writes=[b_g])
            P.add("act", lambda h, d=d: h.activation(out=gg3[:, :, d], in_=gg3[:, :, d], func=AF.Ln, bias=1.0, scale=1.0),
                  reads=[b_g], writes=[b_g])
            P.add("dve", lambda h, d=d, ip=ip: h.tensor_scalar(out=gg3[:, :, d], in0=gg3[:, :, d],
                                                             scalar1=gdp[:, ip:ip + 1], scalar2=None, op0=ALU.mult),
                  reads=[b_g, b_par], writes=[b_g])
            P.add("act", lambda h, d=d: h.activation(out=be3[:, :, d], in_=ab3[:, :, 2 + d], func=AF.Sigmoid),
                  reads=[b_ab], writes=[b_g])
        for d in range(2):
            tri = cst[:, C_TRIU:C_TRIU + 128] if d == 0 else cst[:, C_TRIL:C_TRIL + 128]
            P.add("pe", lambda h, d=d, tri=tri: h.matmul(banks[5][:, d * nT:(d + 1) * nT], lhsT=tri, rhs=gg3[:, :, d],
                                                        start=True, stop=True), reads=[b_g, b_cst], writes=bank_bufs(5))
        P.add("act", lambda h: h.activation(out=gam, in_=banks[5][:, 0:2 * nT], func=AF.Copy), reads=bank_bufs(5),
              writes=[b_g])
        P.add("dve", lambda h: h.tensor_scalar(out=ngam, in0=gam, scalar1=-1.0, scalar2=None, op0=ALU.mult),
              reads=[b_g], writes=[b_g])
        P.add("act", lambda h: h.activation(out=eg, in_=gam, func=AF.Exp), reads=[b_g], writes=[b_g])
        eg3 = eg.rearrange("p (d j) -> p d j", d=2)
        rw3 = rws.rearrange("p (d j) -> p d j", d=2)
        for d in range(2):
            P.add("dve", lambda h, d=d: h.tensor_tensor(out=rw3[:, d, :], in0=eg3[:, d, :], in1=be3[:, :, d], op=ALU.mult),
                  reads=[b_g], writes=[b_g])
        qn = A.bf16(ntok)
        kn = A.bf16(ntok)
        cv = A.bf16(ntok)
        b_qkv = [Buf(), Buf(), Buf()]
        outs = [qn, kn, cv]
        knT = A.bf16(nT * 128)
        vT = A.bf16(nT * 128)
        A.mark()
        xr2 = [A.f32(ntok) for _ in range(2)]
        b_xr2 = [Buf(), Buf()]
        yc2 = [A.f32(ntok) for _ in range(2)]
        b_yc2 = [Buf(), Buf()]
        sqb = A.bf16(512)
        b_sqb = Buf()
        rn = A.f32(512)
        b_rn = Buf()
        nr = ntok // R
        def do_proj(which):
            xr, b_xr = xr2[which % 2], b_xr2[which % 2]
            proj_fm(w3, which * 128, h3, ntok, lambda sb, bk: P.add("act", lambda h: h.activation(
                out=xr[:, sb * 512:(sb + 1) * 512], in_=banks[bk][:, :], func=AF.Copy), reads=bank_bufs(bk), writes=[b_xr]))

        def do_conv(which):
            xr, b_xr, yc, b_yc = xr2[which % 2], b_xr2[which % 2], yc2[which % 2], b_yc2[which % 2]
            cw = T_CONV + ((l * 3 + which) * 8 + hd) * 5
            x3 = xr.rearrange("p (r t) -> p r t", t=R)
            y3 = yc.rearrange("p (r t) -> p r t", t=R)
            P.add("dve", lambda h, cw=cw: h.tensor_scalar(out=yc, in0=xr, scalar1=tab[:, cw + 2:cw + 3], scalar2=None,
                                                         op0=ALU.mult), reads=[b_xr, b_tab], writes=[b_yc])
            for tap, sh in ((1, -1), (0, -2), (3, 1), (4, 2)):
                if sh < 0:
                    ysl = y3[:, :, -sh:R]
                    xsl = x3[:, :, 0:R + sh]
                else:
                    ysl = y3[:, :, 0:R - sh]
                    xsl = x3[:, :, sh:R]
                P.add("dve", lambda h, cw=cw, tap=tap, ysl=ysl, xsl=xsl: h.scalar_tensor_tensor(
                    out=ysl, in0=xsl, scalar=tab[:, cw + tap:cw + tap + 1], in1=ysl, op0=ALU.mult, op1=ALU.add),
                    reads=[b_xr, b_tab, b_yc], writes=[b_yc])
            if which == 2:
                P.add("act", lambda h: h.activation(out=cv, in_=yc, func=AF.Silu), reads=[b_yc], writes=[b_qkv[2]])
            else:
                P.add("act", lambda h: h.activation(out=yc, in_=yc, func=AF.Silu), reads=[b_yc], writes=[b_yc])
                for sb in range(ntok // 512):
                    ssl = slice(sb * 512, (sb + 1) * 512)
                    bk = 4 + sb % 2
                    P.add("act", lambda h, ssl=ssl: h.activation(out=sqb, in_=yc[:, ssl], func=AF.Square), reads=[b_yc],
                          writes=[b_sqb])
                    P.add("pe", lambda h, bk=bk: h.matmul(banks[bk][:, :], lhsT=ones_b, rhs=sqb, start=True, stop=True),
                          reads=[b_sqb, b_cbf], writes=bank_bufs(bk))
                    P.add("act", lambda h, bk=bk: h.activation(out=rn, in_=banks[bk][:, :], func=AF.Ln, bias=EPS, scale=1.0),
                          reads=bank_bufs(bk), writes=[b_rn])
                    P.add("act", lambda h: h.activation(out=rn, in_=rn, func=AF.Exp, scale=-0.5), reads=[b_rn], writes=[b_rn])
                    sc_ = QS if which == 0 else 1.0
                    P.add("dve", lambda h, which=which, ssl=ssl, sc_=sc_: h.scalar_tensor_tensor(
                        out=outs[which][:, ssl], in0=yc[:, ssl], scalar=sc_, in1=rn, op0=ALU.mult, op1=ALU.mult),
                        reads=[b_yc, b_rn], writes=[b_qkv[which]])
        do_proj(0)
        do_proj(1)
        do_conv(0)
        do_proj(2)
        do_conv(1)
        do_conv(2)
        b_tok = Buf()
        for src, dst, bsrc in ((kn, knT, b_qkv[1]), (cv, vT, b_qkv[2])):
            for g4 in range(nT // 4):
                bk = 4 + g4 % 2
                pbf = banks[bk][:, :].bitcast(BF16)

                def f(h, g4=g4, pbf=pbf, src=src):
                    ins = None
                    for u in range(4):
                        j = g4 * 4 + u
                        ins = h.transpose(pbf[:, u * 128:(u + 1) * 128], src[:, j * 128:(j + 1) * 128], ident_b)
                    return ins
                P.add("pe", f, reads=[bsrc, b_cbf], writes=bank_bufs(bk))
                P.add("act", lambda h, g4=g4, pbf=pbf, dst=dst: h.activation(
                    out=dst[:, g4 * 512:(g4 + 1) * 512], in_=pbf[:, 0:512], func=AF.Copy), reads=bank_bufs(bk),
                    writes=[b_tok])
        kn3 = knT.rearrange("p (j c) -> p j c", c=128)
        vt3 = vT.rearrange("p (j c) -> p j c", c=128)
        P.barrier()
        A.release()
        if hd < 7:
            P.add("pool", lambda h: h.dma_start(out=wbuf[:, 0:KC * 640], in_=whg_d[l, hd + 1]), writes=[b_w[0]], dma=True)
        def mk():
            return dict(dg=A.f32(256), grbr=A.f32(256), DT=A.f32(128), ATf=A.f32(128), aqk=A.bf16(128),
                        Tm=A.f32(128), TT=A.f32(128), Xs=A.f32(128), TTb=A.bf16(128),
                        sm=A.f32(4), EGR=A.f32(128), Rw=A.bf16(128), Ru=A.bf16(128), kd=A.bf16(128), qgb=A.bf16(128),
                        wT=A.bf16(128), us=A.f32(128), vn=A.bf16(128), b=Buf())
        W = [mk() for _ in range(4)]
        SfA = [[A.f32(128) for _ in range(2)] for _ in range(2)]
        bSfA = [[Buf(), Buf()] for _ in range(2)]
        Sb = [A.bf16(128) for _ in range(2)]
        bSb = [Buf(), Buf()]
        obanks = list(range(ntok // 512))
        for ob_ in obanks:
            P.add("dve", lambda h, ob_=ob_: h.memset(banks[ob_][:, :], 0.0), writes=bank_bufs(ob_))
        ntile_seq = T // 128
        ntot = len(seqs) * ntile_seq
        gam3 = gam.rearrange("p (d j) -> p d j", d=2)
        ngam3 = ngam.rearrange("p (d j) -> p d j", d=2)
        next_rec = [0, 0]
        DELAY = 33

        def gchain(d, par):
            w = W[d * 2 + par]
            bw = w["b"]
            pb = 4 + d * 2 + par
            for _ in range(par * DELAY):
                yield
            for g in range(par, ntot, 2):
                si = g // ntile_seq
                i = g % ntile_seq
                j = si * ntile_seq + (i if d == 0 else ntile_seq - 1 - i)
                Sf = SfA[d][si % 2]
                bSf = bSfA[d][si % 2]
                ts = slice(j * 128, (j + 1) * 128)
                last = 127 if d == 0 else 0
                gcol = gam3[:, d, j:j + 1]
                ngcol = ngam3[:, d, j:j + 1]
                bcol = be3[:, j, d:d + 1]
                P.add("dve", lambda h, w=w, gcol=gcol: h.tensor_scalar(out=w["dg"][:, 0:128], in0=ident_f, scalar1=gcol,
                                                                      scalar2=None, op0=ALU.mult),
                      reads=[b_g, b_cst], writes=[bw])
                yield
                P.add("dve", lambda h, w=w, bcol=bcol: h.tensor_scalar(out=w["dg"][:, 128:256], in0=ident_f, scalar1=bcol,
                                                                      scalar2=None, op0=ALU.mult),
                      reads=[b_g, b_cst], writes=[bw])
                yield
                P.add("pe", lambda h, w=w, pb=pb: h.matmul(banks[pb][:, 0:256], lhsT=ones_f, rhs=w["dg"], start=True, stop=True),
                      reads=[bw, b_cst], writes=[bslot[pb][0], bslot[pb][1]])
                yield
                P.add("act", lambda h, w=w, pb=pb: h.activation(out=w["grbr"], in_=banks[pb][:, 0:256], func=AF.Copy),
                      reads=[bslot[pb][0], bslot[pb][1]], writes=[bw])
                yield
                negm = negf_b if d == 0 else negb_b

                def fm(h, w=w, pb=pb, negm=negm):
                    h.matmul(banks[pb][:, 256:384], lhsT=ones_f, rhs=w["dg"][:, 0:128], start=True, stop=False,
                             skip_group_check=True)
                    return h.matmul(banks[pb][:, 256:384], lhsT=ident_b, rhs=negm, start=False, stop=True,
                                    skip_group_check=True)
                P.add("pe", fm, reads=[bw, b_cst, b_cbf], writes=[bslot[pb][2]])
                yield
                P.add("act", lambda h, w=w, pb=pb, ngcol=ngcol: h.activation(
                    out=w["DT"], in_=banks[pb][:, 256:384], func=AF.Exp, bias=ngcol, scale=1.0),
                    reads=[bslot[pb][2], b_g], writes=[bw])
                yield
                P.add("pe", lambda h, ts=ts, pb=pb: h.matmul(banks[pb][:, 0:128], lhsT=kn[:, ts], rhs=kn[:, ts], start=True, stop=True),
                      reads=[b_qkv[1]], writes=[bslot[pb][0]])
                yield
                P.add("pe", lambda h, ts=ts, pb=pb: h.matmul(banks[pb][:, 128:256], lhsT=kn[:, ts], rhs=qn[:, ts], start=True, stop=True),
                      reads=[b_qkv[0], b_qkv[1]], writes=[bslot[pb][1]])
                yield
                P.add("dve", lambda h, w=w, pb=pb: h.tensor_tensor(out=w["ATf"], in0=banks[pb][:, 0:128], in1=w["DT"], op=ALU.mult),
                      reads=[bslot[pb][0], bw], writes=[bw])
                yield
                P.add("dve", lambda h, w=w: h.tensor_tensor(out=w["ATf"], in0=w["ATf"], in1=w["grbr"][:, 128:256], op=ALU.mult),
                      reads=[bw], writes=[bw])
                yield
                P.add("dve", lambda h, w=w, pb=pb: h.tensor_tensor(out=w["aqk"], in0=banks[pb][:, 128:256], in1=w["DT"], op=ALU.mult),
                      reads=[bslot[pb][1], bw], writes=[bw])
                yield
                lmA = cst[:, C_LMF:C_LMF + 896] if d == 0 else cst[:, C_LMB:C_LMB + 896]
                lmX = cst[:, C_LMB:C_LMB + 896] if d == 0 else cst[:, C_LMF:C_LMF + 896]
                P.add("dve", lambda h, w=w, lmA=lmA: h.tensor_tensor(out=w["Xs"], in0=w["ATf"], in1=lmA[:, 0:128], op=ALU.mult),
                      reads=[bw, b_cst], writes=[bw])
                yield
                P.add("dve", lambda h, w=w: h.tensor_tensor(out=w["TT"], in0=ident_f, in1=w["Xs"], op=ALU.subtract),
                      reads=[bw, b_cst], writes=[bw])
                yield
                P.add("pe", lambda h, w=w, pb=pb: h.transpose(banks[pb][:, 384:512], w["Xs"], ident_f),
                      reads=[bw, b_cst], writes=[bslot[pb][3]])
                yield
                P.add("dve", lambda h, w=w, pb=pb: h.tensor_tensor(out=w["Tm"], in0=ident_f, in1=banks[pb][:, 384:512],
                                                                  op=ALU.subtract), reads=[bslot[pb][3], b_cst], writes=[bw])
                yield
                for lv in range(1, 7):
                    P.add("pe", lambda h, w=w, pb=pb: h.matmul(
                        banks[pb][:, 0:128], lhsT=w["ATf"], rhs=w["Tm"], start=True, stop=True),
                        reads=[bw], writes=[bslot[pb][0]])
                    yield
                    P.add("dve", lambda h, w=w, pb=pb, lmX=lmX, lv=lv: h.tensor_tensor(
                        out=w["Xs"], in0=banks[pb][:, 0:128], in1=lmX[:, lv * 128:(lv + 1) * 128], op=ALU.mult),
                        reads=[bslot[pb][0], b_cst], writes=[bw])
                    yield
                    b_ = 1 << lv
                    hx = d
                    ho = 1 - d

                    def half(t, hsel, b_=b_, lv=lv):
                        if lv < 5:
                            return t
                        return t.rearrange("p (k two b) -> p k two b", two=2, b=b_)[:, :, hsel, :]

                    def comp(slot, pb=pb, b_=b_, lv=lv):
                        if lv < 5:
                            return banks[pb][:, slot * 128:(slot + 1) * 128]
                        return banks[pb][:, slot * 128:slot * 128 + 64].rearrange("p (k b) -> p k b", b=b_)
                    if lv < 6:
                        P.add("pe", lambda h, w=w, o_=comp(1), r_=half(w["Xs"], hx): h.matmul(
                            o_, lhsT=w["TT"], rhs=r_, start=True, stop=True), reads=[bw], writes=[bslot[pb][1]])
                        yield
                    P.add("pe", lambda h, w=w, o_=comp(2), r_=half(w["TT"], ho): h.matmul(
                        o_, lhsT=w["Xs"], rhs=r_, start=True, stop=True), reads=[bw], writes=[bslot[pb][2]])
                    yield
                    if lv < 6:
                        P.add("dve", lambda h, t_=half(w["Tm"], hx), i_=comp(1): h.tensor_tensor(
                            out=t_, in0=t_, in1=i_, op=ALU.subtract), reads=[bw, bslot[pb][1]], writes=[bw])
                        yield
                    P.add("dve", lambda h, t_=half(w["TT"], ho), i_=comp(2): h.tensor_tensor(
                        out=t_, in0=t_, in1=i_, op=ALU.subtract), reads=[bw, bslot[pb][2]], writes=[bw])
                    yield
                P.add("act", lambda h, w=w: h.activation(out=w["TTb"], in_=w["TT"], func=AF.Copy), reads=[bw], writes=[bw])
                yield
                sm = w["sm"]
                glast = w["grbr"][:, last:last + 1]
                P.add("act", lambda h, w=w, gcol=gcol, glast=glast, sm=sm: h.activation(
                    out=sm[:, 0:1], in_=gcol, func=AF.Exp, bias=glast, scale=-1.0), reads=[bw, b_g], writes=[bw])
                yield
                P.add("act", lambda h, w=w, glast=glast, sm=sm: h.activation(out=sm[:, 1:2], in_=glast, func=AF.Exp),
                      reads=[bw], writes=[bw])
                yield
                P.add("act", lambda h, w=w: h.activation(out=w["EGR"], in_=w["grbr"][:, 0:128], func=AF.Exp), reads=[bw],
                      writes=[bw])
                yield
                rcol = rw3[:, d, j:j + 1]
                P.add("dve", lambda h, w=w, rcol=rcol, j=j: h.tensor_scalar(out=w["Rw"], in0=kn3[:, j, :], scalar1=rcol,
                                                                          scalar2=None, op0=ALU.mult),
                      reads=[b_tok, b_g], writes=[bw])
                yield
                P.add("dve", lambda h, w=w, bcol=bcol, j=j: h.tensor_scalar(out=w["Ru"], in0=vt3[:, j, :], scalar1=bcol,
                                                                          scalar2=None, op0=ALU.mult),
                      reads=[b_tok, b_g], writes=[bw])
                yield
                P.add("dve", lambda h, w=w, sm=sm, j=j: h.tensor_scalar(out=w["kd"], in0=kn3[:, j, :], scalar1=sm[:, 0:1],
                                                                      scalar2=None, op0=ALU.mult),
                      reads=[b_tok, bw], writes=[bw])
                yield
                P.add("dve", lambda h, w=w, ts=ts: h.tensor_tensor(out=w["qgb"], in0=qn[:, ts], in1=w["EGR"], op=ALU.mult),
                      reads=[b_qkv[0], bw], writes=[bw])
                yield
                P.add("pe", lambda h, w=w, pb=pb: h.matmul(banks[pb][:, 0:128], lhsT=w["Rw"], rhs=w["TTb"], start=True, stop=True),
                      reads=[bw], writes=[bslot[pb][0]])
                yield
                P.add("pe", lambda h, w=w, pb=pb: h.matmul(banks[pb][:, 128:256], lhsT=w["TTb"], rhs=w["Ru"], start=True, stop=True),
                      reads=[bw], writes=[bslot[pb][1]])
                yield
                P.add("act", lambda h, w=w, pb=pb: h.activation(out=w["wT"], in_=banks[pb][:, 0:128], func=AF.Copy),
                      reads=[bslot[pb][0]], writes=[bw])
                yield
                P.add("act", lambda h, w=w, pb=pb: h.activation(out=w["us"], in_=banks[pb][:, 128:256], func=AF.Copy),
                      reads=[bslot[pb][1]], writes=[bw])
                yield
                assert next_rec[d] == g, (d, g, next_rec[d])
                next_rec[d] = g + 1
                if i == 0:
                    if grp == 1:
                        P.add("sp", lambda h, Sf=Sf: h.dma_start(out=Sf, in_=st_gd_d[l, d, hd]), writes=[bSf], dma=True)
                    else:
                        P.add("dve", lambda h, Sf=Sf: h.memset(Sf, 0.0), writes=[bSf])
                    yield
                    P.add("pool", lambda h, Sf=Sf: h.tensor_copy(out=Sb[d], in_=Sf), reads=[bSf], writes=[bSb[d]])
                    yield
                P.add("pe", lambda h, w=w, pb=pb: h.matmul(banks[pb][:, 256:384], lhsT=w["wT"], rhs=Sb[d], start=True, stop=True),
                      reads=[bw, bSb[d]], writes=[bslot[pb][2]])
                yield
                P.add("dve", lambda h, w=w, pb=pb: h.tensor_tensor(out=w["vn"], in0=w["us"], in1=banks[pb][:, 256:384],
                                                                  op=ALU.subtract), reads=[bw, bslot[pb][2]], writes=[bw])
                yield
                obk = j // 4
                ocol = (j % 4) * 128

                def fo(h, w=w, obk=obk, ocol=ocol):
                    h.matmul(banks[obk][:, ocol:ocol + 128], lhsT=Sb[d], rhs=w["qgb"], start=False, stop=False,
                             skip_group_check=True)
                    return h.matmul(banks[obk][:, ocol:ocol + 128], lhsT=w["vn"], rhs=w["aqk"], start=False,
                                    stop=True, skip_group_check=True)
                P.add("pe", fo, reads=[bw, bSb[d]], writes=[bslot[obk][j % 4]])
                yield
                P.add("pe", lambda h, w=w, pb=pb: h.matmul(banks[pb][:, 384:512], lhsT=w["kd"], rhs=w["vn"], start=True, stop=True),
                      reads=[bw], writes=[bslot[pb][3]])
                yield
                P.add("dve", lambda h, w=w, pb=pb, sm=sm, Sf=Sf: h.scalar_tensor_tensor(
                    out=Sf, in0=Sf, scalar=sm[:, 1:2], in1=banks[pb][:, 384:512], op0=ALU.mult, op1=ALU.add),
                    reads=[bSf, bw, bslot[pb][3]], writes=[bSf])
                yield
                P.add("pool", lambda h, Sf=Sf: h.tensor_copy(out=Sb[d], in_=Sf), reads=[bSf], writes=[bSb[d]])
                yield
                if grp == 0 and i == ntile_seq - 1:
                    P.add("sp", lambda h, Sf=Sf, s=seqs[si]: h.dma_start(out=ngd_out[s, l, d, hd], in_=Sf), reads=[bSf],
                          dma=True)
                    yield

        round_robin([gchain(0, 0), gchain(1, 0), gchain(0, 1), gchain(1, 1)])
        P.barrier()
        head_finalize(l, 8 + hd, T_GDNW + l, ntok, tok0, gsil, b_gs, obanks)
        A.release()

    def phase_mixer(l):
        for grp in (1, 0):
            A.mark()
            if grp == 1:
                ntok, tok0, seqs, T, r = 2048, 1024, [0], 2048, 1
            else:
                ntok, tok0, seqs, T, r = 1024, 0, [0, 1, 2, 3], 256, 0
            hT = A.bf16(KC * ntok)
            h3 = hT.rearrange("p (k t) -> p k t", k=KC)
            for half in range(ntok // 1024):
                hv = h3[:, :, half * 1024:(half + 1) * 1024]
                norm_mod_view(X, tok0 + half * 1024, 1024, l, 1, r, hv)
            wbuf = A.bf16(KC * 640)
            for hd in range(8):
                hgrn2_head(l, hd, grp, h3, wbuf, ntok, tok0, seqs, T)
                gdn_head(l, hd, grp, h3, wbuf, ntok, tok0, seqs, T)
            A.release()
            A.mark()
            ot = [A.bf16(KC * 512) for _ in range(2)]
            otb = [Buf() for _ in range(2)]
            xo = [A.f32(KC * 512) for _ in range(2)]
            xob = [Buf() for _ in range(2)]
            wres = A.bf16(16 * KC * 128)
            wrb = [Buf() for _ in range(16)]
            nsb = ntok // 512

            def load_tok(sb):
                t0 = tok0 + sb * 512
                q = sb % 2
                o3 = ot[q].rearrange("p (k t) -> p k t", k=KC)
                x3 = xo[q].rearrange("p (k t) -> p k t", k=KC)
                P.add("sp", lambda h: h.dma_start(out=o3, in_=OT[:, :, t0:t0 + 512]), writes=[otb[q]], dma=True)
                P.add("sp", lambda h: h.dma_start(out=x3, in_=X[:, :, t0:t0 + 512]), writes=[xob[q]], dma=True)

            load_tok(0)
            for i in range(16):
                P.add("pool", lambda h, i=i: h.dma_start(out=wres[:, i * KC * 128:(i + 1) * KC * 128], in_=wmo_d[l, i]),
                      writes=[wrb[i]], dma=True)
            for sb in range(nsb):
                t0 = tok0 + sb * 512
                q = sb % 2
                o3 = ot[q].rearrange("p (k t) -> p k t", k=KC)
                x3 = xo[q].rearrange("p (k t) -> p k t", k=KC)
                if sb + 1 < nsb:
                    load_tok(sb + 1)
                for i in range(16):
                    w3 = wres[:, i * KC * 128:(i + 1) * KC * 128].rearrange("p (k c) -> p k c", k=KC)
                    yb = 5 + i % 2
                    ig = tix(l, 1, i, r)

                    def f(h, w3=w3, yb=yb, o3=o3):
                        ins = None
                        for kc in range(KC):
                            ins = h.matmul(banks[yb][:, :], lhsT=w3[:, kc, :], rhs=o3[:, kc, :], start=(kc == 0), stop=(kc == KC - 1))
                        return ins
                    P.add("pe", f, reads=[wrb[i], otb[q]], writes=bank_bufs(yb))
                    P.add("dve", lambda h, x3=x3, i=i, yb=yb, ig=ig: h.scalar_tensor_tensor(
                        out=x3[:, i, :], in0=banks[yb][:, :], scalar=tG[:, ig:ig + 1],
                        in1=x3[:, i, :], op0=ALU.mult, op1=ALU.add),
                        reads=bank_bufs(yb) + [xob[q], b_mod], writes=[xob[q]])
                P.add("sp", lambda h, x3=x3, t0=t0: h.dma_start(out=X[:, :, t0:t0 + 512], in_=x3), reads=[xob[q]], dma=True)
            P.barrier()
            A.release()

    def norm_mod_view(xsrc, tok0, ntok, l, j, r, hv):
        A.mark()
        xs = A.f32(KC * 512)
        xb = Buf()
        sq = A.bf16(KC * 512)
        b_sq = Buf()
        lnv = A.f32(512)
        rstd = A.f32(512)
        b_r = Buf()
        tmp = [A.f32(512) for _ in range(2)]
        tb = [Buf() for _ in range(2)]
        x3 = xs.rearrange("p (k t) -> p k t", k=KC)
        sq3 = sq.rearrange("p (k t) -> p k t", k=KC)
        for sb in range(ntok // 512):
            t0 = tok0 + sb * 512
            P.add("sp", lambda h, t0=t0: h.dma_start(out=x3, in_=xsrc[:, :, t0:t0 + 512]), writes=[xb], dma=True)
            P.add("act", lambda h: h.activation(out=sq, in_=xs, func=AF.Square), reads=[xb], writes=[b_sq])

            def f(h):
                ins = None
                for kc in range(KC):
                    ins = h.matmul(banks[0][:, :], lhsT=ones_b, rhs=sq3[:, kc, :], start=(kc == 0), stop=(kc == KC - 1))
                return ins
            P.add("pe", f, reads=[b_sq, b_cbf], writes=bank_bufs(0))
            P.add("act", lambda h: h.activation(out=lnv, in_=banks[0][:, :], func=AF.Ln, bias=EPS, scale=1.0 / D),
                  reads=bank_bufs(0), writes=[b_r])
            P.add("act", lambda h: h.activation(out=rstd, in_=lnv, func=AF.Exp, scale=-0.5), reads=[b_r], writes=[b_r])
            for kc in range(KC):
                k2 = kc % 2
                ia = tix(l, j, kc, r)
                P.add("dve", lambda h, kc=kc, k2=k2, ia=ia: h.scalar_tensor_tensor(
                    out=tmp[k2], in0=x3[:, kc, :], scalar=tA[:, ia:ia + 1], in1=rstd, op0=ALU.mult, op1=ALU.mult),
                    reads=[xb, b_r, b_mod], writes=[tb[k2]])
                P.add("act", lambda h, kc=kc, k2=k2, ia=ia, sb=sb: h.activation(
                    out=hv[:, kc, sb * 512:(sb + 1) * 512], in_=tmp[k2], func=AF.Identity, bias=tB[:, ia:ia + 1], scale=1.0),
                    reads=[tb[k2], b_mod], writes=[b_hT[0]])
        P.barrier()
        A.release()

    def norm_mod(xsrc, tok0, ntok, l, j, r, hT, hw, sbank):
        hv = hT.rearrange("p (k t) -> p k t", k=KC)
        norm_mod_view(xsrc, tok0, ntok, l, j, r, hv)

    P.barrier()
    phase_mod()
    src = x_in
    done = False
    for l in range(DEPTH):
        phase_ffn(l, 0, src, X)
        src = X
        if stop_after == ("ffn1", l):
            done = True
            break
        phase_mixer(l)
        if stop_after == ("mix", l):
            done = True
            break
        phase_ffn(l, 1, X, X)
    phase_final()
    P.finish()
    P.emit()
    return nc


def _prep_shared(inp):
    f = np.float32
    w_mod = np.asarray(inp["w_mod"], f)
    wmod = np.ascontiguousarray(w_mod.reshape(DEPTH, KC, 128, 36, 512).transpose(0, 3, 2, 1, 4)).reshape(DEPTH, 36, 128, KC * 512)
    fwi = np.asarray(inp["ffn_w_in"], f)
    wfi = np.ascontiguousarray(fwi.reshape(DEPTH, 2, KC, 128, 2, NFC, 128).transpose(0, 1, 5, 3, 2, 4, 6)).reshape(
        DEPTH, 2, NFC, 128, KC * 256)
    fwo = np.asarray(inp["ffn_w_out"], f)
    wfo = np.ascontiguousarray(fwo.reshape(DEPTH, 2, NFC, 128, 16, 128).transpose(0, 1, 4, 3, 2, 5)).reshape(
        DEPTH, 2, 16, 128, NFC * 128)
    w_in = np.asarray(inp["w_in"], f).reshape(DEPTH, KC, 128, 9248)
    whg = np.empty((DEPTH, 8, 128, KC, 640), f)
    wgd = np.empty((DEPTH, 8, 128, KC, 516), f)
    for hd in range(8):
        for qi, off in enumerate((0, 1024, 2048, 3072, 4096)):
            whg[:, hd, :, :, qi * 128:(qi + 1) * 128] = w_in[:, :, :, off + hd * 128:off + (hd + 1) * 128].transpose(0, 2, 1, 3)
        for qi, off in enumerate((5120, 6144, 7168, 8192)):
            wgd[:, hd, :, :, qi * 128:(qi + 1) * 128] = w_in[:, :, :, off + hd * 128:off + (hd + 1) * 128].transpose(0, 2, 1, 3)
        for qi, off in enumerate((9216, 9224, 9232, 9240)):
            wgd[:, hd, :, :, 512 + qi] = w_in[:, :, :, off + hd].transpose(0, 2, 1)
    whg = whg.reshape(DEPTH, 8, 128, KC * 640)
    wgd = wgd.reshape(DEPTH, 8, 128, KC * 516)
    w_out = np.asarray(inp["w_out"], f)
    wmo = np.ascontiguousarray(w_out.reshape(DEPTH, KC, 128, 16, 128).transpose(0, 3, 2, 1, 4)).reshape(DEPTH, 16, 128, KC * 128)
    tab = np.zeros((128, T_END), f)
    tab[:, T_BMOD:T_BMOD + 288] = np.asarray(inp["b_mod"], f).reshape(DEPTH, 144, 128).transpose(2, 0, 1).reshape(128, 288)
    tab[:, T_NORM:T_NORM + 96] = np.asarray(inp["norm_w"], f).reshape(DEPTH, 3, KC, 128).transpose(3, 0, 1, 2).reshape(128, 96)
    tab[:, T_FNW:T_FNW + 16] = np.asarray(inp["final_norm_w"], f).reshape(KC, 128).T
    tab[:, T_HGLB:T_HGLB + 32] = np.asarray(inp["hg_lower_bounds"], f).reshape(2, DEPTH, 8, 128).transpose(3, 0, 1, 2).reshape(128, 32)
    tab[:, T_HGNW:T_HGNW + 2] = np.asarray(inp["hg_norm_w"], f).T
    tab[:, T_GDNW:T_GDNW + 2] = np.asarray(inp["gd_norm_w"], f).T
    cw = np.asarray(inp["gd_conv_w"], f).reshape(DEPTH, 5, 3, 8, 128)
    tab[:, T_CONV:T_CONV + 240] = cw.transpose(4, 0, 2, 3, 1).reshape(128, 240)
    gp = np.stack([np.asarray(inp["gd_A_log"], f), np.asarray(inp["gd_dt_bias"], f)], axis=-1)
    tab[:, T_GDPAR:T_GDPAR + 64] = np.broadcast_to(gp.reshape(1, 64), (128, 64))
    return dict(wmod=wmod, wfi=wfi, wfo=wfo, whg=whg, wgd=wgd, wmo=wmo, cst=build_consts()), tab


def _prep_core(inp, core, tab):
    f = np.float32
    xp = np.asarray(inp["x_prompt"], f)[4 * core:4 * core + 4].reshape(1024, D)
    xs = np.asarray(inp["x_sample"], f)[core].reshape(2048, D)
    xt = np.concatenate([xp, xs], axis=0)
    x_in = np.ascontiguousarray(xt.T.reshape(KC, 128, NTOK).transpose(1, 0, 2))
    t = tab.copy()
    cond = np.stack([np.asarray(inp["c_ctx"], f), np.asarray(inp["c"], f)[core]], axis=0)
    t[:, T_COND:T_COND + 32] = cond.reshape(2, KC, 128).transpose(2, 1, 0).reshape(128, 32)
    return dict(x_in=x_in, tab=t,
                st_hg=np.ascontiguousarray(np.asarray(inp["state_hgrn2"], f)[core]),
                st_gd=np.ascontiguousarray(np.asarray(inp["state_gdn"], f)[core]))


def _unpack_y(y):
    return np.ascontiguousarray(y.transpose(2, 1, 0)).reshape(NTOK, D)


def kernel(**inputs):
    n = 8
    shared, tab = _prep_shared(inputs)
    nc = build_program()
    in_maps = []
    for c in range(n):
        m = dict(shared)
        m.update(_prep_core(inputs, c, tab))
        in_maps.append(m)
    res = run_bass_kernel_spmd(nc, in_maps, core_ids=list(range(n)))
    yp = np.empty((32, 256, D), np.float32)
    ys = np.empty((8, 2048, D), np.float32)
    nhg = np.empty((32, DEPTH, 2, 8, 128, 128), np.float32)
    ngd = np.empty((32, DEPTH, 2, 8, 128, 128), np.float32)
    for c in range(n):
        r = res.results[c]
        y = _unpack_y(np.asarray(r["y_out"], np.float32))
        yp[4 * c:4 * c + 4] = y[:1024].reshape(4, 256, D)
        ys[c] = y[1024:]
        nhg[4 * c:4 * c + 4] = np.asarray(r["nhg_out"], np.float32)
        ngd[4 * c:4 * c + 4] = np.asarray(r["ngd_out"], np.float32)
    return (yp, ys, nhg, ngd)
```

```python
import numpy as np
import concourse.bass as bass
import concourse.mybir as mybir
from concourse.bass_utils import run_bass_kernel_spmd

F32 = mybir.dt.float32
BF16 = mybir.dt.bfloat16
AF = mybir.ActivationFunctionType
ALU = mybir.AluOpType

D = 2048
KC = 16
DFF = 5504
NFC = 43
DEPTH = 2
NTOK = 3072
EPS = 1e-6
QS = 128.0 ** -0.5
NEG = -30000.0


class Buf:
    __slots__ = ("w", "r", "excl")

    def __init__(self, excl=False):
        self.w = None
        self.r = []
        self.excl = excl


class Op:
    __slots__ = ("eng", "fn", "waits", "is_dma", "sem", "val", "marked", "extra_wait")


class Rec:
    def __init__(self):
        self.calls = []

    def __getattr__(self, name):
        def m(*a, **k):
            self.calls.append((name, a, k))
            return self
        return m


class Prog:
    ENGS = ["pe", "act", "dve", "pool", "sp"]

    def __init__(self, nc, n_dma_sems=12):
        self.nc = nc
        self.ops = {e: [] for e in self.ENGS}
        self.dma_ops = []
        self.n_dma_sems = n_dma_sems
        self.last_real = {e: None for e in self.ENGS}
        self.dma_since_barrier = []

    def _new(self, eng, fn, dma):
        op = Op()
        op.eng = eng
        op.fn = fn
        op.is_dma = dma
        op.marked = False
        op.sem = None
        op.val = None
        op.extra_wait = None
        op.waits = []
        return op

    def add(self, eng, fn, reads=(), writes=(), dma=False):
        rec = Rec()
        fn(rec)
        op = self._new(eng, rec.calls, dma)
        waits = op.waits
        seen = set()

        def consider(d, raw):
            if d is None or id(d) in seen:
                return
            if (not d.is_dma) and (not dma) and d.eng == eng:
                if not raw or eng == "pe":
                    return
            seen.add(id(d))
            waits.append(d)

        for b in reads:
            consider(b.w, True)
            if b.excl:
                for r in b.r:
                    consider(r, False)
        for b in writes:
            consider(b.w, False)
            for r in b.r:
                consider(r, False)
        for d in waits:
            d.marked = True
        for b in reads:
            if b.excl:
                b.w = op
                b.r = []
            else:
                b.r.append(op)
        for b in writes:
            b.w = op
            b.r = []
        self.ops[eng].append(op)
        if dma:
            self.dma_ops.append(op)
            self.dma_since_barrier.append(op)
        else:
            self.last_real[eng] = op
        return op

    def barrier(self):
        lasts = [self.last_real[e] for e in self.ENGS if self.last_real[e] is not None]
        dmas = list(self.dma_since_barrier)
        self.dma_since_barrier = []
        for e in self.ENGS:
            op = self._new(e, None, False)
            for d in lasts:
                if d.eng != e:
                    op.waits.append(d)
                    d.marked = True
            op.waits.extend(dmas)
            self.ops[e].append(op)

    def finish(self):
        op = self._new("sp", None, False)
        op.waits = list(self.dma_ops)
        self.ops["sp"].append(op)

    def emit(self):
        nc = self.nc
        sems = {e: nc.alloc_semaphore(name=f"s_{e}") for e in self.ENGS}
        dma_sems = {
            q: [nc.alloc_semaphore(name=f"d_{q}{i}") for i in range(self.n_dma_sems)]
            for q in ("sp", "act", "pool")
        }
        for e in self.ENGS:
            cnt = 0
            dcnt = 0
            uses = [0] * self.n_dma_sems
            for op in self.ops[e]:
                if op.is_dma:
                    slot = dcnt % self.n_dma_sems
                    dcnt += 1
                    op.sem = dma_sems[e][slot]
                    if uses[slot] > 0:
                        op.extra_wait = (op.sem, 16 * uses[slot])
                    uses[slot] += 1
                    op.val = 16 * uses[slot]
                elif op.marked:
                    cnt += 1
                    op.sem = sems[e]
                    op.val = cnt
        progs = self.ops

        def run(e, h):
            waited = {}
            for op in progs[e]:
                ws = [(d.sem, d.val) for d in op.waits]
                if op.extra_wait is not None:
                    ws.append(op.extra_wait)
                for sem, val in ws:
                    k = id(sem)
                    if waited.get(k, 0) >= val:
                        continue
                    waited[k] = val
                    h.wait_ge(sem, val)
                if op.fn is None:
                    continue
                ins = None
                for name, a, k in op.fn:
                    ins = getattr(h, name)(*a, **k)
                if op.is_dma:
                    ins.then_inc(op.sem, 16)
                elif op.marked:
                    ins.then_inc(op.sem, 1)

        with nc.Block() as block:

            @block.sync
            def _(h):
                run("sp", h)

            @block.scalar
            def _(h):
                run("act", h)

            @block.vector
            def _(h):
                run("dve", h)

            @block.gpsimd
            def _(h):
                run("pool", h)

            @block.tensor
            def _(h):
                run("pe", h)


class Arena:
    def __init__(self, nc, nbytes):
        self.words = nbytes // 4
        self.t = nc.alloc_sbuf_tensor("arena", [128, self.words], F32)
        self.off = 0
        self.peak = 0
        self.marks = []

    def mark(self):
        self.marks.append(self.off)

    def release(self):
        self.off = self.marks.pop()

    def f32(self, n):
        o = self.off
        self.off += n
        assert self.off <= self.words, ("arena overflow", self.off * 4)
        self.peak = max(self.peak, self.off)
        return self.t[:, o:o + n]

    def bf16(self, n):
        w = (n + 1) // 2
        o = self.off
        self.off += w
        assert self.off <= self.words, ("arena overflow", self.off * 4)
        self.peak = max(self.peak, self.off)
        return self.t[:, o:o + w].bitcast(BF16)[:, 0:n]


def round_robin(gens):
    gens = list(gens)
    while gens:
        for g in list(gens):
            try:
                next(g)
            except StopIteration:
                gens.remove(g)


def pipelined(n, depth, load, compute):
    for i in range(min(depth - 1, n)):
        load(i, i % depth)
    for i in range(n):
        if i + depth - 1 < n:
            load(i + depth - 1, (i + depth - 1) % depth)
        compute(i, i % depth)


C_IDENT = 0
C_TRIU = 128
C_TRIL = 256
C_NEGF = 384
C_NEGB = 512
C_LMF = 640
C_LMB = 640 + 896
C_HMASK = 640 + 1792
C_ONES = C_HMASK + 64
C_RESET = C_ONES + 128
C_END = C_RESET + 512


def build_consts():
    c = np.zeros((128, C_END), np.float32)
    p = np.arange(128)[:, None]
    f = np.arange(128)[None, :]
    c[:, C_IDENT:C_IDENT + 128] = (p == f)
    c[:, C_TRIU:C_TRIU + 128] = (p <= f)
    c[:, C_TRIL:C_TRIL + 128] = (p >= f)
    c[:, C_NEGF:C_NEGF + 128] = np.where(p <= f, 0.0, NEG)
    c[:, C_NEGB:C_NEGB + 128] = np.where(p >= f, 0.0, NEG)
    for i in range(7):
        b = 1 << i
        same = (p // (2 * b)) == (f // (2 * b))
        lm = same & ((p % (2 * b)) < b) & ((f % (2 * b)) >= b)
        c[:, C_LMF + i * 128:C_LMF + (i + 1) * 128] = lm
        c[:, C_LMB + i * 128:C_LMB + (i + 1) * 128] = lm.T
    s = (np.arange(64) % 32)[:, None]
    t = np.arange(32)[None, :]
    c[:64, C_HMASK:C_HMASK + 32] = (s <= t)
    c[:64, C_HMASK + 32:C_HMASK + 64] = (s >= t)
    c[:, C_ONES:C_ONES + 128] = 1.0
    r = np.ones(512, np.float32)
    r[::32] = 0.0
    c[:, C_RESET:C_RESET + 512] = r[None, :]
    return c


T_COND = 0
T_BMOD = T_COND + 32
T_NORM = T_BMOD + 288
T_FNW = T_NORM + 96
T_HGLB = T_FNW + 16
T_HGNW = T_HGLB + 32
T_GDNW = T_HGNW + 2
T_CONV = T_GDNW + 2
T_GDPAR = T_CONV + 240
T_END = T_GDPAR + 64


def build_program(stop_after=None):
    nc = bass.Bass("TRN2", target_bir_lowering=False)
    P = Prog(nc)

    def din(name, shape, dt=F32):
        return nc.dram_tensor(name, list(shape), dt, kind="ExternalInput").ap()

    def dout(name, shape, dt=F32):
        return nc.dram_tensor(name, list(shape), dt, kind="ExternalOutput").ap()

    x_in = din("x_in", [128, KC, NTOK])
    tab_d = din("tab", [128, T_END])
    cst_d = din("cst", [128, C_END])
    st_hg_d = din("st_hg", [DEPTH, 2, 8, 128, 128])
    st_gd_d = din("st_gd", [DEPTH, 2, 8, 128, 128])
    wmod_d = din("wmod", [DEPTH, 36, 128, KC * 512])
    wfi_d = din("wfi", [DEPTH, 2, NFC, 128, KC * 256])
    wfo_d = din("wfo", [DEPTH, 2, 16, 128, NFC * 128])
    whg_d = din("whg", [DEPTH, 8, 128, KC * 640])
    wgd_d = din("wgd", [DEPTH, 8, 128, KC * 516])
    wmo_d = din("wmo", [DEPTH, 16, 128, KC * 128])
    y_out = dout("y_out", [128, KC, NTOK])
    nhg_out = dout("nhg_out", [4, DEPTH, 2, 8, 128, 128])
    ngd_out = dout("ngd_out", [4, DEPTH, 2, 8, 128, 128])
    X = nc.dram_tensor("Xs", [128, KC, NTOK], F32).ap()
    OT = nc.dram_tensor("OTs", [128, KC, NTOK], BF16).ap()

    A = Arena(nc, 206 * 1024)
    banks = [nc.alloc_psum_tensor(f"pb{i}", [128, 512], F32) for i in range(8)]
    bslot = []
    for _i in range(8):
        _b = Buf(excl=True)
        bslot.append([_b, _b, _b, _b])

    def bank_bufs(i):
        return bslot[i]

    cst = A.f32(C_END)
    b_cst = Buf()
    P.add("sp", lambda h: h.dma_start(out=cst, in_=cst_d), writes=[b_cst], dma=True)
    tab = A.f32(T_END)
    b_tab = Buf()
    P.add("sp", lambda h: h.dma_start(out=tab, in_=tab_d), writes=[b_tab], dma=True)
    ident_f = cst[:, C_IDENT:C_IDENT + 128]
    ones_f = cst[:, C_ONES:C_ONES + 128]
    cbf = A.bf16(512)
    b_cbf = Buf()
    ident_b = cbf[:, 0:128]
    ones_b = cbf[:, 128:256]
    negf_b = cbf[:, 256:384]
    negb_b = cbf[:, 384:512]
    P.add("dve", lambda h: h.tensor_copy(out=ident_b, in_=ident_f), reads=[b_cst], writes=[b_cbf])
    P.add("dve", lambda h: h.tensor_copy(out=ones_b, in_=ones_f), reads=[b_cst], writes=[b_cbf])
    P.add("dve", lambda h: h.tensor_copy(out=cbf[:, 256:512], in_=cst[:, C_NEGF:C_NEGF + 256]),
          reads=[b_cst], writes=[b_cbf])
    hmask = cst[0:64, C_HMASK:C_HMASK + 64]

    tA = A.f32(DEPTH * 3 * KC * 2)
    tB = A.f32(DEPTH * 3 * KC * 2)
    tG = A.f32(DEPTH * 3 * KC * 2)
    b_mod = Buf()
    lbt = A.f32(32)
    oml = A.f32(32)
    gdp = A.f32(64)
    b_par = Buf()

    def tix(l, j, kc, r):
        return ((l * 3 + j) * KC + kc) * 2 + r

    def phase_mod():
        A.mark()
        sc = A.bf16(32)
        b_sc = Buf()
        P.add("act", lambda h: h.activation(out=sc, in_=tab[:, T_COND:T_COND + 32], func=AF.Silu),
              reads=[b_tab], writes=[b_sc])
        wr = [A.bf16(KC * 512) for _ in range(2)]
        wb = [Buf() for _ in range(2)]
        modT = A.f32(DEPTH * 288)
        mps = banks[0]
        for l in range(DEPTH):
            def load(i, s, l=l):
                P.add("pool", lambda h: h.dma_start(out=wr[s], in_=wmod_d[l, i]), writes=[wb[s]], dma=True)

            def comp(i, s, l=l):
                w3 = wr[s].rearrange("p (k c) -> p k c", k=KC)
                sc3 = sc.rearrange("p (k r) -> p k r", k=KC)

                def f(h):
                    ins = None
                    for cc in range(4):
                        g = i * 4 + cc
                        for kc in range(KC):
                            ins = h.matmul(mps[:, g * 2:g * 2 + 2], lhsT=w3[:, kc, cc * 128:(cc + 1) * 128],
                                           rhs=sc3[:, kc, :], start=(kc == 0), stop=(kc == KC - 1))
                    return ins
                P.add("pe", f, reads=[wb[s], b_sc], writes=bank_bufs(0))
            pipelined(36, 2, load, comp)
            mt = modT[:, l * 288:(l + 1) * 288]
            bm = tab[:, T_BMOD + l * 144:T_BMOD + (l + 1) * 144]
            P.add("dve", lambda h, mt=mt, bm=bm: h.tensor_tensor(
                out=mt.rearrange("p (g r) -> p g r", r=2), in0=mps[:, 0:288].rearrange("p (g r) -> p g r", r=2),
                in1=bm.unsqueeze(2).to_broadcast([128, 144, 2]), op=ALU.add),
                reads=bank_bufs(0) + [b_tab], writes=[b_mod])
            for j in range(3):
                o = tix(l, j, 0, 0)
                sh = mt[:, (3 * j) * 32:(3 * j + 1) * 32]
                scl = mt[:, (3 * j + 1) * 32:(3 * j + 2) * 32]
                gt = mt[:, (3 * j + 2) * 32:(3 * j + 3) * 32]
                nw = tab[:, T_NORM + (l * 3 + j) * KC:T_NORM + (l * 3 + j + 1) * KC]
                P.add("dve", lambda h, o=o, scl=scl, nw=nw: h.scalar_tensor_tensor(
                    out=tA[:, o:o + 32].rearrange("p (k r) -> p k r", r=2),
                    in0=scl.rearrange("p (k r) -> p k r", r=2), scalar=1.0,
                    in1=nw.unsqueeze(2).to_broadcast([128, KC, 2]), op0=ALU.add, op1=ALU.mult),
                    reads=[b_mod, b_tab], writes=[b_mod])
                P.add("dve", lambda h, o=o, sh=sh: h.tensor_copy(out=tB[:, o:o + 32], in_=sh),
                      reads=[b_mod], writes=[b_mod])
                gs = 1.0 if j == 1 else 0.5
                P.add("dve", lambda h, o=o, gt=gt, gs=gs: h.tensor_scalar(
                    out=tG[:, o:o + 32], in0=gt, scalar1=gs, scalar2=None, op0=ALU.mult),
                    reads=[b_mod], writes=[b_mod])
        hg = tab[:, T_HGLB:T_HGLB + 32].rearrange("p (d l h) -> p d l h", d=2, l=2)
        lb4 = lbt.rearrange("p (d l h) -> p d l h", d=2, l=2)
        P.add("dve", lambda h: h.memset(lbt, 0.0), writes=[b_par])
        P.add("dve", lambda h: h.tensor_tensor(out=lb4[:, :, 1, :], in0=hg[:, :, 1, :], in1=hg[:, :, 0, :],
                                               op=ALU.subtract), reads=[b_tab], writes=[b_par])
        P.add("act", lambda h: h.activation(out=lb4[:, :, 1, :], in_=lb4[:, :, 1, :], func=AF.Sigmoid),
              reads=[b_par], writes=[b_par])
        P.add("dve", lambda h: h.tensor_scalar(out=oml, in0=lbt, scalar1=-1.0, scalar2=1.0,
                                               op0=ALU.mult, op1=ALU.add), reads=[b_par], writes=[b_par])
        gp = tab[:, T_GDPAR:T_GDPAR + 64].rearrange("p (x two) -> p x two", two=2)
        gd3 = gdp.rearrange("p (x two) -> p x two", two=2)
        P.add("act", lambda h: h.activation(out=gd3[:, :, 0], in_=gp[:, :, 0], func=AF.Exp),
              reads=[b_tab], writes=[b_par])
        P.add("dve", lambda h: h.tensor_scalar(out=gd3[:, :, 0], in0=gd3[:, :, 0], scalar1=-1.0, scalar2=None,
                                               op0=ALU.mult), reads=[b_par], writes=[b_par])
        P.add("dve", lambda h: h.tensor_copy(out=gd3[:, :, 1], in_=gp[:, :, 1]), reads=[b_tab], writes=[b_par])
        P.barrier()
        A.release()

    def phase_ffn(l, j, xsrc, xdst):
        jn = 0 if j == 0 else 2
        A.mark()
        hT = A.bf16(KC * 1024)
        actT = A.bf16(NFC * 1024)
        h3 = hT.rearrange("p (k t) -> p k t", k=KC)
        a3 = actT.rearrange("p (k t) -> p k t", k=NFC)
        for blk in range(3):
            r = 0 if blk == 0 else 1
            tok0 = blk * 1024
            norm_mod(xsrc, tok0, 1024, l, jn, r, hT, 1024, 0)
            A.mark()
            wr = [A.bf16(KC * 256) for _ in range(3)]
            wb = [Buf() for _ in range(3)]
            sg = [A.f32(512) for _ in range(2)]
            sgb = [Buf() for _ in range(2)]
            b_act = Buf()

            def load1(i, s):
                P.add("pool", lambda h: h.dma_start(out=wr[s], in_=wfi_d[l, j, i]), writes=[wb[s]], dma=True)

            def comp1(i, s):
                w3 = wr[s].rearrange("p (k c) -> p k c", k=KC)
                for sb in range(2):
                    q = (i * 2 + sb) % 2
                    gb, ub = 1 + q, 3 + q

                    def f(h, sb=sb, gb=gb, ub=ub):
                        ins = None
                        for kc in range(KC):
                            ins = h.matmul(banks[gb][:, :], lhsT=w3[:, kc, 0:128], rhs=h3[:, kc, sb * 512:(sb + 1) * 512],
                                           start=(kc == 0), stop=(kc == KC - 1))
                        for kc in range(KC):
                            ins = h.matmul(banks[ub][:, :], lhsT=w3[:, kc, 128:256], rhs=h3[:, kc, sb * 512:(sb + 1) * 512],
                                           start=(kc == 0), stop=(kc == KC - 1))
                        return ins
                    P.add("pe", f, reads=[wb[s]], writes=bank_bufs(gb) + bank_bufs(ub))
                    P.add("act", lambda h, q=q, gb=gb: h.activation(out=sg[q], in_=banks[gb][:, :], func=AF.Silu),
                          reads=bank_bufs(gb), writes=[sgb[q]])
                    P.add("dve", lambda h, q=q, ub=ub, sb=sb: h.tensor_tensor(
                        out=a3[:, i, sb * 512:(sb + 1) * 512], in0=sg[q], in1=banks[ub][:, :], op=ALU.mult),
                        reads=[sgb[q]] + bank_bufs(ub), writes=[b_act])
            pipelined(NFC, 3, load1, comp1)
            P.barrier()
            A.release()
            A.mark()
            wo = [A.bf16(NFC * 128) for _ in range(3)]
            wob = [Buf() for _ in range(3)]
            xo = [A.f32(1024) for _ in range(3)]
            xob = [Buf() for _ in range(3)]

            def load2(i, s):
                P.add("pool", lambda h: h.dma_start(out=wo[s], in_=wfo_d[l, j, i]), writes=[wob[s]], dma=True)
                P.add("sp", lambda h: h.dma_start(out=xo[s], in_=xsrc[:, i, tok0:tok0 + 1024]), writes=[xob[s]], dma=True)

            def comp2(i, s):
                w3 = wo[s].rearrange("p (k c) -> p k c", k=NFC)
                ig = tix(l, jn, i, r)
                for sb in range(2):
                    yb = 5 + (i * 2 + sb) % 2

                    def f(h, sb=sb, yb=yb):
                        ins = None
                        for fc in range(NFC):
                            ins = h.matmul(banks[yb][:, :], lhsT=w3[:, fc, :], rhs=a3[:, fc, sb * 512:(sb + 1) * 512],
                                           start=(fc == 0), stop=(fc == NFC - 1))
                        return ins
                    P.add("pe", f, reads=[wob[s]], writes=bank_bufs(yb))
                    P.add("dve", lambda h, sb=sb, yb=yb: h.scalar_tensor_tensor(
                        out=xo[s][:, sb * 512:(sb + 1) * 512], in0=banks[yb][:, :], scalar=tG[:, ig:ig + 1],
                        in1=xo[s][:, sb * 512:(sb + 1) * 512], op0=ALU.mult, op1=ALU.add),
                        reads=bank_bufs(yb) + [xob[s], b_mod], writes=[xob[s]])
                P.add("sp", lambda h: h.dma_start(out=xdst[:, i, tok0:tok0 + 1024], in_=xo[s]), reads=[xob[s]], dma=True)
            pipelined(16, 3, load2, comp2)
            P.barrier()
            A.release()
        A.release()

    def phase_final():
        A.mark()
        xs = A.f32(KC * 512)
        b_x = Buf()
        sq = A.bf16(KC * 512)
        b_sq = Buf()
        lnv = A.f32(512)
        rstd = A.f32(512)
        b_r = Buf()
        yo = A.f32(KC * 512)
        b_y = Buf()
        x3 = xs.rearrange("p (k t) -> p k t", k=KC)
        y3 = yo.rearrange("p (k t) -> p k t", k=KC)
        sq3 = sq.rearrange("p (k t) -> p k t", k=KC)
        for sb in range(NTOK // 512):
            t0 = sb * 512
            P.add("sp", lambda h, t0=t0: h.dma_start(out=x3, in_=X[:, :, t0:t0 + 512]), writes=[b_x], dma=True)
            P.add("act", lambda h: h.activation(out=sq, in_=xs, func=AF.Square), reads=[b_x], writes=[b_sq])

            def f(h):
                ins = None
                for kc in range(KC):
                    ins = h.matmul(banks[0][:, :], lhsT=ones_b, rhs=sq3[:, kc, :], start=(kc == 0), stop=(kc == KC - 1))
                return ins
            P.add("pe", f, reads=[b_sq, b_cbf], writes=bank_bufs(0))
            P.add("act", lambda h: h.activation(out=lnv, in_=banks[0][:, :], func=AF.Ln, bias=EPS, scale=1.0 / D),
                  reads=bank_bufs(0), writes=[b_r])
            P.add("act", lambda h: h.activation(out=rstd, in_=lnv, func=AF.Exp, scale=-0.5), reads=[b_r], writes=[b_r])
            for kc in range(KC):
                P.add("dve", lambda h, kc=kc: h.scalar_tensor_tensor(
                    out=y3[:, kc, :], in0=x3[:, kc, :], scalar=tab[:, T_FNW + kc:T_FNW + kc + 1], in1=rstd,
                    op0=ALU.mult, op1=ALU.mult), reads=[b_x, b_r, b_tab], writes=[b_y])
            P.add("sp", lambda h, t0=t0: h.dma_start(out=y_out[:, :, t0:t0 + 512], in_=y3), reads=[b_y], dma=True)
        P.barrier()
        A.release()

    def head_finalize(l, hidx, nwcol, ntok, tok0, gsil, b_gs, obanks):
        A.mark()
        osq = [A.bf16(512) for _ in range(2)]
        osb = [Buf() for _ in range(2)]
        lnv = [A.f32(512) for _ in range(2)]
        lb_ = [Buf() for _ in range(2)]
        t1 = [A.f32(512) for _ in range(2)]
        t1b = [Buf() for _ in range(2)]
        ob = [A.bf16(512) for _ in range(2)]
        obb = [Buf() for _ in range(2)]
        for sb in range(ntok // 512):
            q = sb % 2
            op_ = banks[obanks[sb]]
            P.add("act", lambda h, q=q, op_=op_: h.activation(out=osq[q], in_=op_[:, :], func=AF.Square),
                  reads=bank_bufs(obanks[sb]), writes=[osb[q]])
            sbk = 4 + q
            P.add("pe", lambda h, q=q, sbk=sbk: h.matmul(banks[sbk][:, :], lhsT=ones_b, rhs=osq[q], start=True, stop=True),
                  reads=[osb[q], b_cbf], writes=bank_bufs(sbk))
            P.add("act", lambda h, q=q, sbk=sbk: h.activation(out=lnv[q], in_=banks[sbk][:, :], func=AF.Ln, bias=EPS,
                                                             scale=1.0 / 128.0), reads=bank_bufs(sbk), writes=[lb_[q]])
            P.add("act", lambda h, q=q: h.activation(out=lnv[q], in_=lnv[q], func=AF.Exp, scale=-0.5),
                  reads=[lb_[q]], writes=[lb_[q]])
            P.add("dve", lambda h, q=q, op_=op_: h.scalar_tensor_tensor(
                out=t1[q], in0=op_[:, :], scalar=tab[:, nwcol:nwcol + 1], in1=lnv[q], op0=ALU.mult, op1=ALU.mult),
                reads=bank_bufs(obanks[sb]) + [lb_[q], b_tab], writes=[t1b[q]])
            P.add("dve", lambda h, q=q, sb=sb: h.tensor_tensor(out=ob[q], in0=t1[q], in1=gsil[:, sb * 512:(sb + 1) * 512],
                                                               op=ALU.mult), reads=[t1b[q], b_gs], writes=[obb[q]])
            P.add("sp", lambda h, q=q, sb=sb: h.dma_start(out=OT[:, hidx, tok0 + sb * 512:tok0 + (sb + 1) * 512], in_=ob[q]),
                  reads=[obb[q]], dma=True)
        P.barrier()
        A.release()

    def proj_fm(w3, c0, h3, ntok, evac):
        for sb in range(ntok // 512):
            bk = 4 + sb % 2

            def f(h, sb=sb, bk=bk):
                ins = None
                for kc in range(KC):
                    ins = h.matmul(banks[bk][:, :], lhsT=w3[:, kc, c0:c0 + 128], rhs=h3[:, kc, sb * 512:(sb + 1) * 512],
                                   start=(kc == 0), stop=(kc == KC - 1))
                return ins
            P.add("pe", f, reads=[b_w[0], b_hT[0]], writes=bank_bufs(bk))
            evac(sb, bk)

    b_w = [Buf()]
    b_hT = [Buf()]

    def hgrn2_head(l, hd, grp, h3, wbuf, ntok, tok0, seqs, T):
        A.mark()
        nch = ntok // 32
        ntt = ntok // 64
        w3 = wbuf.rearrange("p (k c) -> p k c", k=KC)[:, :, 0:640]
        if hd == 0:
            P.add("pool", lambda h: h.dma_start(out=wbuf[:, 0:KC * 640], in_=whg_d[l, hd]), writes=[b_w[0]], dma=True)
        qf = A.bf16(ntok)
        b_q = Buf()
        gsil = A.bf16(ntok)
        b_gs = Buf()
        vtok = A.bf16(ntt * 128)
        b_v = Buf()
        v3 = vtok.rearrange("p (t c) -> p t c", c=128)
        qb, qg, kg, kgz, kbT, dec, b_d = [], [], [], [], [], [], []
        for d in range(2):
            qb.append(A.bf16(ntok))
            qg.append(A.bf16(ntok))
            kg.append(A.bf16(ntok))
            kgz.append(A.bf16(ntok))
            kbT.append(A.bf16(ntt * 128))
            dec.append(A.f32(nch))
            b_d.append(Buf())
            P.add("pool", lambda h, d=d: h.memset(kgz[d], 0.0), writes=[b_d[d]])
        Sf = [A.f32(128) for _ in range(2)]
        Sb = [A.bf16(128) for _ in range(2)]
        bSf = [Buf(), Buf()]
        bSb = [Buf(), Buf()]
        att = [[A.bf16(32) for _ in range(4)] for _ in range(2)]
        attb = [[Buf() for _ in range(4)] for _ in range(2)]
        A.mark()
        ff = A.f32(ntok)
        kk = A.bf16(ntok)
        bc = A.f32(ntok)
        ee = A.f32(ntok)
        kbf = A.bf16(ntok)
        b_t = Buf()
        resetm = cst[:, C_RESET:C_RESET + 512]
        proj_fm(w3, 0, h3, ntok, lambda sb, bk: P.add("act", lambda h: h.activation(
            out=qf[:, sb * 512:(sb + 1) * 512], in_=banks[bk][:, :], func=AF.Silu), reads=bank_bufs(bk), writes=[b_q]))
        proj_fm(w3, 512, h3, ntok, lambda sb, bk: P.add("act", lambda h: h.activation(
            out=gsil[:, sb * 512:(sb + 1) * 512], in_=banks[bk][:, :], func=AF.Silu), reads=bank_bufs(bk), writes=[b_gs]))
        for g4 in range(ntt // 4):
            bk = 4 + g4 % 2

            def f(h, g4=g4, bk=bk):
                ins = None
                for u in range(4):
                    tt = g4 * 4 + u
                    for kc in range(KC):
                        ins = h.matmul(banks[bk][0:64, u * 128:(u + 1) * 128], lhsT=h3[:, kc, tt * 64:(tt + 1) * 64],
                                       rhs=w3[:, kc, 128:256], start=(kc == 0), stop=(kc == KC - 1))
                return ins
            P.add("pe", f, reads=[b_w[0], b_hT[0]], writes=bank_bufs(bk))
            P.add("act", lambda h, g4=g4, bk=bk: h.activation(out=vtok[0:64, g4 * 512:(g4 + 1) * 512],
                                                            in_=banks[bk][0:64, :], func=AF.Copy),
                  reads=bank_bufs(bk), writes=[b_v])
        for d in range(2):
            il = (d * 2 + l) * 8 + hd
            proj_fm(w3, 256 + d * 128, h3, ntok, lambda sb, bk: P.add("act", lambda h: h.activation(
                out=ff[:, sb * 512:(sb + 1) * 512], in_=banks[bk][:, :], func=AF.Sigmoid),
                reads=bank_bufs(bk), writes=[b_t]))
            P.add("dve", lambda h, il=il: h.tensor_scalar(out=ff, in0=ff, scalar1=oml[:, il:il + 1],
                                                          scalar2=lbt[:, il:il + 1], op0=ALU.mult, op1=ALU.add),
                  reads=[b_t, b_par], writes=[b_t])
            P.add("dve", lambda h: h.tensor_scalar(out=kk, in0=ff, scalar1=-1.0, scalar2=1.0, op0=ALU.mult, op1=ALU.add),
                  reads=[b_t], writes=[b_t])
            P.add("act", lambda h: h.activation(out=ff, in_=ff, func=AF.Ln), reads=[b_t], writes=[b_t])
            for sg4 in range(ntok // 512):
                ssl = slice(sg4 * 512, (sg4 + 1) * 512)
                P.add("dve", lambda h, ssl=ssl: h.tensor_tensor_scan(out=bc[:, ssl], data0=resetm, data1=ff[:, ssl],
                                                                     initial=0.0, op0=ALU.mult, op1=ALU.add),
                      reads=[b_t, b_cst], writes=[b_t])
            bc3 = bc.rearrange("p (c k) -> p c k", k=32)
            ff3 = ff.rearrange("p (c k) -> p c k", k=32)
            ee3 = ee.rearrange("p (c k) -> p c k", k=32)
            if d == 1:
                P.add("dve", lambda h, bc3=bc3, ee3=ee3: h.tensor_tensor(
                    out=ee3, in0=bc3[:, :, 31:32].to_broadcast([128, nch, 32]), in1=bc3, op=ALU.subtract),
                    reads=[b_t], writes=[b_t])
                P.add("dve", lambda h: h.tensor_tensor(out=bc, in0=ee, in1=ff, op=ALU.add), reads=[b_t], writes=[b_t])
            last = 31 if d == 0 else 0
            P.add("act", lambda h: h.activation(out=ee, in_=bc, func=AF.Exp), reads=[b_t], writes=[b_t])
            P.add("dve", lambda h, d=d: h.scalar_tensor_tensor(out=qb[d], in0=qf, scalar=QS, in1=ee, op0=ALU.mult,
                                                             op1=ALU.mult), reads=[b_q, b_t], writes=[b_d[d]])
            P.add("act", lambda h, d=d, last=last, bc3=bc3: h.activation(out=dec[d], in_=bc3[:, :, last], func=AF.Exp),
                  reads=[b_t], writes=[b_d[d]])
            P.add("dve", lambda h, bc3=bc3, ff3=ff3: h.tensor_tensor(
                out=ff3, in0=bc3, in1=bc3[:, :, 15:16].to_broadcast([128, nch, 32]), op=ALU.subtract),
                reads=[b_t], writes=[b_t])
            P.add("act", lambda h: h.activation(out=ee, in_=ff, func=AF.Exp), reads=[b_t], writes=[b_t])
            P.add("dve", lambda h, d=d: h.scalar_tensor_tensor(out=qg[d], in0=qf, scalar=QS, in1=ee, op0=ALU.mult,
                                                             op1=ALU.mult), reads=[b_q, b_t], writes=[b_d[d]])
            P.add("act", lambda h: h.activation(out=ee, in_=ff, func=AF.Exp, scale=-1.0), reads=[b_t], writes=[b_t])
            P.add("dve", lambda h, d=d: h.tensor_tensor(out=kg[d], in0=kk, in1=ee, op=ALU.mult), reads=[b_t],
                  writes=[b_d[d]])
            hs = slice(0, 16) if d == 0 else slice(16, 32)
            kg3 = kg[d].rearrange("p (c k) -> p c k", k=32)
            kz3 = kgz[d].rearrange("p (c k) -> p c k", k=32)
            P.add("pool", lambda h, kg3=kg3, kz3=kz3, hs=hs: h.tensor_copy(out=kz3[:, :, hs], in_=kg3[:, :, hs]),
                  reads=[b_d[d]], writes=[b_d[d]])
            P.add("dve", lambda h, last=last, bc3=bc3, ff3=ff3: h.tensor_tensor(
                out=ff3, in0=bc3[:, :, last:last + 1].to_broadcast([128, nch, 32]), in1=bc3, op=ALU.subtract),
                reads=[b_t], writes=[b_t])
            P.add("act", lambda h: h.activation(out=ee, in_=ff, func=AF.Exp), reads=[b_t], writes=[b_t])
            P.add("dve", lambda h: h.tensor_tensor(out=kbf, in0=kk, in1=ee, op=ALU.mult), reads=[b_t], writes=[b_t])
            for g4 in range(ntt // 4):
                bk = 4 + g4 % 2
                pbf = banks[bk][:, :].bitcast(BF16)

                def f(h, g4=g4, pbf=pbf):
                    ins = None
                    for u in range(4):
                        tt = g4 * 4 + u
                        ins = h.transpose(pbf[0:64, u * 128:(u + 1) * 128], kbf[:, tt * 64:(tt + 1) * 64], ident_b)
                    return ins
                P.add("pe", f, reads=[b_t, b_cbf], writes=bank_bufs(bk))
                P.add("act", lambda h, d=d, g4=g4, pbf=pbf: h.activation(
                    out=kbT[d][0:64, g4 * 512:(g4 + 1) * 512], in_=pbf[0:64, 0:512], func=AF.Copy),
                    reads=bank_bufs(bk), writes=[b_d[d]])
        P.barrier()
        A.release()
        P.add("pool", lambda h: h.dma_start(out=wbuf[:, 0:KC * 516], in_=wgd_d[l, hd]), writes=[b_w[0]], dma=True)
        obanks = list(range(ntok // 512))
        for ob_ in obanks:
            P.add("dve", lambda h, ob_=ob_: h.memset(banks[ob_][:, :], 0.0), writes=bank_bufs(ob_))
        Sf2 = [[Sf[d], A.f32(128)] for d in range(2)]
        Sb2 = [[Sb[d], A.bf16(128)] for d in range(2)]
        bSf2 = [[Buf(), Buf()] for _ in range(2)]
        bSb2 = [[Buf(), Buf()] for _ in range(2)]
        nseqch = T // 32

        def hchain(d, si, s):
            sc0 = si * nseqch
            order = [sc0 + (i if d == 0 else nseqch - 1 - i) for i in range(nseqch)]
            k3 = kbT[d].rearrange("p (t c) -> p t c", c=128)
            if grp == 1:
                P.add("sp", lambda h: h.dma_start(out=Sf2[d][0], in_=st_hg_d[l, d, hd]), writes=[bSf2[d][0]], dma=True)
            else:
                P.add("dve", lambda h: h.memset(Sf2[d][0], 0.0), writes=[bSf2[d][0]])
            yield
            P.add("pool", lambda h: h.tensor_copy(out=Sb2[d][0], in_=Sf2[d][0]), reads=[bSf2[d][0]], writes=[bSb2[d][0]])
            yield

            def geo(i):
                c = order[i]
                bank = (4 + d) if i % 2 == 0 else (6 + d)
                return c, c // 2, 32 * (c % 2), i % 4, bank

            def stage_a(i):
                c, tt, p0, a, bank = geo(i)
                cs = slice(c * 32, (c + 1) * 32)
                c32 = c * 32

                def fa(h):
                    l0 = kgz[d] if d == 0 else kg[d]
                    l1 = kg[d] if d == 0 else kgz[d]
                    h.matmul(banks[bank][p0:p0 + 32, 0:16], lhsT=l0[:, cs], rhs=qg[d][:, c32:c32 + 16], start=True, stop=True)
                    h.matmul(banks[bank][p0:p0 + 32, 16:32], lhsT=l1[:, cs], rhs=qg[d][:, c32 + 16:c32 + 32],
                             start=True, stop=True)
                    return h.matmul(banks[bank][:, 128:256], lhsT=k3[p0:p0 + 32, tt, :], rhs=v3[p0:p0 + 32, tt, :],
                                    start=True, stop=True)
                P.add("pe", fa, reads=[b_d[d], b_v], writes=bank_bufs(bank))
                yield
                P.add("dve", lambda h: h.tensor_tensor(
                    out=att[d][a][p0:p0 + 32, :], in0=banks[bank][p0:p0 + 32, 0:32],
                    in1=hmask[p0:p0 + 32, d * 32:(d + 1) * 32], op=ALU.mult),
                    reads=bank_bufs(bank) + [b_cst], writes=[attb[d][a]])
                yield

            def stage_b(i):
                c, tt, p0, a, bank = geo(i)
                cs = slice(c * 32, (c + 1) * 32)
                obk = c // 16
                ocol = (c % 16) * 32
                k0, k1 = i % 2, (i + 1) % 2

                def fo(h):
                    h.matmul(banks[obk][:, ocol:ocol + 32], lhsT=Sb2[d][k0], rhs=qb[d][:, cs], start=False, stop=False,
                             skip_group_check=True)
                    return h.matmul(banks[obk][:, ocol:ocol + 32], lhsT=v3[p0:p0 + 32, tt, :], rhs=att[d][a][p0:p0 + 32, :],
                                    start=False, stop=True, skip_group_check=True)
                P.add("pe", fo, reads=[bSb2[d][k0], b_d[d], b_v, attb[d][a]], writes=bank_bufs(obk))
                yield
                P.add("dve", lambda h: h.scalar_tensor_tensor(
                    out=Sf2[d][k1], in0=Sf2[d][k0], scalar=dec[d][:, c:c + 1], in1=banks[bank][:, 128:256],
                    op0=ALU.mult, op1=ALU.add), reads=[bSf2[d][k0], b_d[d]] + bank_bufs(bank), writes=[bSf2[d][k1]])
                yield
                P.add("pool", lambda h: h.tensor_copy(out=Sb2[d][k1], in_=Sf2[d][k1]), reads=[bSf2[d][k1]],
                      writes=[bSb2[d][k1]])
                yield

            yield from stage_a(0)
            for i in range(nseqch):
                if i + 1 < nseqch:
                    yield from stage_a(i + 1)
                yield from stage_b(i)
            if grp == 0:
                kf = nseqch % 2
                P.add("sp", lambda h: h.dma_start(out=nhg_out[s, l, d, hd], in_=Sf2[d][kf]), reads=[bSf2[d][kf]], dma=True)
                yield

        for si, s in enumerate(seqs):
            round_robin([hchain(0, si, s), hchain(1, si, s)])
        P.barrier()
        head_finalize(l, hd, T_HGNW + l, ntok, tok0, gsil, b_gs, obanks)
        A.release()

    def gdn_head(l, hd, grp, h3, wbuf, ntok, tok0, seqs, T):
        A.mark()
        nT = ntok // 128
        R = 64 if grp == 1 else 256
        w3 = wbuf[:, 0:KC * 516].rearrange("p (k c) -> p k c", k=KC)
        gsil = A.bf16(ntok)
        b_gs = Buf()
        proj_fm(w3, 384, h3, ntok, lambda sb, bk: P.add("act", lambda h: h.activation(
            out=gsil[:, sb * 512:(sb + 1) * 512], in_=banks[bk][:, :], func=AF.Silu), reads=bank_bufs(bk), writes=[b_gs]))
        abt = A.f32(nT * 4)
        b_ab = Buf()

        def fab(h):
            ins = None
            for j in range(nT):
                for kc in range(KC):
                    ins = h.matmul(banks[4][:, j * 4:(j + 1) * 4], lhsT=h3[:, kc, j * 128:(j + 1) * 128],
                                   rhs=w3[:, kc, 512:516], start=(kc == 0), stop=(kc == KC - 1))
            return ins
        P.add("pe", fab, reads=[b_w[0], b_hT[0]], writes=bank_bufs(4))
        P.add("act", lambda h: h.activation(out=abt, in_=banks[4][:, 0:nT * 4], func=AF.Copy), reads=bank_bufs(4),
              writes=[b_ab])
        ab3 = abt.rearrange("p (j c) -> p j c", c=4)
        gg = A.f32(nT * 2)
        be = A.f32(nT * 2)
        gam = A.f32(nT * 2)
        ngam = A.f32(nT * 2)
        eg = A.f32(nT * 2)
        rws = A.f32(nT * 2)
        b_g = Buf()
        gg3 = gg.rearrange("p (j d) -> p j d", d=2)
        be3 = be.rearrange("p (j d) -> p j d", d=2)
        for d in range(2):
            ip = ((l * 2 + d) * 8 + hd) * 2
            P.add("act", lambda h, d=d, ip=ip: h.activation(out=gg3[:, :, d], in_=ab3[:, :, d], func=AF.Exp,
                                                          bias=gdp[:, ip + 1:ip + 2], scale=1.0),
                  reads=[b_ab, b_par], writes=[b_g])
            P.add("act", lambda h, d=d: h.activation(out=gg3[:, :, d], in_=gg3[:, :, d], func=AF.Ln, bias=1.0, scale=1.0),
                  reads=[b_g], writes=[b_g])
            P.add("dve", lambda h, d=d, ip=ip: h.tensor_scalar(out=gg3[:, :, d], in0=gg3[:, :, d],
                                                             scalar1=gdp[:, ip:ip + 1], scalar2=None, op0=ALU.mult),
                  reads=[b_g, b_par], writes=[b_g])
            P.add("act", lambda h, d=d: h.activation(out=be3[:, :, d], in_=ab3[:, :, 2 + d], func=AF.Sigmoid),
                  reads=[b_ab], writes=[b_g])
        for d in range(2):
            tri = cst[:, C_TRIU:C_TRIU + 128] if d == 0 else cst[:, C_TRIL:C_TRIL + 128]
            P.add("pe", lambda h, d=d, tri=tri: h.matmul(banks[5][:, d * nT:(d + 1) * nT], lhsT=tri, rhs=gg3[:, :, d],
                                                        start=True, stop=True), reads=[b_g, b_cst], writes=bank_bufs(5))
        P.add("act", lambda h: h.activation(out=gam, in_=banks[5][:, 0:2 * nT], func=AF.Copy), reads=bank_bufs(5),
              writes=[b_g])
        P.add("dve", lambda h: h.tensor_scalar(out=ngam, in0=gam, scalar1=-1.0, scalar2=None, op0=ALU.mult),
              reads=[b_g], writes=[b_g])
        P.add("act", lambda h: h.activation(out=eg, in_=gam, func=AF.Exp), reads=[b_g], writes=[b_g])
        eg3 = eg.rearrange("p (d j) -> p d j", d=2)
        rw3 = rws.rearrange("p (d j) -> p d j", d=2)
        for d in range(2):
            P.add("dve", lambda h, d=d: h.tensor_tensor(out=rw3[:, d, :], in0=eg3[:, d, :], in1=be3[:, :, d], op=ALU.mult),
                  reads=[b_g], writes=[b_g])
        qn = A.bf16(ntok)
        kn = A.bf16(ntok)
        cv = A.bf16(ntok)
        b_qkv = [Buf(), Buf(), Buf()]
        outs = [qn, kn, cv]
        knT = A.bf16(nT * 128)
        vT = A.bf16(nT * 128)
        A.mark()
        xr2 = [A.f32(ntok) for _ in range(2)]
        b_xr2 = [Buf(), Buf()]
        yc2 = [A.f32(ntok) for _ in range(2)]
        b_yc2 = [Buf(), Buf()]
        sqb = A.bf16(512)
        b_sqb = Buf()
        rn = A.f32(512)
        b_rn = Buf()
        nr = ntok // R
        def do_proj(which):
            xr, b_xr = xr2[which % 2], b_xr2[which % 2]
            proj_fm(w3, which * 128, h3, ntok, lambda sb, bk: P.add("act", lambda h: h.activation(
                out=xr[:, sb * 512:(sb + 1) * 512], in_=banks[bk][:, :], func=AF.Copy), reads=bank_bufs(bk), writes=[b_xr]))

        def do_conv(which):
            xr, b_xr, yc, b_yc = xr2[which % 2], b_xr2[which % 2], yc2[which % 2], b_yc2[which % 2]
            cw = T_CONV + ((l * 3 + which) * 8 + hd) * 5
            x3 = xr.rearrange("p (r t) -> p r t", t=R)
            y3 = yc.rearrange("p (r t) -> p r t", t=R)
            P.add("dve", lambda h, cw=cw: h.tensor_scalar(out=yc, in0=xr, scalar1=tab[:, cw + 2:cw + 3], scalar2=None,
                                                         op0=ALU.mult), reads=[b_xr, b_tab], writes=[b_yc])
            for tap, sh in ((1, -1), (0, -2), (3, 1), (4, 2)):
                if sh < 0:
                    ysl = y3[:, :, -sh:R]
                    xsl = x3[:, :, 0:R + sh]
                else:
                    ysl = y3[:, :, 0:R - sh]
                    xsl = x3[:, :, sh:R]
                P.add("dve", lambda h, cw=cw, tap=tap, ysl=ysl, xsl=xsl: h.scalar_tensor_tensor(
                    out=ysl, in0=xsl, scalar=tab[:, cw + tap:cw + tap + 1], in1=ysl, op0=ALU.mult, op1=ALU.add),
                    reads=[b_xr, b_tab, b_yc], writes=[b_yc])
            if which == 2:
                P.add("act", lambda h: h.activation(out=cv, in_=yc, func=AF.Silu), reads=[b_yc], writes=[b_qkv[2]])
            else:
                P.add("act", lambda h: h.activation(out=yc, in_=yc, func=AF.Silu), reads=[b_yc], writes=[b_yc])
                for sb in range(ntok // 512):
                    ssl = slice(sb * 512, (sb + 1) * 512)
                    bk = 4 + sb % 2
                    P.add("act", lambda h, ssl=ssl: h.activation(out=sqb, in_=yc[:, ssl], func=AF.Square), reads=[b_yc],
                          writes=[b_sqb])
                    P.add("pe", lambda h, bk=bk: h.matmul(banks[bk][:, :], lhsT=ones_b, rhs=sqb, start=True, stop=True),
                          reads=[b_sqb, b_cbf], writes=bank_bufs(bk))
                    P.add("act", lambda h, bk=bk: h.activation(out=rn, in_=banks[bk][:, :], func=AF.Ln, bias=EPS, scale=1.0),
                          reads=bank_bufs(bk), writes=[b_rn])
                    P.add("act", lambda h: h.activation(out=rn, in_=rn, func=AF.Exp, scale=-0.5), reads=[b_rn], writes=[b_rn])
                    sc_ = QS if which == 0 else 1.0
                    P.add("dve", lambda h, which=which, ssl=ssl, sc_=sc_: h.scalar_tensor_tensor(
                        out=outs[which][:, ssl], in0=yc[:, ssl], scalar=sc_, in1=rn, op0=ALU.mult, op1=ALU.mult),
                        reads=[b_yc, b_rn], writes=[b_qkv[which]])
        do_proj(0)
        do_proj(1)
        do_conv(0)
        do_proj(2)
        do_conv(1)
        do_conv(2)
        b_tok = Buf()
        for src, dst, bsrc in ((kn, knT, b_qkv[1]), (cv, vT, b_qkv[2])):
            for g4 in range(nT // 4):
                bk = 4 + g4 % 2
                pbf = banks[bk][:, :].bitcast(BF16)

                def f(h, g4=g4, pbf=pbf, src=src):
                    ins = None
                    for u in range(4):
                        j = g4 * 4 + u
                        ins = h.transpose(pbf[:, u * 128:(u + 1) * 128], src[:, j * 128:(j + 1) * 128], ident_b)
                    return ins
                P.add("pe", f, reads=[bsrc, b_cbf], writes=bank_bufs(bk))
                P.add("act", lambda h, g4=g4, pbf=pbf, dst=dst: h.activation(
                    out=dst[:, g4 * 512:(g4 + 1) * 512], in_=pbf[:, 0:512], func=AF.Copy), reads=bank_bufs(bk),
                    writes=[b_tok])
        kn3 = knT.rearrange("p (j c) -> p j c", c=128)
        vt3 = vT.rearrange("p (j c) -> p j c", c=128)
        P.barrier()
        A.release()
        if hd < 7:
            P.add("pool", lambda h: h.dma_start(out=wbuf[:, 0:KC * 640], in_=whg_d[l, hd + 1]), writes=[b_w[0]], dma=True)
        def mk():
            return dict(dg=A.f32(256), grbr=A.f32(256), DT=A.f32(128), ATf=A.f32(128), aqk=A.bf16(128),
                        Tm=A.f32(128), TT=A.f32(128), Xs=A.f32(128), TTb=A.bf16(128),
                        sm=A.f32(4), EGR=A.f32(128), Rw=A.bf16(128), Ru=A.bf16(128), kd=A.bf16(128), qgb=A.bf16(128),
                        wT=A.bf16(128), us=A.f32(128), vn=A.bf16(128), b=Buf())
        W = [mk() for _ in range(4)]
        SfA = [[A.f32(128) for _ in range(2)] for _ in range(2)]
        bSfA = [[Buf(), Buf()] for _ in range(2)]
        Sb = [A.bf16(128) for _ in range(2)]
        bSb = [Buf(), Buf()]
        obanks = list(range(ntok // 512))
        for ob_ in obanks:
            P.add("dve", lambda h, ob_=ob_: h.memset(banks[ob_][:, :], 0.0), writes=bank_bufs(ob_))
        ntile_seq = T // 128
        ntot = len(seqs) * ntile_seq
        gam3 = gam.rearrange("p (d j) -> p d j", d=2)
        ngam3 = ngam.rearrange("p (d j) -> p d j", d=2)
        next_rec = [0, 0]
        DELAY = 33

        def gchain(d, par):
            w = W[d * 2 + par]
            bw = w["b"]
            pb = 4 + d * 2 + par
            for _ in range(par * DELAY):
                yield
            for g in range(par, ntot, 2):
                si = g // ntile_seq
                i = g % ntile_seq
                j = si * ntile_seq + (i if d == 0 else ntile_seq - 1 - i)
                Sf = SfA[d][si % 2]
                bSf = bSfA[d][si % 2]
                ts = slice(j * 128, (j + 1) * 128)
                last = 127 if d == 0 else 0
                gcol = gam3[:, d, j:j + 1]
                ngcol = ngam3[:, d, j:j + 1]
                bcol = be3[:, j, d:d + 1]
                P.add("dve", lambda h, w=w, gcol=gcol: h.tensor_scalar(out=w["dg"][:, 0:128], in0=ident_f, scalar1=gcol,
                                                                      scalar2=None, op0=ALU.mult),
                      reads=[b_g, b_cst], writes=[bw])
                yield
                P.add("dve", lambda h, w=w, bcol=bcol: h.tensor_scalar(out=w["dg"][:, 128:256], in0=ident_f, scalar1=bcol,
                                                                      scalar2=None, op0=ALU.mult),
                      reads=[b_g, b_cst], writes=[bw])
                yield
                P.add("pe", lambda h, w=w, pb=pb: h.matmul(banks[pb][:, 0:256], lhsT=ones_f, rhs=w["dg"], start=True, stop=True),
                      reads=[bw, b_cst], writes=[bslot[pb][0], bslot[pb][1]])
                yield
                P.add("act", lambda h, w=w, pb=pb: h.activation(out=w["grbr"], in_=banks[pb][:, 0:256], func=AF.Copy),
                      reads=[bslot[pb][0], bslot[pb][1]], writes=[bw])
                yield
                negm = negf_b if d == 0 else negb_b

                def fm(h, w=w, pb=pb, negm=negm):
                    h.matmul(banks[pb][:, 256:384], lhsT=ones_f, rhs=w["dg"][:, 0:128], start=True, stop=False,
                             skip_group_check=True)
                    return h.matmul(banks[pb][:, 256:384], lhsT=ident_b, rhs=negm, start=False, stop=True,
                                    skip_group_check=True)
                P.add("pe", fm, reads=[bw, b_cst, b_cbf], writes=[bslot[pb][2]])
                yield
                P.add("act", lambda h, w=w, pb=pb, ngcol=ngcol: h.activation(
                    out=w["DT"], in_=banks[pb][:, 256:384], func=AF.Exp, bias=ngcol, scale=1.0),
                    reads=[bslot[pb][2], b_g], writes=[bw])
                yield
                P.add("pe", lambda h, ts=ts, pb=pb: h.matmul(banks[pb][:, 0:128], lhsT=kn[:, ts], rhs=kn[:, ts], start=True, stop=True),
                      reads=[b_qkv[1]], writes=[bslot[pb][0]])
                yield
                P.add("pe", lambda h, ts=ts, pb=pb: h.matmul(banks[pb][:, 128:256], lhsT=kn[:, ts], rhs=qn[:, ts], start=True, stop=True),
                      reads=[b_qkv[0], b_qkv[1]], writes=[bslot[pb][1]])
                yield
                P.add("dve", lambda h, w=w, pb=pb: h.tensor_tensor(out=w["ATf"], in0=banks[pb][:, 0:128], in1=w["DT"], op=ALU.mult),
                      reads=[bslot[pb][0], bw], writes=[bw])
                yield
                P.add("dve", lambda h, w=w: h.tensor_tensor(out=w["ATf"], in0=w["ATf"], in1=w["grbr"][:, 128:256], op=ALU.mult),
                      reads=[bw], writes=[bw])
                yield
                P.add("dve", lambda h, w=w, pb=pb: h.tensor_tensor(out=w["aqk"], in0=banks[pb][:, 128:256], in1=w["DT"], op=ALU.mult),
                      reads=[bslot[pb][1], bw], writes=[bw])
                yield
                lmA = cst[:, C_LMF:C_LMF + 896] if d == 0 else cst[:, C_LMB:C_LMB + 896]
                lmX = cst[:, C_LMB:C_LMB + 896] if d == 0 else cst[:, C_LMF:C_LMF + 896]
                P.add("dve", lambda h, w=w, lmA=lmA: h.tensor_tensor(out=w["Xs"], in0=w["ATf"], in1=lmA[:, 0:128], op=ALU.mult),
                      reads=[bw, b_cst], writes=[bw])
                yield
                P.add("dve", lambda h, w=w: h.tensor_tensor(out=w["TT"], in0=ident_f, in1=w["Xs"], op=ALU.subtract),
                      reads=[bw, b_cst], writes=[bw])
                yield
                P.add("pe", lambda h, w=w, pb=pb: h.transpose(banks[pb][:, 384:512], w["Xs"], ident_f),
                      reads=[bw, b_cst], writes=[bslot[pb][3]])
                yield
                P.add("dve", lambda h, w=w, pb=pb: h.tensor_tensor(out=w["Tm"], in0=ident_f, in1=banks[pb][:, 384:512],
                                                                  op=ALU.subtract), reads=[bslot[pb][3], b_cst], writes=[bw])
                yield
                for lv in range(1, 7):
                    P.add("pe", lambda h, w=w, pb=pb: h.matmul(
                        banks[pb][:, 0:128], lhsT=w["ATf"], rhs=w["Tm"], start=True, stop=True),
                        reads=[bw], writes=[bslot[pb][0]])
                    yield
                    P.add("dve", lambda h, w=w, pb=pb, lmX=lmX, lv=lv: h.tensor_tensor(
                        out=w["Xs"], in0=banks[pb][:, 0:128], in1=lmX[:, lv * 128:(lv + 1) * 128], op=ALU.mult),
                        reads=[bslot[pb][0], b_cst], writes=[bw])
                    yield
                    b_ = 1 << lv
                    hx = d
                    ho = 1 - d

                    def half(t, hsel, b_=b_, lv=lv):
                        if lv < 5:
                            return t
                        return t.rearrange("p (k two b) -> p k two b", two=2, b=b_)[:, :, hsel, :]

                    def comp(slot, pb=pb, b_=b_, lv=lv):
                        if lv < 5:
                            return banks[pb][:, slot * 128:(slot + 1) * 128]
                        return banks[pb][:, slot * 128:slot * 128 + 64].rearrange("p (k b) -> p k b", b=b_)
                    if lv < 6:
                        P.add("pe", lambda h, w=w, o_=comp(1), r_=half(w["Xs"], hx): h.matmul(
                            o_, lhsT=w["TT"], rhs=r_, start=True, stop=True), reads=[bw], writes=[bslot[pb][1]])
                        yield
                    P.add("pe", lambda h, w=w, o_=comp(2), r_=half(w["TT"], ho): h.matmul(
                        o_, lhsT=w["Xs"], rhs=r_, start=True, stop=True), reads=[bw], writes=[bslot[pb][2]])
                    yield
                    if lv < 6:
                        P.add("dve", lambda h, t_=half(w["Tm"], hx), i_=comp(1): h.tensor_tensor(
                            out=t_, in0=t_, in1=i_, op=ALU.subtract), reads=[bw, bslot[pb][1]], writes=[bw])
                        yield
                    P.add("dve", lambda h, t_=half(w["TT"], ho), i_=comp(2): h.tensor_tensor(
                        out=t_, in0=t_, in1=i_, op=ALU.subtract), reads=[bw, bslot[pb][2]], writes=[bw])
                    yield
                P.add("act", lambda h, w=w: h.activation(out=w["TTb"], in_=w["TT"], func=AF.Copy), reads=[bw], writes=[bw])
                yield
                sm = w["sm"]
                glast = w["grbr"][:, last:last + 1]
                P.add("act", lambda h, w=w, gcol=gcol, glast=glast, sm=sm: h.activation(
                    out=sm[:, 0:1], in_=gcol, func=AF.Exp, bias=glast, scale=-1.0), reads=[bw, b_g], writes=[bw])
                yield
                P.add("act", lambda h, w=w, glast=glast, sm=sm: h.activation(out=sm[:, 1:2], in_=glast, func=AF.Exp),
                      reads=[bw], writes=[bw])
                yield
                P.add("act", lambda h, w=w: h.activation(out=w["EGR"], in_=w["grbr"][:, 0:128], func=AF.Exp), reads=[bw],
                      writes=[bw])
                yield
                rcol = rw3[:, d, j:j + 1]
                P.add("dve", lambda h, w=w, rcol=rcol, j=j: h.tensor_scalar(out=w["Rw"], in0=kn3[:, j, :], scalar1=rcol,
                                                                          scalar2=None, op0=ALU.mult),
                      reads=[b_tok, b_g], writes=[bw])
                yield
                P.add("dve", lambda h, w=w, bcol=bcol, j=j: h.tensor_scalar(out=w["Ru"], in0=vt3[:, j, :], scalar1=bcol,
                                                                          scalar2=None, op0=ALU.mult),
                      reads=[b_tok, b_g], writes=[bw])
                yield
                P.add("dve", lambda h, w=w, sm=sm, j=j: h.tensor_scalar(out=w["kd"], in0=kn3[:, j, :], scalar1=sm[:, 0:1],
                                                                      scalar2=None, op0=ALU.mult),
                      reads=[b_tok, bw], writes=[bw])
                yield
                P.add("dve", lambda h, w=w, ts=ts: h.tensor_tensor(out=w["qgb"], in0=qn[:, ts], in1=w["EGR"], op=ALU.mult),
                      reads=[b_qkv[0], bw], writes=[bw])
                yield
                P.add("pe", lambda h, w=w, pb=pb: h.matmul(banks[pb][:, 0:128], lhsT=w["Rw"], rhs=w["TTb"], start=True, stop=True),
                      reads=[bw], writes=[bslot[pb][0]])
                yield
                P.add("pe", lambda h, w=w, pb=pb: h.matmul(banks[pb][:, 128:256], lhsT=w["TTb"], rhs=w["Ru"], start=True, stop=True),
                      reads=[bw], writes=[bslot[pb][1]])
                yield
                P.add("act", lambda h, w=w, pb=pb: h.activation(out=w["wT"], in_=banks[pb][:, 0:128], func=AF.Copy),
                      reads=[bslot[pb][0]], writes=[bw])
                yield
                P.add("act", lambda h, w=w, pb=pb: h.activation(out=w["us"], in_=banks[pb][:, 128:256], func=AF.Copy),
                      reads=[bslot[pb][1]], writes=[bw])
                yield
                assert next_rec[d] == g, (d, g, next_rec[d])
                next_rec[d] = g + 1
                if i == 0:
                    if grp == 1:
                        P.add("sp", lambda h, Sf=Sf: h.dma_start(out=Sf, in_=st_gd_d[l, d, hd]), writes=[bSf], dma=True)
                    else:
                        P.add("dve", lambda h, Sf=Sf: h.memset(Sf, 0.0), writes=[bSf])
                    yield
                    P.add("pool", lambda h, Sf=Sf: h.tensor_copy(out=Sb[d], in_=Sf), reads=[bSf], writes=[bSb[d]])
                    yield
                P.add("pe", lambda h, w=w, pb=pb: h.matmul(banks[pb][:, 256:384], lhsT=w["wT"], rhs=Sb[d], start=True, stop=True),
                      reads=[bw, bSb[d]], writes=[bslot[pb][2]])
                yield
                P.add("dve", lambda h, w=w, pb=pb: h.tensor_tensor(out=w["vn"], in0=w["us"], in1=banks[pb][:, 256:384],
                                                                  op=ALU.subtract), reads=[bw, bslot[pb][2]], writes=[bw])
                yield
                obk = j // 4
                ocol = (j % 4) * 128

                def fo(h, w=w, obk=obk, ocol=ocol):
                    h.matmul(banks[obk][:, ocol:ocol + 128], lhsT=Sb[d], rhs=w["qgb"], start=False, stop=False,
                             skip_group_check=True)
                    return h.matmul(banks[obk][:, ocol:ocol + 128], lhsT=w["vn"], rhs=w["aqk"], start=False,
                                    stop=True, skip_group_check=True)
                P.add("pe", fo, reads=[bw, bSb[d]], writes=[bslot[obk][j % 4]])
                yield
                P.add("pe", lambda h, w=w, pb=pb: h.matmul(banks[pb][:, 384:512], lhsT=w["kd"], rhs=w["vn"], start=True, stop=True),
                      reads=[bw], writes=[bslot[pb][3]])
                yield
                P.add("dve", lambda h, w=w, pb=pb, sm=sm, Sf=Sf: h.scalar_tensor_tensor(
                    out=Sf, in0=Sf, scalar=sm[:, 1:2], in1=banks[pb][:, 384:512], op0=ALU.mult, op1=ALU.add),
                    reads=[bSf, bw, bslot[pb][3]], writes=[bSf])
                yield
                P.add("pool", lambda h, Sf=Sf: h.tensor_copy(out=Sb[d], in_=Sf), reads=[bSf], writes=[bSb[d]])
                yield
                if grp == 0 and i == ntile_seq - 1:
                    P.add("sp", lambda h, Sf=Sf, s=seqs[si]: h.dma_start(out=ngd_out[s, l, d, hd], in_=Sf), reads=[bSf],
                          dma=True)
                    yield

        round_robin([gchain(0, 0), gchain(1, 0), gchain(0, 1), gchain(1, 1)])
        P.barrier()
        head_finalize(l, 8 + hd, T_GDNW + l, ntok, tok0, gsil, b_gs, obanks)
        A.release()

    def phase_mixer(l):
        for grp in (1, 0):
            A.mark()
            if grp == 1:
                ntok, tok0, seqs, T, r = 2048, 1024, [0], 2048, 1
            else:
                ntok, tok0, seqs, T, r = 1024, 0, [0, 1, 2, 3], 256, 0
            hT = A.bf16(KC * ntok)
            h3 = hT.rearrange("p (k t) -> p k t", k=KC)
            for half in range(ntok // 1024):
                hv = h3[:, :, half * 1024:(half + 1) * 1024]
                norm_mod_view(X, tok0 + half * 1024, 1024, l, 1, r, hv)
            wbuf = A.bf16(KC * 640)
            for hd in range(8):
                hgrn2_head(l, hd, grp, h3, wbuf, ntok, tok0, seqs, T)
                gdn_head(l, hd, grp, h3, wbuf, ntok, tok0, seqs, T)
            A.release()
            A.mark()
            ot = [A.bf16(KC * 512) for _ in range(2)]
            otb = [Buf() for _ in range(2)]
            xo = [A.f32(KC * 512) for _ in range(2)]
            xob = [Buf() for _ in range(2)]
            wres = A.bf16(16 * KC * 128)
            wrb = [Buf() for _ in range(16)]
            nsb = ntok // 512

            def load_tok(sb):
                t0 = tok0 + sb * 512
                q = sb % 2
                o3 = ot[q].rearrange("p (k t) -> p k t", k=KC)
                x3 = xo[q].rearrange("p (k t) -> p k t", k=KC)
                P.add("sp", lambda h: h.dma_start(out=o3, in_=OT[:, :, t0:t0 + 512]), writes=[otb[q]], dma=True)
                P.add("sp", lambda h: h.dma_start(out=x3, in_=X[:, :, t0:t0 + 512]), writes=[xob[q]], dma=True)

            load_tok(0)
            for i in range(16):
                P.add("pool", lambda h, i=i: h.dma_start(out=wres[:, i * KC * 128:(i + 1) * KC * 128], in_=wmo_d[l, i]),
                      writes=[wrb[i]], dma=True)
            for sb in range(nsb):
                t0 = tok0 + sb * 512
                q = sb % 2
                o3 = ot[q].rearrange("p (k t) -> p k t", k=KC)
                x3 = xo[q].rearrange("p (k t) -> p k t", k=KC)
                if sb + 1 < nsb:
                    load_tok(sb + 1)
                for i in range(16):
                    w3 = wres[:, i * KC * 128:(i + 1) * KC * 128].rearrange("p (k c) -> p k c", k=KC)
                    yb = 5 + i % 2
                    ig = tix(l, 1, i, r)

                    def f(h, w3=w3, yb=yb, o3=o3):
                        ins = None
                        for kc in range(KC):
                            ins = h.matmul(banks[yb][:, :], lhsT=w3[:, kc, :], rhs=o3[:, kc, :], start=(kc == 0), stop=(kc == KC - 1))
                        return ins
                    P.add("pe", f, reads=[wrb[i], otb[q]], writes=bank_bufs(yb))
                    P.add("dve", lambda h, x3=x3, i=i, yb=yb, ig=ig: h.scalar_tensor_tensor(
                        out=x3[:, i, :], in0=banks[yb][:, :], scalar=tG[:, ig:ig + 1],
                        in1=x3[:, i, :], op0=ALU.mult, op1=ALU.add),
                        reads=bank_bufs(yb) + [xob[q], b_mod], writes=[xob[q]])
                P.add("sp", lambda h, x3=x3, t0=t0: h.dma_start(out=X[:, :, t0:t0 + 512], in_=x3), reads=[xob[q]], dma=True)
            P.barrier()
            A.release()

    def norm_mod_view(xsrc, tok0, ntok, l, j, r, hv):
        A.mark()
        SW = 256
        nsb = ntok // SW
        xs = [A.f32(KC * SW) for _ in range(2)]
        xb = [Buf() for _ in range(2)]
        sq = [A.bf16(KC * SW) for _ in range(2)]
        b_sq = [Buf() for _ in range(2)]
        lnv = [A.f32(SW) for _ in range(2)]
        rstd = [A.f32(SW) for _ in range(2)]
        b_r = [Buf() for _ in range(2)]
        tmp = [A.f32(SW) for _ in range(2)]
        tb = [Buf() for _ in range(2)]

        def load(sb):
            q = sb % 2
            t0 = tok0 + sb * SW
            x3 = xs[q].rearrange("p (k t) -> p k t", k=KC)
            P.add("sp", lambda h: h.dma_start(out=x3, in_=xsrc[:, :, t0:t0 + SW]), writes=[xb[q]], dma=True)

        load(0)
        if nsb > 1:
            load(1)
        for sb in range(nsb):
            q = sb % 2
            x3 = xs[q].rearrange("p (k t) -> p k t", k=KC)
            sq3 = sq[q].rearrange("p (k t) -> p k t", k=KC)
            pq = banks[q][:, 0:SW]
            P.add("act", lambda h, q=q: h.activation(out=sq[q], in_=xs[q], func=AF.Square), reads=[xb[q]], writes=[b_sq[q]])

            def f(h, sq3=sq3, pq=pq):
                ins = None
                for kc in range(KC):
                    ins = h.matmul(pq, lhsT=ones_b, rhs=sq3[:, kc, :], start=(kc == 0), stop=(kc == KC - 1))
                return ins
            P.add("pe", f, reads=[b_sq[q], b_cbf], writes=bank_bufs(q))
            P.add("act", lambda h, q=q, pq=pq: h.activation(out=lnv[q], in_=pq, func=AF.Ln, bias=EPS, scale=1.0 / D),
                  reads=bank_bufs(q), writes=[b_r[q]])
            P.add("act", lambda h, q=q: h.activation(out=rstd[q], in_=lnv[q], func=AF.Exp, scale=-0.5), reads=[b_r[q]],
                  writes=[b_r[q]])
            for kc in range(KC):
                k2 = kc % 2
                ia = tix(l, j, kc, r)
                P.add("dve", lambda h, kc=kc, k2=k2, ia=ia, x3=x3, q=q: h.scalar_tensor_tensor(
                    out=tmp[k2], in0=x3[:, kc, :], scalar=tA[:, ia:ia + 1], in1=rstd[q], op0=ALU.mult, op1=ALU.mult),
                    reads=[xb[q], b_r[q], b_mod], writes=[tb[k2]])
                P.add("act", lambda h, kc=kc, k2=k2, ia=ia, sb=sb: h.activation(
                    out=hv[:, kc, sb * SW:(sb + 1) * SW], in_=tmp[k2], func=AF.Identity, bias=tB[:, ia:ia + 1], scale=1.0),
                    reads=[tb[k2], b_mod], writes=[b_hT[0]])
            if sb + 2 < nsb:
                load(sb + 2)
        P.barrier()
        A.release()

    def norm_mod(xsrc, tok0, ntok, l, j, r, hT, hw, sbank):
        hv = hT.rearrange("p (k t) -> p k t", k=KC)
        norm_mod_view(xsrc, tok0, ntok, l, j, r, hv)

    P.barrier()
    phase_mod()
    src = x_in
    done = False
    for l in range(DEPTH):
        phase_ffn(l, 0, src, X)
        src = X
        if stop_after == ("ffn1", l):
            done = True
            break
        phase_mixer(l)
        if stop_after == ("mix", l):
            done = True
            break
        phase_ffn(l, 1, X, X)
    phase_final()
    P.finish()
    P.emit()
    return nc


def _prep_shared(inp):
    f = np.float32
    w_mod = np.asarray(inp["w_mod"], f)
    wmod = np.ascontiguousarray(w_mod.reshape(DEPTH, KC, 128, 36, 512).transpose(0, 3, 2, 1, 4)).reshape(DEPTH, 36, 128, KC * 512)
    fwi = np.asarray(inp["ffn_w_in"], f)
    wfi = np.ascontiguousarray(fwi.reshape(DEPTH, 2, KC, 128, 2, NFC, 128).transpose(0, 1, 5, 3, 2, 4, 6)).reshape(
        DEPTH, 2, NFC, 128, KC * 256)
    fwo = np.asarray(inp["ffn_w_out"], f)
    wfo = np.ascontiguousarray(fwo.reshape(DEPTH, 2, NFC, 128, 16, 128).transpose(0, 1, 4, 3, 2, 5)).reshape(
        DEPTH, 2, 16, 128, NFC * 128)
    w_in = np.asarray(inp["w_in"], f).reshape(DEPTH, KC, 128, 9248)
    whg = np.empty((DEPTH, 8, 128, KC, 640), f)
    wgd = np.empty((DEPTH, 8, 128, KC, 516), f)
    for hd in range(8):
        for qi, off in enumerate((0, 1024, 2048, 3072, 4096)):
            whg[:, hd, :, :, qi * 128:(qi + 1) * 128] = w_in[:, :, :, off + hd * 128:off + (hd + 1) * 128].transpose(0, 2, 1, 3)
        for qi, off in enumerate((5120, 6144, 7168, 8192)):
            wgd[:, hd, :, :, qi * 128:(qi + 1) * 128] = w_in[:, :, :, off + hd * 128:off + (hd + 1) * 128].transpose(0, 2, 1, 3)
        for qi, off in enumerate((9216, 9224, 9232, 9240)):
            wgd[:, hd, :, :, 512 + qi] = w_in[:, :, :, off + hd].transpose(0, 2, 1)
    whg = whg.reshape(DEPTH, 8, 128, KC * 640)
    wgd = wgd.reshape(DEPTH, 8, 128, KC * 516)
    w_out = np.asarray(inp["w_out"], f)
    wmo = np.ascontiguousarray(w_out.reshape(DEPTH, KC, 128, 16, 128).transpose(0, 3, 2, 1, 4)).reshape(DEPTH, 16, 128, KC * 128)
    tab = np.zeros((128, T_END), f)
    tab[:, T_BMOD:T_BMOD + 288] = np.asarray(inp["b_mod"], f).reshape(DEPTH, 144, 128).transpose(2, 0, 1).reshape(128, 288)
    tab[:, T_NORM:T_NORM + 96] = np.asarray(inp["norm_w"], f).reshape(DEPTH, 3, KC, 128).transpose(3, 0, 1, 2).reshape(128, 96)
    tab[:, T_FNW:T_FNW + 16] = np.asarray(inp["final_norm_w"], f).reshape(KC, 128).T
    tab[:, T_HGLB:T_HGLB + 32] = np.asarray(inp["hg_lower_bounds"], f).reshape(2, DEPTH, 8, 128).transpose(3, 0, 1, 2).reshape(128, 32)
    tab[:, T_HGNW:T_HGNW + 2] = np.asarray(inp["hg_norm_w"], f).T
    tab[:, T_GDNW:T_GDNW + 2] = np.asarray(inp["gd_norm_w"], f).T
    cw = np.asarray(inp["gd_conv_w"], f).reshape(DEPTH, 5, 3, 8, 128)
    tab[:, T_CONV:T_CONV + 240] = cw.transpose(4, 0, 2, 3, 1).reshape(128, 240)
    gp = np.stack([np.asarray(inp["gd_A_log"], f), np.asarray(inp["gd_dt_bias"], f)], axis=-1)
    tab[:, T_GDPAR:T_GDPAR + 64] = np.broadcast_to(gp.reshape(1, 64), (128, 64))
    return dict(wmod=wmod, wfi=wfi, wfo=wfo, whg=whg, wgd=wgd, wmo=wmo, cst=build_consts()), tab


def _prep_core(inp, core, tab):
    f = np.float32
    xp = np.asarray(inp["x_prompt"], f)[4 * core:4 * core + 4].reshape(1024, D)
    xs = np.asarray(inp["x_sample"], f)[core].reshape(2048, D)
    xt = np.concatenate([xp, xs], axis=0)
    x_in = np.ascontiguousarray(xt.T.reshape(KC, 128, NTOK).transpose(1, 0, 2))
    t = tab.copy()
    cond = np.stack([np.asarray(inp["c_ctx"], f), np.asarray(inp["c"], f)[core]], axis=0)
    t[:, T_COND:T_COND + 32] = cond.reshape(2, KC, 128).transpose(2, 1, 0).reshape(128, 32)
    return dict(x_in=x_in, tab=t,
                st_hg=np.ascontiguousarray(np.asarray(inp["state_hgrn2"], f)[core]),
                st_gd=np.ascontiguousarray(np.asarray(inp["state_gdn"], f)[core]))


def _unpack_y(y):
    return np.ascontiguousarray(y.transpose(2, 1, 0)).reshape(NTOK, D)


def kernel(**inputs):
    n = 8
    shared, tab = _prep_shared(inputs)
    nc = build_program()
    in_maps = []
    for c in range(n):
        m = dict(shared)
        m.update(_prep_core(inputs, c, tab))
        in_maps.append(m)
    res = run_bass_kernel_spmd(nc, in_maps, core_ids=list(range(n)))
    yp = np.empty((32, 256, D), np.float32)
    ys = np.empty((8, 2048, D), np.float32)
    nhg = np.empty((32, DEPTH, 2, 8, 128, 128), np.float32)
    ngd = np.empty((32, DEPTH, 2, 8, 128, 128), np.float32)
    for c in range(n):
        r = res.results[c]
        y = _unpack_y(np.asarray(r["y_out"], np.float32))
        yp[4 * c:4 * c + 4] = y[:1024].reshape(4, 256, D)
        ys[c] = y[1024:]
        nhg[4 * c:4 * c + 4] = np.asarray(r["nhg_out"], np.float32)
        ngd[4 * c:4 * c + 4] = np.asarray(r["ngd_out"], np.float32)
    return (yp, ys, nhg, ngd)
```
